# Optimizing a Trainium2 kernel written in Bass

```python
import math
import jax, jax.numpy as jnp
from jax import lax
import numpy as np

D_MODEL = 1024
BATCH = 8
SEQ = 2048
DEPTH = 4
DEC_BATCH = 128
DEC_SEQ = 1
PAST_LEN = 2048
PAGE_SIZE = 128

N_A_LAYERS = DEPTH // 2
N_B_LAYERS = DEPTH - N_A_LAYERS
S5_GROUP = 16
S5_GROUPS = D_MODEL // S5_GROUP
S5_STATE = 64
DT_MIN = 1e-3
DT_MAX = 1e-1
D_FF = 4 * D_MODEL
HEAD_DIM = 64
N_Q_HEADS = D_MODEL // HEAD_DIM
N_KV_HEADS = 4
Q_PER_KV = N_Q_HEADS // N_KV_HEADS
N_BRANCH = 3
CMP_BLOCK = 32
SEL_BLOCK = 64
N_SEL = 16
WINDOW = 512
CMP_HIDDEN = 4 * HEAD_DIM
Q_BLOCK = 64
ROPE_THETA = 10000.0
RMS_EPS = 1e-6
FORCE_SCORE = 1e3

kernel_name = 's5_nsa_yoco_decoder_step'


def _rms_norm(x, g):
    xf = x.astype(jnp.float32)
    y = xf * lax.rsqrt(jnp.mean(xf * xf, axis=-1, keepdims=True) + RMS_EPS)
    return (y * g.astype(jnp.float32)).astype(x.dtype)


def _rope(x, pos):
    half = HEAD_DIM // 2
    inv_freq = ROPE_THETA ** (-jnp.arange(half, dtype=jnp.float32) / half)
    ang = pos.astype(jnp.float32)[:, None] * inv_freq[None, :]
    cos = jnp.cos(ang)[:, None, :]
    sin = jnp.sin(ang)[:, None, :]
    xf = x.astype(jnp.float32)
    x1, x2 = xf[..., :half], xf[..., half:]
    return jnp.concatenate([x1 * cos - x2 * sin, x2 * cos + x1 * sin], axis=-1).astype(x.dtype)


def _masked_softmax(s, mask):
    s = jnp.where(mask, s, -jnp.inf)
    m = jnp.max(s, axis=-1, keepdims=True)
    m = jnp.where(jnp.isfinite(m), m, 0.0)
    p = jnp.exp(s - m)
    return p / jnp.maximum(jnp.sum(p, axis=-1, keepdims=True), 1e-30)


def _complex_affine_combine(e1, e2):
    a1r, a1i, b1r, b1i = e1
    a2r, a2i, b2r, b2i = e2
    return (a1r * a2r - a1i * a2i, a1r * a2i + a1i * a2r,
            a2r * b1r - a2i * b1i + b2r, a2r * b1i + a2i * b1r + b2i)


def _s5_mixer(h, s0, a_re, a_im, log_dt, b_re, b_im, c_re, c_im, d_skip, w_glu):
    nb, t, dm = h.shape
    f32 = jnp.float32
    u = h.astype(f32).reshape(nb, t, S5_GROUPS, S5_GROUP)
    dt = jnp.exp(log_dt.astype(f32))[:, None]
    ar, ai = a_re.astype(f32), a_im.astype(f32)
    mag = jnp.exp(ar * dt)
    abar_r, abar_i = mag * jnp.cos(ai * dt), mag * jnp.sin(ai * dt)
    den = ar * ar + ai * ai
    zoh_r = ((abar_r - 1.0) * ar + abar_i * ai) / den
    zoh_i = (abar_i * ar - (abar_r - 1.0) * ai) / den
    br, bi = b_re.astype(f32), b_im.astype(f32)
    bbar_r = zoh_r[..., None] * br - zoh_i[..., None] * bi
    bbar_i = zoh_r[..., None] * bi + zoh_i[..., None] * br
    bu_r = jnp.einsum('btgc,gpc->btgp', u, bbar_r)
    bu_i = jnp.einsum('btgc,gpc->btgp', u, bbar_i)
    a_r = jnp.broadcast_to(abar_r, (1, t) + abar_r.shape)
    a_i = jnp.broadcast_to(abar_i, (1, t) + abar_i.shape)
    acr, aci, bcr, bci = lax.associative_scan(_complex_affine_combine, (a_r, a_i, bu_r, bu_i), axis=1)
    s0r = s0[..., 0].astype(f32)[:, None]
    s0i = s0[..., 1].astype(f32)[:, None]
    sr = bcr + acr * s0r - aci * s0i
    si = bci + acr * s0i + aci * s0r
    y = (jnp.einsum('gcp,btgp->btgc', c_re.astype(f32), sr)
         - jnp.einsum('gcp,btgp->btgc', c_im.astype(f32), si))
    y = y.reshape(nb, t, dm) + d_skip.astype(f32) * h.astype(f32)
    z = jax.nn.gelu(y) @ w_glu.astype(f32)
    out = z[..., :dm] * jax.nn.sigmoid(z[..., dm:])
    return out.astype(h.dtype), jnp.stack([sr[:, -1], si[:, -1]], axis=-1)


def _sqrelu_mlp(h, w_up, w_down):
    return jnp.square(jax.nn.relu(h @ w_up)) @ w_down


def _kv_rows(x, pos, kv_norm, w_kv, k_norm):
    nb, t, _ = x.shape
    kv = (_rms_norm(x, kv_norm) @ w_kv).reshape(nb, t, N_BRANCH, 2, N_KV_HEADS, HEAD_DIM)

    def keyed(br):
        k = _rope(_rms_norm(kv[:, :, br, 0], k_norm[br]), pos)
        return jnp.stack([k, kv[:, :, br, 1]], axis=2)
    return kv[:, :, 0], keyed(1), keyed(2)


def _compress(rows, cmp_pos, cmp_w1, cmp_w2, k_gain):
    nb, t = rows.shape[:2]
    n_blk = t // CMP_BLOCK
    blk = rows[:, :n_blk * CMP_BLOCK].reshape(nb, n_blk, CMP_BLOCK, 2, N_KV_HEADS, HEAD_DIM)
    blk = blk + cmp_pos.transpose(1, 0, 2)[:, :, None, :]
    w1 = cmp_w1.reshape(2, CMP_BLOCK, HEAD_DIM, CMP_HIDDEN)
    hid = jax.nn.gelu(jnp.einsum('bnlsgd,sldh->bnsgh', blk, w1))
    comp = jnp.einsum('bnsgh,she->bnsge', hid, cmp_w2)
    c_end = (jnp.arange(n_blk) + 1) * CMP_BLOCK - 1
    kc = _rope(_rms_norm(comp[:, :, 0], k_gain), c_end)
    return kc, comp[:, :, 1], c_end


def _nsa_attend(q, gates, q_pos, kc, vc, c_end, gather_sel, n_sel_blocks, kw, vw, kw_pos):
    nb, nq = q.shape[:2]
    f32 = jnp.float32
    qg = q.astype(f32).reshape(nb, nq, N_KV_HEADS, Q_PER_KV, HEAD_DIM) * (HEAD_DIM ** -0.5)
    s_c = jnp.einsum('bqgrd,bngd->bqgrn', qg, kc.astype(f32))
    p_c = _masked_softmax(s_c, (c_end[None, :] <= q_pos[:, None])[None, :, None, None, :])
    o_c = jnp.einsum('bqgrn,bngd->bqgrd', p_c, vc.astype(f32))
    ratio = SEL_BLOCK // CMP_BLOCK
    imp = p_c.sum(axis=3)
    imp = jnp.pad(imp, ((0, 0), (0, 0), (0, 0), (0, n_sel_blocks * ratio - imp.shape[-1])))
    imp = imp.reshape(nb, nq, N_KV_HEADS, n_sel_blocks, ratio).sum(-1)
    cur = q_pos // SEL_BLOCK
    j = jnp.arange(n_sel_blocks)[None, :]
    causal_blk = j <= cur[:, None]
    forced = (j == 0) | (j == cur[:, None]) | (j == cur[:, None] - 1)
    score = jnp.where(causal_blk[None, :, None], imp + FORCE_SCORE * forced[None, :, None], -jnp.inf)
    n_top = min(N_SEL, n_sel_blocks)
    _, top_idx = lax.top_k(score, n_top)
    ks, vs = gather_sel(top_idx)
    tok_pos = top_idx[..., None] * SEL_BLOCK + jnp.arange(SEL_BLOCK)
    sel_mask = (tok_pos <= q_pos[None, :, None, None, None]).reshape(nb, nq, N_KV_HEADS, 1, n_top * SEL_BLOCK)
    s_s = jnp.einsum('bqgrd,bqgkld->bqgrkl', qg, ks.astype(f32)).reshape(nb, nq, N_KV_HEADS, Q_PER_KV, n_top * SEL_BLOCK)
    p_s = _masked_softmax(s_s, sel_mask)
    o_s = jnp.einsum('bqgrm,bqgmd->bqgrd', p_s, vs.astype(f32).reshape(nb, nq, N_KV_HEADS, n_top * SEL_BLOCK, HEAD_DIM))
    dist = q_pos[:, None] - kw_pos[None, :]
    w_mask = (dist >= 0) & (dist < WINDOW) & (kw_pos[None, :] >= 0)
    s_w = jnp.einsum('bqgrd,blgd->bqgrl', qg, kw.astype(f32))
    p_w = _masked_softmax(s_w, w_mask[None, :, None, None, :])
    o_w = jnp.einsum('bqgrl,blgd->bqgrd', p_w, vw.astype(f32))
    g = gates.astype(f32).reshape(nb, nq, N_KV_HEADS, Q_PER_KV, N_BRANCH)
    o = g[..., 0:1] * o_c + g[..., 1:2] * o_s + g[..., 2:3] * o_w
    return o.reshape(nb, nq, N_Q_HEADS * HEAD_DIM).astype(q.dtype)


def _nsa_queries(x, pos, norm_g, w_in, q_gain):
    nb, t, _ = x.shape
    proj = _rms_norm(x, norm_g) @ w_in
    qd = N_Q_HEADS * HEAD_DIM
    q = _rope(_rms_norm(proj[..., :qd].reshape(nb, t, N_Q_HEADS, HEAD_DIM), q_gain), pos)
    gates = jax.nn.sigmoid(proj[..., qd:].astype(jnp.float32)).reshape(nb, t, N_Q_HEADS, N_BRANCH)
    return q, gates


def _prompt_mixer_b(x, pos, kv_norm, w_kv, k_norm, cmp_pos, cmp_w1, cmp_w2):
    nb, t, _ = x.shape
    cmp_rows, slc_rows, win_rows = _kv_rows(x, pos, kv_norm, w_kv, k_norm)
    kc, vc, c_end = _compress(cmp_rows, cmp_pos, cmp_w1, cmp_w2, k_norm[0])
    n_sel_blk = t // SEL_BLOCK
    sel_blocks = slc_rows.reshape(nb, n_sel_blk, SEL_BLOCK, 2, N_KV_HEADS, HEAD_DIM)
    win_pad = jnp.pad(win_rows, ((0, 0), (WINDOW, 0), (0, 0), (0, 0), (0, 0)))
    b_ar = jnp.arange(nb)[:, None, None, None]
    g_ar = jnp.arange(N_KV_HEADS)[None, None, :, None]

    def gather(idx):
        blk = sel_blocks[b_ar, idx, :, :, g_ar]
        return blk[..., 0, :], blk[..., 1, :]

    def mixer(q, gates):
        def one_block(i):
            start = i * Q_BLOCK
            qb = lax.dynamic_slice_in_dim(q, start, Q_BLOCK, axis=1)
            gb = lax.dynamic_slice_in_dim(gates, start, Q_BLOCK, axis=1)
            wb = lax.dynamic_slice_in_dim(win_pad, start, WINDOW + Q_BLOCK, axis=1)
            q_pos = start + jnp.arange(Q_BLOCK)
            kw_pos = start - WINDOW + jnp.arange(WINDOW + Q_BLOCK)
            return _nsa_attend(qb, gb, q_pos, kc, vc, c_end, gather, n_sel_blk, wb[:, :, 0], wb[:, :, 1], kw_pos)
        out = lax.map(one_block, jnp.arange(t // Q_BLOCK))
        return out.transpose(1, 0, 2, 3).reshape(nb, t, N_Q_HEADS * HEAD_DIM)

    return mixer, (cmp_rows, slc_rows, win_rows[:, -min(WINDOW, t):])


def _sample_mixer_b(x, pos, cache_cmp_kv, cache_slc_kv, cache_win_kv, page_table,
                    kv_norm, w_kv, k_norm, cmp_pos, cmp_w1, cmp_w2):
    nb, t, _ = x.shape
    past_len = page_table.shape[1] * PAGE_SIZE
    row_shape = (2, N_KV_HEADS, HEAD_DIM)
    cmp_new, slc_new, win_new = _kv_rows(x, pos, kv_norm, w_kv, k_norm)
    past_cmp = cache_cmp_kv[page_table].reshape((nb, past_len) + row_shape)
    kc, vc, c_end = _compress(jnp.concatenate([past_cmp, cmp_new], axis=1), cmp_pos, cmp_w1, cmp_w2, k_norm[0])
    bpp = PAGE_SIZE // SEL_BLOCK
    nb_past = past_len // SEL_BLOCK
    nb_new = -(-t // SEL_BLOCK)
    pool_blocks = cache_slc_kv.reshape((-1, SEL_BLOCK) + row_shape)
    j_past = jnp.arange(nb_past)
    phys_tbl = page_table[:, j_past // bpp] * bpp + j_past % bpp
    new_blocks = jnp.pad(slc_new, ((0, 0), (0, nb_new * SEL_BLOCK - t), (0, 0), (0, 0), (0, 0)))
    new_blocks = new_blocks.reshape((nb, nb_new, SEL_BLOCK) + row_shape)
    b_ar = jnp.arange(nb)[:, None, None, None]
    g_ar = jnp.arange(N_KV_HEADS)[None, None, :, None]

    def gather(idx):
        is_past = (idx < nb_past)[..., None, None, None]
        phys = phys_tbl[b_ar, jnp.minimum(idx, nb_past - 1)]
        from_past = pool_blocks[phys, :, :, g_ar]
        from_new = new_blocks[b_ar, jnp.clip(idx - nb_past, 0, nb_new - 1), :, :, g_ar]
        blk = jnp.where(is_past, from_past, from_new)
        return blk[..., 0, :], blk[..., 1, :]

    win_buf = cache_win_kv.shape[1]
    win_all = jnp.concatenate([cache_win_kv, win_new], axis=1)
    kw_pos = past_len - win_buf + jnp.arange(win_buf + t)

    def mixer(q, gates):
        return _nsa_attend(q, gates, pos, kc, vc, c_end, gather, nb_past + nb_new,
                           win_all[:, :, 0], win_all[:, :, 1], kw_pos)

    return mixer, (cmp_new, slc_new, win_all[:, -win_buf:])


def _trunk(x, pos, s5_init, build_mixer_b, norm_mix, norm_mlp, w_up, w_down, s5_params,
           nsa_w_in, q_norm, nsa_w_o):
    s5_final = []
    mixer_b, new_rows = None, None
    for layer in range(DEPTH):
        if layer < N_A_LAYERS:
            out, s_last = _s5_mixer(_rms_norm(x, norm_mix[layer]), s5_init[layer],
                                    *[p[layer] for p in s5_params])
            s5_final.append(s_last)
        else:
            if layer == N_A_LAYERS:
                mixer_b, new_rows = build_mixer_b(x)
            jb = layer - N_A_LAYERS
            q, gates = _nsa_queries(x, pos, norm_mix[layer], nsa_w_in[jb], q_norm[jb])
            out = mixer_b(q, gates) @ nsa_w_o[jb]
        x = x + out
        x = x + _sqrelu_mlp(_rms_norm(x, norm_mlp[layer]), w_up[layer], w_down[layer])
    return x, new_rows, jnp.stack(s5_final)


def setup_inputs(seed: int = 0) -> dict:
    key = jax.random.key(seed)
    ks = iter(jax.random.split(key, 40))
    f32 = jnp.float32

    def nrm(shape, scale):
        return scale * jax.random.normal(next(ks), shape, f32)

    n_pages = PAST_LEN // PAGE_SIZE
    n_pool = (5 * DEC_BATCH * n_pages + 3) // 4
    win_buf = min(WINDOW, PAST_LEN)
    row = (2, N_KV_HEADS, HEAD_DIM)
    inp = {}
    inp['x_prompt'] = nrm((BATCH, SEQ, D_MODEL), 1.0)
    inp['x_sample'] = nrm((DEC_BATCH, DEC_SEQ, D_MODEL), 1.0)
    inp['cache_cmp_kv'] = nrm((n_pool, PAGE_SIZE) + row, 1.0)
    inp['cache_slc_kv'] = nrm((n_pool, PAGE_SIZE) + row, 1.0)
    inp['cache_win_kv'] = nrm((DEC_BATCH, win_buf) + row, 1.0)
    inp['state_s5'] = nrm((N_A_LAYERS, DEC_BATCH, S5_GROUPS, S5_STATE, 2), 0.3)
    perm = jax.random.permutation(next(ks), n_pool)
    inp['page_table'] = perm[:DEC_BATCH * n_pages].reshape(DEC_BATCH, n_pages).astype(jnp.int32)
    inp['norm_mix'] = 1.0 + nrm((DEPTH, D_MODEL), 0.05)
    inp['norm_mlp'] = 1.0 + nrm((DEPTH, D_MODEL), 0.05)
    inp['w_up'] = nrm((DEPTH, D_MODEL, D_FF), D_MODEL ** -0.5)
    inp['w_down'] = nrm((DEPTH, D_FF, D_MODEL), D_FF ** -0.5)
    inp['s5_a_re'] = -0.5 + nrm((N_A_LAYERS, S5_GROUPS, S5_STATE), 0.01)
    inp['s5_a_im'] = math.pi * jnp.arange(S5_STATE, dtype=f32) + nrm((N_A_LAYERS, S5_GROUPS, S5_STATE), 0.01)
    u = jax.random.uniform(next(ks), (N_A_LAYERS, S5_GROUPS), f32)
    inp['s5_log_dt'] = math.log(DT_MIN) + u * (math.log(DT_MAX) - math.log(DT_MIN))
    inp['s5_b_re'] = nrm((N_A_LAYERS, S5_GROUPS, S5_STATE, S5_GROUP), S5_GROUP ** -0.5)
    inp['s5_b_im'] = nrm((N_A_LAYERS, S5_GROUPS, S5_STATE, S5_GROUP), S5_GROUP ** -0.5)
    inp['s5_c_re'] = nrm((N_A_LAYERS, S5_GROUPS, S5_GROUP, S5_STATE), (2 * S5_STATE) ** -0.5)
    inp['s5_c_im'] = nrm((N_A_LAYERS, S5_GROUPS, S5_GROUP, S5_STATE), (2 * S5_STATE) ** -0.5)
    inp['s5_d'] = nrm((N_A_LAYERS, D_MODEL), 0.5)
    inp['s5_w_glu'] = nrm((N_A_LAYERS, D_MODEL, 2 * D_MODEL), D_MODEL ** -0.5)
    inp['kv_norm'] = 1.0 + nrm((D_MODEL,), 0.05)
    inp['w_kv'] = nrm((D_MODEL, N_BRANCH * 2 * N_KV_HEADS * HEAD_DIM), D_MODEL ** -0.5)
    inp['k_norm'] = 1.0 + nrm((N_BRANCH, HEAD_DIM), 0.05)
    inp['cmp_pos'] = nrm((2, CMP_BLOCK, HEAD_DIM), 0.1)
    inp['cmp_w1'] = nrm((2, CMP_BLOCK * HEAD_DIM, CMP_HIDDEN), (CMP_BLOCK * HEAD_DIM) ** -0.5)
    inp['cmp_w2'] = nrm((2, CMP_HIDDEN, HEAD_DIM), CMP_HIDDEN ** -0.5)
    inp['nsa_w_in'] = nrm((N_B_LAYERS, D_MODEL, N_Q_HEADS * HEAD_DIM + N_BRANCH * N_Q_HEADS), D_MODEL ** -0.5)
    inp['q_norm'] = 1.0 + nrm((N_B_LAYERS, HEAD_DIM), 0.05)
    inp['nsa_w_o'] = nrm((N_B_LAYERS, N_Q_HEADS * HEAD_DIM, D_MODEL), (N_Q_HEADS * HEAD_DIM) ** -0.5)
    return inp


def reference(x_prompt, x_sample, cache_cmp_kv, cache_slc_kv, cache_win_kv, state_s5, page_table,
              norm_mix, norm_mlp, w_up, w_down,
              s5_a_re, s5_a_im, s5_log_dt, s5_b_re, s5_b_im, s5_c_re, s5_c_im, s5_d, s5_w_glu,
              kv_norm, w_kv, k_norm, cmp_pos, cmp_w1, cmp_w2, nsa_w_in, q_norm, nsa_w_o):
    s5_params = (s5_a_re, s5_a_im, s5_log_dt, s5_b_re, s5_b_im, s5_c_re, s5_c_im, s5_d, s5_w_glu)
    past_len = page_table.shape[1] * PAGE_SIZE
    pos_p = jnp.arange(x_prompt.shape[1])
    pos_s = past_len + jnp.arange(x_sample.shape[1])
    s5_zero = jnp.zeros((N_A_LAYERS, x_prompt.shape[0], S5_GROUPS, S5_STATE, 2), jnp.float32)

    def build_prompt(h):
        return _prompt_mixer_b(h, pos_p, kv_norm, w_kv, k_norm, cmp_pos, cmp_w1, cmp_w2)

    def build_sample(h):
        return _sample_mixer_b(h, pos_s, cache_cmp_kv, cache_slc_kv, cache_win_kv, page_table,
                               kv_norm, w_kv, k_norm, cmp_pos, cmp_w1, cmp_w2)

    y_prompt, rows_p, s5_prompt = _trunk(x_prompt, pos_p, s5_zero, build_prompt, norm_mix, norm_mlp,
                                         w_up, w_down, s5_params, nsa_w_in, q_norm, nsa_w_o)
    y_sample, rows_s, s5_sample = _trunk(x_sample, pos_s, state_s5, build_sample, norm_mix, norm_mlp,
                                         w_up, w_down, s5_params, nsa_w_in, q_norm, nsa_w_o)
    cmp_kv_prompt, slc_kv_prompt, win_kv_prompt = rows_p
    cmp_kv_sample, slc_kv_sample, win_kv_sample = rows_s
    return (y_prompt, y_sample, cmp_kv_prompt, cmp_kv_sample, slc_kv_prompt, slc_kv_sample,
            win_kv_prompt, win_kv_sample, s5_prompt, s5_sample)
```

```python
import math
from contextlib import ExitStack

import numpy as np
import concourse.bass as bass
import concourse.mybir as mybir
from concourse.bass_utils import run_bass_kernel_spmd

F32 = mybir.dt.float32
BF16 = mybir.dt.bfloat16
I32 = mybir.dt.int32
AF = mybir.ActivationFunctionType
ALU = mybir.AluOpType
AX = mybir.AxisListType

NS_DMA = 6
NCORES = 8
TP = 2048
ND = 16
NT = TP + ND
TT = [(0, 512), (512, 512), (1024, 512), (1536, 512), (2048, 16)]
D = 1024
NDC = 8
DFF = 4096
RMS_EPS = 1e-6
TWO_PI = 2.0 * math.pi


class Buf:
    __slots__ = ("name", "w", "r")

    def __init__(self, name):
        self.name = name
        self.w = None
        self.r = {}


class Prog:
    COMPUTE = ("pe", "act", "dve", "pool")
    QUEUES = ("sp", "pool", "act")

    def __init__(self, nc):
        self.nc = nc
        self.es = ExitStack()
        self.streams = {e: [] for e in ("pe", "act", "dve", "pool", "sp")}
        self.ccount = {e: 0 for e in self.COMPUTE}
        self.dcount = {q: 0 for q in self.QUEUES}
        self.known = {e: {} for e in self.streams}
        self.csem = {e: self.es.enter_context(nc.semaphore("cs_" + e)) for e in self.COMPUTE}
        self.dsem = {q: [self.es.enter_context(nc.semaphore("ds_%s%d" % (q, j))) for j in range(NS_DMA)]
                     for q in self.QUEUES}
        self.nbuf = 0
        self.pending_noinc = {e: False for e in self.COMPUTE}
        self.global_deps = []

    def barrier(self):
        deps = []
        for e in self.COMPUTE:
            assert not self.pending_noinc[e], e
            if self.ccount[e]:
                deps.append(("c", e, self.ccount[e]))
        for q in self.QUEUES:
            n = self.dcount[q]
            for j in range(max(0, n - NS_DMA), n):
                deps.append(("d", q, j))
        self.global_deps = deps

    def sbuf(self, name, shape, dt):
        return self.es.enter_context(self.nc.sbuf_tensor(name, list(shape), dt))

    def psum(self, name, shape, dt):
        return self.es.enter_context(self.nc.psum_tensor(name, list(shape), dt))

    def buf(self, name=None):
        self.nbuf += 1
        return Buf(name or ("b%d" % self.nbuf))

    def bufs(self, *dims):
        if len(dims) == 1:
            return [self.buf() for _ in range(dims[0])]
        return [self.bufs(*dims[1:]) for _ in range(dims[0])]

    def _deps(self, reads, writes):
        deps = []
        for b in reads:
            if b.w is not None:
                deps.append(b.w)
        for b in writes:
            if b.w is not None:
                deps.append(b.w)
            deps.extend(b.r.values())
        return deps

    def _emit_waits(self, stream, deps, is_dma=False):
        kn = self.known[stream]
        need = {}
        for kind, who, idx in deps:
            if kind == "c":
                if who == stream and not is_dma and who == "pe":
                    continue
                key = ("c", who)
                val = idx
            else:
                key = ("d", who, idx % NS_DMA)
                val = 16 * (idx // NS_DMA + 1)
            if kn.get(key, 0) >= val:
                continue
            if need.get(key, 0) < val:
                need[key] = val
        for key, val in need.items():
            kn[key] = val
            sem = self.csem[key[1]] if key[0] == "c" else self.dsem[key[1]][key[2]]
            self.streams[stream].append(("wait", sem, val))

    def op(self, eng, fn, reads=(), writes=(), inc=True):
        self._emit_waits(eng, self._deps(reads, writes) + self.global_deps)
        if inc:
            self.ccount[eng] += 1
            idx = self.ccount[eng]
            self.streams[eng].append(("op", fn, self.csem[eng], 1))
            self.pending_noinc[eng] = False
        else:
            idx = self.ccount[eng] + 1
            self.streams[eng].append(("op", fn, None, 0))
            self.pending_noinc[eng] = True
        me = ("c", eng, idx)
        for b in reads:
            b.r[eng] = me
        for b in writes:
            b.w = me
            b.r = {}
        return me

    def dma(self, fn, reads=(), writes=(), q="sp"):
        deps = self._deps(reads, writes) + self.global_deps
        n = self.dcount[q]
        if n >= NS_DMA:
            deps.append(("d", q, n - NS_DMA))
        self._emit_waits(q, deps, is_dma=True)
        self.dcount[q] += 1
        self.streams[q].append(("op", fn, self.dsem[q][n % NS_DMA], 16))
        me = ("d", q, n)
        key = "dma_%s%d" % (q, n % NS_DMA)
        for b in reads:
            b.r[key] = me
        for b in writes:
            b.w = me
            b.r = {}
        return me

    def finish(self):
        for e in self.COMPUTE:
            assert not self.pending_noinc[e], e
        deps = []
        for q in self.QUEUES:
            n = self.dcount[q]
            for j in range(max(0, n - NS_DMA), n):
                deps.append(("d", q, j))
        for e in self.COMPUTE:
            if self.ccount[e]:
                deps.append(("c", e, self.ccount[e]))
        self._emit_waits("sp", deps)
        streams = self.streams

        def run(engobj, items):
            for it in items:
                if it[0] == "wait":
                    engobj.wait_ge(it[1], it[2])
                else:
                    ins = it[1](engobj)
                    if it[2] is not None:
                        ins.then_inc(it[2], it[3])

        with self.nc.Block() as block:
            @block.sync
            def _(e):
                run(e, streams["sp"])

            @block.tensor
            def _(e):
                run(e, streams["pe"])

            @block.scalar
            def _(e):
                run(e, streams["act"])

            @block.vector
            def _(e):
                run(e, streams["dve"])

            @block.gpsimd
            def _(e):
                run(e, streams["pool"])
        self.es.close()


class KB:
    def __init__(self, nc, stage):
        self.nc = nc
        self.P = Prog(nc)
        self.stage = stage

    def mm(self, out, lhsT, rhs, start, stop, R, W, inc=True, tp=None):
        if tp is None:
            fn = lambda e: e.matmul(out, lhsT=lhsT, rhs=rhs, start=start, stop=stop)
        else:
            fn = lambda e: e.matmul(out, lhsT=lhsT, rhs=rhs, start=start, stop=stop, tile_position=tp)
        return self.P.op("pe", fn, R, W, inc=inc)

    def tr(self, out, in_, ident, R, W):
        return self.P.op("pe", lambda e: e.transpose(out, in_, ident), R, W)

    def act(self, out, in_, func, R, W, scale=1.0, bias=0.0):
        return self.P.op("act", lambda e: e.activation(out=out, in_=in_, func=func, bias=bias, scale=scale), R, W)

    def tt(self, eng, out, a, b, op, R, W):
        return self.P.op(eng, lambda e: e.tensor_tensor(out=out, in0=a, in1=b, op=op), R, W)

    def ts(self, eng, out, a, s1, s2, op0, op1, R, W):
        if s2 is None:
            fn = lambda e: e.tensor_scalar(out=out, in0=a, scalar1=s1, scalar2=None, op0=op0)
        else:
            fn = lambda e: e.tensor_scalar(out=out, in0=a, scalar1=s1, scalar2=s2, op0=op0, op1=op1)
        return self.P.op(eng, fn, R, W)

    def stt(self, out, in0, scalar, in1, op0, op1, R, W):
        return self.P.op("dve", lambda e: e.scalar_tensor_tensor(out=out, in0=in0, scalar=scalar, in1=in1,
                                                                  op0=op0, op1=op1), R, W)

    def cp(self, eng, out, in_, R, W):
        if eng == "act":
            return self.P.op("act", lambda e: e.activation(out=out, in_=in_, func=AF.Copy), R, W)
        return self.P.op(eng, lambda e: e.tensor_copy(out=out, in_=in_), R, W)

    def ms(self, eng, ap, val, W):
        return self.P.op(eng, lambda e: e.memset(ap, val), (), W)

    def dma(self, out, in_, R, W, q="sp"):
        return self.P.dma(lambda e: e.dma_start(out=out, in_=in_), R, W, q=q)

    def dbg(self, name, ap, R):
        if not getattr(self, "debug", False):
            return
        shape = list(ap.shape)
        t = self.nc.dram_tensor("dbg_" + name, shape, ap.dtype, kind="ExternalOutput")
        self.dma(t.ap(), ap, R, [])

    def declare_io(self):
        nc = self.nc
        di = lambda n, s, dt=F32: nc.dram_tensor(n, list(s), dt, kind="ExternalInput")
        do = lambda n, s, dt=F32: nc.dram_tensor(n, list(s), dt, kind="ExternalOutput")
        I = {}
        I["xp"] = di("xp", [TP, D])
        I["xs"] = di("xs", [ND, D])
        I["st5"] = di("st5", [2, ND, 8192])
        I["norm_mix"] = di("norm_mix", [4, D])
        I["norm_mlp"] = di("norm_mlp", [4, D])
        I["w_up"] = di("w_up", [4, D, DFF])
        I["w_down"] = di("w_down", [4, DFF, D])
        I["s5_a_re"] = di("s5_a_re", [2, 32, 128])
        I["s5_a_im"] = di("s5_a_im", [2, 32, 128])
        I["s5_log_dt"] = di("s5_log_dt", [2, 32, 2])
        I["s5_b_re"] = di("s5_b_re", [2, 32, 128, 16])
        I["s5_b_im"] = di("s5_b_im", [2, 32, 128, 16])
        I["s5_c_re"] = di("s5_c_re", [2, 64, 16, 64])
        I["s5_c_im"] = di("s5_c_im", [2, 64, 16, 64])
        I["s5_d"] = di("s5_d", [2, D])
        I["s5_w_glu"] = di("s5_w_glu", [2, D, 2 * D])
        I["kv_norm"] = di("kv_norm", [1, D])
        I["w_kv"] = di("w_kv", [D, 1536])
        I["k_norm"] = di("k_norm", [1, 192])
        I["cache_win"] = di("cache_win", [ND, 512 * 512])
        I["q_norm"] = di("q_norm", [1, 128])
        I["cmp_pos"] = di("cmp_pos", [2, 32, 64])
        I["cmp_w1"] = di("cmp_w1", [2, 2048, 256])
        I["cmp_w2"] = di("cmp_w2", [2, 256, 64])
        I["nsa_w_in"] = di("nsa_w_in", [2, D, 1072])
        I["nsa_w_o"] = di("nsa_w_o", [2, D, D])
        I["pt"] = di("pt", [1, ND * 16], I32)
        if self.stage >= 8:
            I["cache_cmp"] = di("cache_cmp", [2560 * 128, 512])
            I["cache_slc"] = di("cache_slc", [2560 * 128, 512])
        self.kc_scr = nc.dram_tensor("kc_scr", [ND, 128, 128], BF16, kind="Internal")
        self.vc_scr = nc.dram_tensor("vc_scr", [ND, 64, 256], BF16, kind="Internal")
        self.od_scr = nc.dram_tensor("od_scr", [2, ND, 4, 4, 65], F32, kind="Internal")
        O = {}
        O["yp"] = do("yp", [TP, D])
        O["ys"] = do("ys", [ND, D])
        O["cmp_p"] = do("cmp_p", [TP, 512])
        O["cmp_s"] = do("cmp_s", [ND, 512])
        O["slc_p"] = do("slc_p", [TP, 512])
        O["slc_s"] = do("slc_s", [ND, 512])
        O["win_p"] = do("win_p", [512, 512])
        O["win_s"] = do("win_s", [ND, 512, 512])
        O["s5p"] = do("s5p", [2, 32, 256])
        O["s5s"] = do("s5s", [2, ND, 8192])
        self.I, self.O = I, O

    def setup(self):
        P = self.P
        self.x = P.sbuf("x", [128, NDC, NT], F32)
        self.xB = P.bufs(NDC, 5)
        self.u = P.sbuf("u", [128, NDC, NT], BF16)
        self.uB = P.bufs(NDC, 5)
        self.identf = P.sbuf("identf", [128, 128], F32)
        self.identB = P.buf()
        self.onesb = P.sbuf("onesb", [128, 128], BF16)
        self.onesB = P.buf()
        self.G = P.sbuf("G", [128, NDC, 11], F32)
        self.GrowB = P.buf()
        self.ropeC = P.sbuf("ropeC", [128, 18, 32], F32)
        self.ropeS = P.sbuf("ropeS", [128, 18, 32], F32)
        self.ropeB = P.buf()
        self.Mneg = P.sbuf("Mneg", [128, 2, 3], F32)
        self.MnegB = P.buf()
        self.kn = P.sbuf("kn", [128, 5, 64], F32)
        self.knB = P.buf()
        self.GB = P.buf()
        self.sq = P.sbuf("sq", [128, NDC, 512], BF16)
        self.sqB = P.buf()
        self.rstd = [P.sbuf("rstd%d" % i, [128, 512], F32) for i in range(2)]
        self.rstdB = P.bufs(2)
        self.tmpf = [P.sbuf("tmpf%d" % i, [128, 512], F32) for i in range(3)]
        self.tmpfB = P.bufs(3)
        self.tmpi = 0
        self.ps = [P.psum("ps%d" % i, [128, 512], F32) for i in range(8)]
        self.psB = P.bufs(8)
        self.psrr = 0
        self.ov_bytes = 82 * 1024
        self.ov = P.sbuf("ov", [128, self.ov_bytes // 4], F32)
        self.ovB = P.buf()
        self.Grow = self.ov_view("Grow", 16384, [11, D], F32)

        self.ms("pool", self.identf[:], 1.0, [self.identB])
        P.op("pool", lambda e: e.affine_select(out=self.identf[:], in_=self.identf[:], pattern=[[-1, 128]],
                                               compare_op=ALU.is_equal, fill=0.0, base=0, channel_multiplier=1),
             [self.identB], [self.identB])
        self.ms("pool", self.onesb[:], 1.0, [self.onesB])
        I = self.I
        for i in range(4):
            self.dma(self.Grow[i:i + 1, :], I["norm_mix"].ap()[i:i + 1, :], [], [self.GrowB])
            self.dma(self.Grow[4 + i:5 + i, :], I["norm_mlp"].ap()[i:i + 1, :], [], [self.GrowB])
        for i in range(2):
            self.dma(self.Grow[8 + i:9 + i, :], I["s5_d"].ap()[i:i + 1, :], [], [self.GrowB])
        self.dma(self.Grow[10:11, :], I["kv_norm"].ap(), [], [self.GrowB])
        ps, psB = self.ps[0], self.psB[0]
        for dc in range(NDC):
            self.tr(ps[:, dc * 11:(dc + 1) * 11], self.Grow[:, dc * 128:(dc + 1) * 128], self.identf[:11, :11],
                    [self.GrowB, self.identB], [psB])
        self.cp("dve", self.G[:].rearrange("p a b -> p (a b)"), ps[:, 0:88], [psB], [self.GB])

    def next_ps(self, lo=0, hi=8):
        n = hi - lo
        i = lo + (self.psrr % n)
        self.psrr += 1
        return self.ps[i], self.psB[i]

    def next_tmp(self):
        i = self.tmpi % 3
        self.tmpi += 1
        return self.tmpf[i], self.tmpfB[i]

    def ov_view(self, name, byte_off, shape, dt):
        esz = 4 if dt in (F32, I32) else 2
        n = 1
        for s in shape[1:]:
            n *= s
        assert byte_off % 4 == 0 and byte_off + n * esz <= self.ov_bytes, (name, byte_off, n * esz)
        base = self.ov[:shape[0], byte_off // 4: byte_off // 4 + (n * esz + 3) // 4]
        v = base.bitcast(dt) if dt != F32 else base
        if len(shape) > 2:
            names = " ".join("a%d" % i for i in range(len(shape) - 1))
            kw = {"a%d" % i: shape[i + 1] for i in range(len(shape) - 1)}
            v = v.rearrange("p (%s) -> p %s" % (names, names), **kw)
        return v

    def load_x(self):
        I = self.I
        stg = [self.ov_view("stg%d" % i, i * 4096, [128, 1024], F32) for i in range(3)]
        stgB = self.P.bufs(3)
        tiles = [(I["xp"].ap()[t * 128:(t + 1) * 128, :], 128, t * 128) for t in range(16)]
        tiles.append((I["xs"].ap(), ND, TP))
        for ti, (src, n, t0) in enumerate(tiles):
            s, sB = stg[ti % 3], stgB[ti % 3]
            self.dma(s[:n, :], src, [], [sB])
            tt = min(t0 // 512, 4)
            for half in range(2):
                ps, psB = self.next_ps(0, 4)
                for j in range(4):
                    dc = half * 4 + j
                    self.tr(ps[:, j * 128:j * 128 + n], s[:n, dc * 128:(dc + 1) * 128], self.identf[:n, :n],
                            [sB, self.identB], [psB])
                eng = "act" if half == 0 else "dve"
                src_ap = ps[:].rearrange("p (j t) -> p j t", j=4)[:, :, :n]
                self.cp(eng, self.x[:, half * 4:half * 4 + 4, t0:t0 + n], src_ap,
                        [psB], [self.xB[dc][tt] for dc in range(half * 4, half * 4 + 4)])

    def rmsnorm(self, gi, dstB=None):
        for tt, (t0, n) in enumerate(TT):
            self.act(self.sq[:, :, :n], self.x[:, :, t0:t0 + n], AF.Square,
                     [self.xB[dc][tt] for dc in range(NDC)], [self.sqB])
            ps, psB = self.next_ps(0, 4)
            for dc in range(NDC):
                self.mm(ps[:, :n], self.onesb[:], self.sq[:, dc, :n], dc == 0, dc == NDC - 1,
                        [self.sqB, self.onesB], [psB], inc=(dc == NDC - 1))
            r, rB = self.rstd[tt % 2], self.rstdB[tt % 2]
            self.act(r[:, :n], ps[:, :n], AF.Sqrt, [psB], [rB], scale=1.0 / D, bias=RMS_EPS)
            self.P.op("dve", lambda e, r=r, n=n: e.reciprocal(out=r[:, :n], in_=r[:, :n]), [rB], [rB])
            for dc in range(NDC):
                self.stt(self.u[:, dc, t0:t0 + n], self.x[:, dc, t0:t0 + n], self.G[:, dc, gi:gi + 1], r[:, :n],
                         ALU.mult, ALU.mult, [self.xB[dc][tt], self.GB, rB], [self.uB[dc][tt]])

    def mlp(self, layer):
        I = self.I
        self.P.barrier()
        self.rmsnorm(4 + layer)
        FB = 512
        nfb = DFF // FB
        wup = [self.ov_view("wup%d" % i, i * 8192, [128, NDC, FB], BF16) for i in range(2)]
        wdn = [self.ov_view("wdn%d" % i, 16384 + i * 8192, [128, 4, D], BF16) for i in range(2)]
        hb = self.ov_view("hblk", 32768, [128, 4, NT], BF16)
        wupB, wdnB = self.P.bufs(2), self.P.bufs(2)
        hB = self.P.bufs(4, 5)
        wu, wd = I["w_up"], I["w_down"]
        for fb in range(nfb):
            k = fb % 2
            src = bass.AP(tensor=wu, offset=layer * D * DFF + fb * FB, ap=[[DFF, 128], [128 * DFF, NDC], [1, FB]])
            self.dma(wup[k][:], src, [], [wupB[k]], q="pool")
            src = bass.AP(tensor=wd, offset=layer * DFF * D + fb * FB * D, ap=[[D, 128], [128 * D, 4], [1, D]])
            self.dma(wdn[k][:], src, [], [wdnB[k]], q="pool")
            def up(tt):
                t0, n = TT[tt]
                for fc in range(4):
                    ps, psB = self.next_ps(0, 4)
                    for dc in range(NDC):
                        self.mm(ps[:, :n], wup[k][:, dc, fc * 128:(fc + 1) * 128], self.u[:, dc, t0:t0 + n],
                                dc == 0, dc == NDC - 1, [wupB[k], self.uB[dc][tt]], [psB], inc=(dc == NDC - 1))
                    tm, tmB = self.next_tmp()
                    self.act(tm[:, :n], ps[:, :n], AF.Relu, [psB], [tmB])
                    self.act(hb[:, fc, t0:t0 + n], tm[:, :n], AF.Square, [tmB], [hB[fc][tt]])

            def down(tt):
                t0, n = TT[tt]
                for dc in range(NDC):
                    ps, psB = self.next_ps(4, 8)
                    for fc in range(4):
                        self.mm(ps[:, :n], wdn[k][:, fc, dc * 128:(dc + 1) * 128], hb[:, fc, t0:t0 + n],
                                fc == 0, fc == 3, [wdnB[k], hB[fc][tt]], [psB], inc=(fc == 3))
                    self.tt("dve", self.x[:, dc, t0:t0 + n], self.x[:, dc, t0:t0 + n], ps[:, :n], ALU.add,
                            [psB, self.xB[dc][tt]], [self.xB[dc][tt]])
            up(0)
            for tt in range(1, 5):
                up(tt)
                down(tt - 1)
            down(4)

    def s5_prep(self, layer):
        I, P = self.I, self.P
        L = layer
        o = 0
        P.barrier()

        def ovt(name, shape, dt):
            nonlocal o
            esz = 4 if dt in (F32, I32) else 2
            n = 1
            for s in shape[1:]:
                n *= s
            v = self.ov_view(name, o, shape, dt)
            o += (n * esz + 31) // 32 * 32
            return v
        T = {}
        T["Wbr"] = ovt("Wbr", [128, NDC, 128], BF16)
        T["Wbi"] = ovt("Wbi", [128, NDC, 128], BF16)
        T["Cwr"] = ovt("Cwr", [128, 32, 32], BF16)
        T["Cwi"] = ovt("Cwi", [128, 32, 32], BF16)
        T["Pr"] = ovt("Pr", [128, 11, 32], F32)
        T["Pi"] = ovt("Pi", [128, 11, 32], F32)
        T["NPi"] = ovt("NPi", [128, 11, 32], F32)
        T["S0r"] = ovt("S0r", [128, 32, ND], F32)
        T["S0i"] = ovt("S0i", [128, 32, ND], F32)
        T["S1r"] = ovt("S1r", [128, 32, ND], F32)
        T["S1i"] = ovt("S1i", [128, 32, ND], F32)
        T["Finr"] = ovt("Finr", [128, 32], F32)
        T["Fini"] = ovt("Fini", [128, 32], F32)
        self.s5_scan_off = o
        TB = {k: P.buf("T_" + k) for k in T}
        self.T, self.TB = T, TB
        o2 = o
        def tmp(name, shape, dt=F32):
            nonlocal o2
            esz = 4 if dt in (F32, I32) else 2
            n = 1
            for s in shape[1:]:
                n *= s
            v = self.ov_view(name, o2, shape, dt)
            o2 += (n * esz + 31) // 32 * 32
            return v, P.buf(name)
        araw, arawB = tmp("araw", [32, 3, 128])
        st0c = [tmp("st0c%d" % i, [ND, 2048]) for i in range(2)]
        cnc = [tmp("cnc%d" % i, [16, 1024]) for i in range(2)]
        bre, breB = tmp("bre", [128, 32, 16])
        bim, bimB = tmp("bim", [128, 32, 16])
        A, AB = tmp("A", [128, 3, 32])
        E = {}
        for nm in ("lam", "th", "mag", "phi", "sn", "cs", "ar1", "ai1", "den", "t1", "t2", "zr", "zi"):
            E[nm] = tmp("e_" + nm, [128, 32])
        bbr, bbrB = tmp("bbr", [128, 32, 16])
        bbi, bbiB = tmp("bbi", [128, 32, 16])
        tb1, tb1B = tmp("tb1", [128, 32, 16])
        bpad, bpadB = tmp("bpad", [128, NDC, 128])
        ctr, ctrB = tmp("ctr", [128, 32, 16])
        cti, ctiB = tmp("cti", [128, 32, 16])

        self.dma(araw[:, 0, :], I["s5_a_re"].ap()[L], [], [arawB])
        self.dma(araw[:, 1, :], I["s5_a_im"].ap()[L], [], [arawB])
        ldr, ldrB = tmp("ldr", [32, 2])
        self.dma(ldr[:], I["s5_log_dt"].ap()[L], [], [ldrB])
        self.cp("dve", araw[:, 2, :].rearrange("p (g q) -> p g q", g=2), ldr[:].unsqueeze(2).to_broadcast([32, 2, 64]),
                [ldrB], [arawB])
        for c8 in range(8):
            src = bass.AP(tensor=I["s5_b_re"], offset=L * 65536 + c8 * 4 * 2048, ap=[[16, 128], [2048, 4], [1, 16]])
            self.dma(bre[:, 4 * c8:4 * c8 + 4, :], src, [], [breB])
            src = bass.AP(tensor=I["s5_b_im"], offset=L * 65536 + c8 * 4 * 2048, ap=[[16, 128], [2048, 4], [1, 16]])
            self.dma(bim[:, 4 * c8:4 * c8 + 4, :], src, [], [bimB])

        ps, psB = self.next_ps(0, 4)
        for j in range(3):
            self.tr(ps[:, j * 32:(j + 1) * 32], araw[:, j, :], self.identf[:32, :32], [arawB, self.identB], [psB])
        self.cp("dve", A[:].rearrange("p a b -> p (a b)"), ps[:, 0:96], [psB], [AB])
        ar, ai, ldt = A[:, 0, :], A[:, 1, :], A[:, 2, :]
        e = lambda nm: E[nm][0][:]
        eb = lambda nm: E[nm][1]
        V = "dve"
        self.act(e("t1"), ldt, AF.Exp, [AB], [eb("t1")])
        self.tt(V, e("lam"), ar, e("t1"), ALU.mult, [AB, eb("t1")], [eb("lam")])
        self.tt(V, e("th"), ai, e("t1"), ALU.mult, [AB, eb("t1")], [eb("th")])
        self.act(e("mag"), e("lam"), AF.Exp, [eb("lam")], [eb("mag")])
        qi, qiB = tmp("qi", [128, 32], I32)
        for shift, outn, key in ((0.0, "sn", "Pi"), (0.25, "cs", "Pr")):
            self.ts(V, e("phi"), e("th"), 1.0 / TWO_PI, shift, ALU.mult, ALU.add, [eb("th")], [eb("phi")])
            self.cp(V, qi[:], e("phi"), [eb("phi")], [qiB])
            self.cp(V, e("t2"), qi[:], [qiB], [eb("t2")])
            self.tt(V, e("phi"), e("phi"), e("t2"), ALU.subtract, [eb("phi"), eb("t2")], [eb("phi")])
            self.stt(e("t2"), e("phi"), 0.5, e("phi"), ALU.is_gt, ALU.subtract, [eb("phi")], [eb("t2")])
            self.stt(e("phi"), e("t2"), 0.5, e("t2"), ALU.is_gt, ALU.subtract, [eb("t2")], [eb("phi")])
            self.act(e(outn), e("phi"), AF.Sin, [eb("phi")], [eb(outn)], scale=TWO_PI)
            self.tt(V, T[key][:, 0, :], e(outn), e("mag"), ALU.mult, [eb(outn), eb("mag")], [TB[key]])
        self.dbg("A%d" % L, A[:], [AB])
        for nm_ in ("lam", "th", "mag", "sn", "cs", "phi"):
            self.dbg("e_%s%d" % (nm_, L), e(nm_), [eb(nm_)])
        for s in range(10):
            pr, pi = T["Pr"][:, s, :], T["Pi"][:, s, :]
            self.tt(V, e("t1"), pr, pr, ALU.mult, [TB["Pr"]], [eb("t1")])
            self.tt(V, e("t2"), pi, pi, ALU.mult, [TB["Pi"]], [eb("t2")])
            self.tt(V, T["Pr"][:, s + 1, :], e("t1"), e("t2"), ALU.subtract, [eb("t1"), eb("t2")], [TB["Pr"]])
            self.stt(T["Pi"][:, s + 1, :], pr, 2.0, pi, ALU.mult, ALU.mult, [TB["Pr"], TB["Pi"]], [TB["Pi"]])
        self.ts(V, T["NPi"][:], T["Pi"][:], -1.0, None, ALU.mult, None, [TB["Pi"]], [TB["NPi"]])
        self.tt(V, e("t1"), ar, ar, ALU.mult, [AB], [eb("t1")])
        self.tt(V, e("t2"), ai, ai, ALU.mult, [AB], [eb("t2")])
        self.tt(V, e("den"), e("t1"), e("t2"), ALU.add, [eb("t1"), eb("t2")], [eb("den")])
        self.P.op(V, lambda en: en.reciprocal(out=e("den"), in_=e("den")), [eb("den")], [eb("den")])
        self.ts(V, e("ar1"), T["Pr"][:, 0, :], -1.0, None, ALU.add, None, [TB["Pr"]], [eb("ar1")])
        self.tt(V, e("t1"), e("ar1"), ar, ALU.mult, [eb("ar1"), AB], [eb("t1")])
        self.tt(V, e("t2"), T["Pi"][:, 0, :], ai, ALU.mult, [TB["Pi"], AB], [eb("t2")])
        self.tt(V, e("zr"), e("t1"), e("t2"), ALU.add, [eb("t1"), eb("t2")], [eb("zr")])
        self.tt(V, e("zr"), e("zr"), e("den"), ALU.mult, [eb("zr"), eb("den")], [eb("zr")])
        self.tt(V, e("t1"), T["Pi"][:, 0, :], ar, ALU.mult, [TB["Pi"], AB], [eb("t1")])
        self.tt(V, e("t2"), e("ar1"), ai, ALU.mult, [eb("ar1"), AB], [eb("t2")])
        self.tt(V, e("zi"), e("t1"), e("t2"), ALU.subtract, [eb("t1"), eb("t2")], [eb("zi")])
        self.tt(V, e("zi"), e("zi"), e("den"), ALU.mult, [eb("zi"), eb("den")], [eb("zi")])
        zrb = E["zr"][0][:].unsqueeze(2).to_broadcast([128, 32, 16])
        zib = E["zi"][0][:].unsqueeze(2).to_broadcast([128, 32, 16])
        self.tt(V, bbr[:], bre[:], zrb, ALU.mult, [breB, eb("zr")], [bbrB])
        self.tt(V, tb1[:], bim[:], zib, ALU.mult, [bimB, eb("zi")], [tb1B])
        self.tt(V, bbr[:], bbr[:], tb1[:], ALU.subtract, [bbrB, tb1B], [bbrB])
        self.tt(V, bbi[:], bim[:], zrb, ALU.mult, [bimB, eb("zr")], [bbiB])
        self.tt(V, tb1[:], bre[:], zib, ALU.mult, [breB, eb("zi"), bbrB], [tb1B])
        self.tt(V, bbi[:], bbi[:], tb1[:], ALU.add, [bbiB, tb1B], [bbiB])
        for bb, bbB, key in ((bbr, bbrB, "Wbr"), (bbi, bbiB, "Wbi")):
            self.ms("pool", bpad[:], 0.0, [bpadB])
            for g2 in range(2):
                dst = bpad[64 * g2:64 * g2 + 64].rearrange("p d (q g c) -> p d q g c", q=4, g=2)[:, :, :, g2, :]
                srcv = bb[64 * g2:64 * g2 + 64].rearrange("p (d q) c -> p d q c", q=4)
                self.cp("pool", dst, srcv, [bbB], [bpadB])
            for half in range(2):
                ps, psB = self.next_ps(0, 4)
                for j in range(4):
                    dc = half * 4 + j
                    self.tr(ps[:, j * 128:(j + 1) * 128], bpad[:, dc, :], self.identf[:], [bpadB, self.identB], [psB])
                self.cp("act", T[key][:, half * 4:half * 4 + 4, :],
                        ps[:].rearrange("p (j t) -> p j t", j=4), [psB], [TB[key]])
        ci = 0
        for key, ct, ctB in (("s5_c_re", ctr, ctrB), ("s5_c_im", cti, ctiB)):
            ps, psB = self.next_ps(0, 4)
            for ch in range(4):
                cn, cnB = cnc[ci % 2]
                ci += 1
                src = bass.AP(tensor=I[key], offset=L * 65536 + ch * 16 * 1024, ap=[[64, 16], [1024, 16], [1, 64]])
                self.dma(cn[:].rearrange("c (g p) -> c g p", g=16), src, [], [cnB])
                for j in range(8):
                    pr_ = ch * 8 + j
                    self.tr(ps[:, pr_ * 16:(pr_ + 1) * 16], cn[:, j * 128:(j + 1) * 128], self.identf[:16, :16],
                            [cnB, self.identB], [psB])
            self.cp("dve", ct[:].rearrange("p a b -> p (a b)"), ps[:], [psB], [ctB])
        self.ms("pool", T["Cwr"][:], 0.0, [TB["Cwr"]])
        self.ms("pool", T["Cwi"][:], 0.0, [TB["Cwi"]])
        for g2 in range(2):
            self.cp("pool", T["Cwr"][64 * g2:64 * g2 + 64, :, 16 * g2:16 * g2 + 16], ctr[64 * g2:64 * g2 + 64],
                    [ctrB], [TB["Cwr"]])
            self.ts("pool", T["Cwi"][64 * g2:64 * g2 + 64, :, 16 * g2:16 * g2 + 16], cti[64 * g2:64 * g2 + 64],
                    -1.0, None, ALU.mult, None, [ctiB], [TB["Cwi"]])
        psr, psrB = self.next_ps(0, 4)
        psi, psiB = self.next_ps(0, 4)
        for ch in range(4):
            st0, st0B = st0c[ch % 2]
            self.dma(st0[:], I["st5"].ap()[L, :, ch * 2048:(ch + 1) * 2048], [], [st0B])
            st0v = st0[:].rearrange("b (q n r) -> b q n r", q=8, r=2)
            for j in range(8):
                pr_ = ch * 8 + j
                self.tr(psr[:, pr_ * ND:(pr_ + 1) * ND], st0v[:, j, :, 0], self.identf[:ND, :ND],
                        [st0B, self.identB], [psrB])
                self.tr(psi[:, pr_ * ND:(pr_ + 1) * ND], st0v[:, j, :, 1], self.identf[:ND, :ND],
                        [st0B, self.identB], [psiB])
        self.cp("dve", T["S0r"][:].rearrange("p a b -> p (a b)"), psr[:], [psrB], [TB["S0r"]])
        self.cp("dve", T["S0i"][:].rearrange("p a b -> p (a b)"), psi[:], [psiB], [TB["S0i"]])
        P.barrier()

    def s5_layer(self, layer):
        I, P, O = self.I, self.P, self.O
        P.barrier()
        self.rmsnorm(layer)
        self.dbg("u%d" % layer, self.u[:, :, 0:64], [b for r in self.uB for b in r])
        self.s5_prep(layer)
        T, TB = self.T, self.TB
        for k_ in ("Pr", "Pi", "Wbr", "Cwr", "Cwi", "S0r"):
            self.dbg("%s%d" % (k_, layer), T[k_][:], [TB[k_]])
        o = self.s5_scan_off
        LCH = 16
        NCH = TP // LCH
        WX = TP + LCH
        XA = [self.ov_view("XAr", o, [128, WX], F32), self.ov_view("XAi", o + 8256, [128, WX], F32)]
        XBf = [self.ov_view("XBr", o + 16512, [128, WX], F32), self.ov_view("XBi", o + 24768, [128, WX], F32)]
        S16 = [self.ov_view("S16r", o + 33024, [128, NT], BF16), self.ov_view("S16i", o + 33024 + 4160, [128, NT], BF16)]
        bud = [self.ov_view("budr", o + 33024 + 8320, [128, ND], F32), self.ov_view("budi", o + 33024 + 8320 + 64, [128, ND], F32)]
        HA = self.ov_view("HA", o + 41472 + 8192 + 1024, [128, 2, NCH], F32)
        HBt = self.ov_view("HB", o + 41472 + 8192 + 2048, [128, 2, NCH], F32)
        HAB, HBB = P.buf(), P.buf()
        XAB, XBB, S16B, budB = P.bufs(2), P.bufs(2), P.bufs(2), P.bufs(2)
        u, uB = self.u, self.uB
        ybank = [4, 5, 6, 7, 3]
        for dc in range(NDC):
            for q4 in range(4):
                pair = dc * 4 + q4
                rs = slice(32 * q4, 32 * q4 + 32)
                for tt, (t0, n) in enumerate(TT):
                    for ri, key in ((0, "Wbr"), (1, "Wbi")):
                        ps, psB = self.next_ps(0, 3)
                        self.mm(ps[:, :n], T[key][rs, dc, :], u[rs, dc, t0:t0 + n], True, True,
                                [TB[key], uB[dc][tt]], [psB], tp=(32 * q4, 0))
                        if tt < 4:
                            self.cp("act", XA[ri][:, t0:t0 + n], ps[:, :n], [psB], [XAB[ri]])
                        else:
                            self.cp("act", bud[ri][:], ps[:, :n], [psB], [budB[ri]])
                pr0, pi0, npi0 = T["Pr"][:, 0, pair:pair + 1], T["Pi"][:, 0, pair:pair + 1], T["NPi"][:, 0, pair:pair + 1]
                s0r, s0i = T["S0r"][:, pair, :], T["S0i"][:, pair, :]
                s1r, s1i = T["S1r"][:, pair, :], T["S1i"][:, pair, :]
                self.stt(s1r, s0r, pr0, bud[0][:], ALU.mult, ALU.add, [TB["S0r"], TB["Pr"], budB[0]], [TB["S1r"]])
                self.stt(s1r, s0i, npi0, s1r, ALU.mult, ALU.add, [TB["S0i"], TB["NPi"], TB["S1r"]], [TB["S1r"]])
                self.stt(s1i, s0i, pr0, bud[1][:], ALU.mult, ALU.add, [TB["S0i"], TB["Pr"], budB[1]], [TB["S1i"]])
                self.stt(s1i, s0r, pi0, s1i, ALU.mult, ALU.add, [TB["S0r"], TB["Pi"], TB["S1i"]], [TB["S1i"]])
                self.cp("pool", S16[0][:, TP:NT], s1r, [TB["S1r"]], [S16B[0]])
                self.cp("pool", S16[1][:, TP:NT], s1i, [TB["S1i"]], [S16B[1]])
                cur, curB, nxt, nxtB = XA, XAB, XBf, XBB
                v3 = lambda ap: ap[:, 0:WX].rearrange("p (k j) -> p k j", j=LCH)
                for ri, tab in ((0, "Pr"), (1, "Pi")):
                    self.ms("dve", cur[ri][:, TP:WX], 0.0, [curB[ri]])
                    self.cp("dve", cur[ri][:, TP:TP + 1], T[tab][:, 0, pair:pair + 1], [TB[tab]], [curB[ri]])
                for s in range(4):
                    d = 1 << s
                    pr = T["Pr"][:, s, pair:pair + 1]
                    pi = T["Pi"][:, s, pair:pair + 1]
                    npi = T["NPi"][:, s, pair:pair + 1]
                    RB = [curB[0], curB[1], TB["Pr"], TB["Pi"], TB["NPi"]]
                    c3 = [v3(cur[0]), v3(cur[1])]
                    n3 = [v3(nxt[0]), v3(nxt[1])]
                    self.cp("act", n3[0][:, :, 0:d], c3[0][:, :, 0:d], [curB[0]], [nxtB[0]])
                    self.cp("act", n3[1][:, :, 0:d], c3[1][:, :, 0:d], [curB[1]], [nxtB[1]])
                    self.stt(n3[0][:, :, d:LCH], c3[0][:, :, 0:LCH - d], pr, c3[0][:, :, d:LCH], ALU.mult, ALU.add, RB, [nxtB[0]])
                    self.stt(n3[0][:, :, d:LCH], c3[1][:, :, 0:LCH - d], npi, n3[0][:, :, d:LCH], ALU.mult, ALU.add, RB + [nxtB[0]], [nxtB[0]])
                    self.stt(n3[1][:, :, d:LCH], c3[1][:, :, 0:LCH - d], pr, c3[1][:, :, d:LCH], ALU.mult, ALU.add, RB, [nxtB[1]])
                    self.stt(n3[1][:, :, d:LCH], c3[0][:, :, 0:LCH - d], pi, n3[1][:, :, d:LCH], ALU.mult, ALU.add, RB + [nxtB[1]], [nxtB[1]])
                    cur, curB, nxt, nxtB = nxt, nxtB, cur, curB
                c3 = [v3(cur[0]), v3(cur[1])]
                for ri in range(2):
                    self.cp("act", HA[:, ri, :], c3[ri][:, 0:NCH, LCH - 1], [curB[ri]], [HAB])
                hc, hcB, hn, hnB = HA, HAB, HBt, HBB
                for s in range(7):
                    d = 1 << s
                    pr = T["Pr"][:, 4 + s, pair:pair + 1]
                    pi = T["Pi"][:, 4 + s, pair:pair + 1]
                    npi = T["NPi"][:, 4 + s, pair:pair + 1]
                    RB = [hcB, TB["Pr"], TB["Pi"], TB["NPi"]]
                    self.cp("act", hn[:, :, 0:d], hc[:, :, 0:d], [hcB], [hnB])
                    self.stt(hn[:, 0, d:NCH], hc[:, 0, 0:NCH - d], pr, hc[:, 0, d:NCH], ALU.mult, ALU.add, RB, [hnB])
                    self.stt(hn[:, 0, d:NCH], hc[:, 1, 0:NCH - d], npi, hn[:, 0, d:NCH], ALU.mult, ALU.add, RB + [hnB], [hnB])
                    self.stt(hn[:, 1, d:NCH], hc[:, 1, 0:NCH - d], pr, hc[:, 1, d:NCH], ALU.mult, ALU.add, RB, [hnB])
                    self.stt(hn[:, 1, d:NCH], hc[:, 0, 0:NCH - d], pi, hn[:, 1, d:NCH], ALU.mult, ALU.add, RB + [hnB], [hnB])
                    hc, hcB, hn, hnB = hn, hnB, hc, hcB
                shp = [128, NCH - 1, LCH]
                Trb = c3[0][:, NCH:NCH + 1, :].to_broadcast(shp)
                Tib = c3[1][:, NCH:NCH + 1, :].to_broadcast(shp)
                Hrb = hc[:, 0, 0:NCH - 1].unsqueeze(2).to_broadcast(shp)
                Hib = hc[:, 1, 0:NCH - 1].unsqueeze(2).to_broadcast(shp)
                g0 = v3(nxt[0])[:, 1:NCH, :]
                g1 = v3(nxt[1])[:, 1:NCH, :]
                xr = c3[0][:, 1:NCH, :]
                xi = c3[1][:, 1:NCH, :]
                CB = [curB[0], curB[1], hcB]
                self.tt("dve", g0, Trb, Hrb, ALU.mult, CB, [nxtB[0]])
                self.tt("dve", g1, Tib, Hib, ALU.mult, CB, [nxtB[1]])
                self.tt("dve", xr, xr, g0, ALU.add, [curB[0], nxtB[0]], [curB[0]])
                self.tt("dve", xr, xr, g1, ALU.subtract, [curB[0], nxtB[1]], [curB[0]])
                self.tt("dve", g0, Trb, Hib, ALU.mult, CB, [nxtB[0]])
                self.tt("dve", g1, Tib, Hrb, ALU.mult, CB, [nxtB[1]])
                self.tt("dve", xi, xi, g0, ALU.add, [curB[1], nxtB[0]], [curB[1]])
                self.tt("dve", xi, xi, g1, ALU.add, [curB[1], nxtB[1]], [curB[1]])
                self.cp("pool", T["Finr"][:, pair:pair + 1], cur[0][:, TP - 1:TP], [curB[0]], [TB["Finr"]])
                self.cp("pool", T["Fini"][:, pair:pair + 1], cur[1][:, TP - 1:TP], [curB[1]], [TB["Fini"]])
                self.cp("act", S16[0][:, 0:TP], cur[0][:, 0:TP], [curB[0]], [S16B[0]])
                self.cp("act", S16[1][:, 0:TP], cur[1][:, 0:TP], [curB[1]], [S16B[1]])
                for tt, (t0, n) in enumerate(TT):
                    yb = ybank[tt]
                    self.mm(self.ps[yb][rs, :n], T["Cwr"][:, pair, :], S16[0][:, t0:t0 + n], True, False,
                            [TB["Cwr"], S16B[0]], [self.psB[yb]], inc=False, tp=(0, 32 * q4))
                    self.mm(self.ps[yb][rs, :n], T["Cwi"][:, pair, :], S16[1][:, t0:t0 + n], False, True,
                            [TB["Cwi"], S16B[1]], [self.psB[yb]], tp=(0, 32 * q4))
            for tt, (t0, n) in enumerate(TT):
                yb = ybank[tt]
                y, yB = self.next_tmp()
                w, wB = self.next_tmp()
                self.stt(y[:, :n], u[:, dc, t0:t0 + n], self.G[:, dc, 8 + layer:9 + layer], self.ps[yb][:, :n],
                         ALU.mult, ALU.add, [uB[dc][tt], self.GB, self.psB[yb]], [yB])
                self.act(u[:, dc, t0:t0 + n], y[:, :n], AF.Gelu_apprx_tanh, [yB], [uB[dc][tt]])
        P.barrier()
        rows, rowsB = self.ov_view("s5rows", o + 41472 + 8192, [32, 128, 2], F32), P.buf()
        for ri, key in ((0, "Finr"), (1, "Fini")):
            ps, psB = self.next_ps(0, 3)
            self.tr(ps[:32, :128], T[key][:], self.identf[:], [TB[key], self.identB], [psB])
            self.cp("dve", rows[:, :, ri], ps[:32, :128], [psB, XAB[0], XAB[1]], [rowsB])
        self.dma(O["s5p"].ap()[layer], rows[:].rearrange("p a b -> p (a b)"), [rowsB], [])
        drow, drowB = self.ov_view("s5drows", o, [ND, 32, 128, 2], F32), P.buf()
        for ri, key in ((0, "S1r"), (1, "S1i")):
            for half in range(8):
                ps, psB = self.next_ps(0, 3)
                for j in range(4):
                    pr_ = half * 4 + j
                    self.tr(ps[:ND, j * 128:(j + 1) * 128], T[key][:, pr_, :], self.identf[:], [TB[key], self.identB], [psB])
                self.cp("dve", drow[:, half * 4:half * 4 + 4, :, ri], ps[:ND, :].rearrange("p (j t) -> p j t", j=4),
                        [psB, XBB[0], XBB[1]], [drowB])
        self.dma(O["s5s"].ap()[layer], drow[:].rearrange("p a b c -> p (a b c)"), [drowB], [])
        wg = [self.ov_view("wg%d" % i, o + 41472 + i * 4096, [128, NDC, 256], BF16) for i in range(2)]
        wgB = P.bufs(2)
        wsrc = I["s5_w_glu"]
        for fc in range(NDC):
            k = fc % 2
            for h in range(2):
                src = bass.AP(tensor=wsrc, offset=layer * D * 2 * D + h * D + fc * 128,
                              ap=[[2 * D, 128], [128 * 2 * D, NDC], [1, 128]])
                self.dma(wg[k][:, :, h * 128:(h + 1) * 128], src, [], [wgB[k]], q="pool")
            for tt, (t0, n) in enumerate(TT):
                ps1, ps1B = self.next_ps(0, 3)
                ps2, ps2B = self.next_ps(0, 3)
                for dc in range(NDC):
                    self.mm(ps1[:, :n], wg[k][:, dc, 0:128], u[:, dc, t0:t0 + n], dc == 0, dc == NDC - 1,
                            [wgB[k], uB[dc][tt]], [ps1B], inc=(dc == NDC - 1))
                for dc in range(NDC):
                    self.mm(ps2[:, :n], wg[k][:, dc, 128:256], u[:, dc, t0:t0 + n], dc == 0, dc == NDC - 1,
                            [wgB[k], uB[dc][tt]], [ps2B], inc=(dc == NDC - 1))
                sg, sgB = self.next_tmp()
                self.act(sg[:, :n], ps2[:, :n], AF.Sigmoid, [ps2B], [sgB])
                self.tt("dve", sg[:, :n], sg[:, :n], ps1[:, :n], ALU.mult, [sgB, ps1B], [sgB])
                self.tt("pool", self.x[:, fc, t0:t0 + n], self.x[:, fc, t0:t0 + n], sg[:, :n], ALU.add,
                        [sgB, self.xB[fc][tt]], [self.xB[fc][tt]])


    def rope_tables(self):
        P = self.P
        pos, posB = self.ov_view("rp_pos", 0, [128, 18], F32), P.buf()
        jf, jfB = self.ov_view("rp_j", 128, [128, 32], F32), P.buf()
        ang, angB = self.ov_view("rp_ang", 256, [128, 18, 32], F32), P.buf()
        q1, q1B = self.ov_view("rp_q1", 256 + 2304, [128, 18, 32], F32), P.buf()
        q2, q2B = self.ov_view("rp_q2", 256 + 2 * 2304, [128, 18, 32], F32), P.buf()
        qi, qiB = self.ov_view("rp_qi", 256 + 3 * 2304, [128, 18, 32], I32), P.buf()
        P.op("pool", lambda e: e.iota(pos[:, 0:16], pattern=[[128, 16]], base=0, channel_multiplier=1,
                                      allow_small_or_imprecise_dtypes=True), [], [posB])
        self.ms("pool", pos[:, 16:17], float(TP), [posB])
        P.op("pool", lambda e: e.iota(pos[:, 17:18], pattern=[[0, 1]], base=31, channel_multiplier=32,
                                      allow_small_or_imprecise_dtypes=True), [], [posB])
        P.op("pool", lambda e: e.iota(jf[:], pattern=[[1, 32]], base=0, channel_multiplier=0,
                                      allow_small_or_imprecise_dtypes=True), [], [jfB])
        self.act(jf[:], jf[:], AF.Exp, [jfB], [jfB], scale=-math.log(10000.0) / 32.0)
        self.tt("dve", ang[:], pos[:].unsqueeze(2).to_broadcast([128, 18, 32]),
                jf[:].unsqueeze(1).to_broadcast([128, 18, 32]), ALU.mult, [posB, jfB], [angB])
        for shift, dst in ((0.0, self.ropeS), (0.25, self.ropeC)):
            self.ts("dve", q1[:], ang[:], 1.0 / TWO_PI, shift, ALU.mult, ALU.add, [angB], [q1B])
            self.cp("dve", qi[:], q1[:], [q1B], [qiB])
            self.cp("dve", q2[:], qi[:], [qiB], [q2B])
            self.tt("dve", q1[:], q1[:], q2[:], ALU.subtract, [q1B, q2B], [q1B])
            self.stt(q2[:], q1[:], 0.5, q1[:], ALU.is_gt, ALU.subtract, [q1B], [q2B])
            self.stt(q1[:], q2[:], 0.5, q2[:], ALU.is_gt, ALU.subtract, [q2B], [q1B])
            self.act(dst[:], q1[:], AF.Sin, [q1B], [self.ropeB], scale=TWO_PI)
        src = bass.AP(tensor=self.I["k_norm"], offset=0, ap=[[0, 128], [1, 192]])
        self.dma(self.kn[:, 0:3, :].rearrange("p a b -> p (a b)"), src, [], [self.knB])
        src = bass.AP(tensor=self.I["q_norm"], offset=0, ap=[[0, 128], [1, 128]])
        self.dma(self.kn[:, 3:5, :].rearrange("p a b -> p (a b)"), src, [], [self.knB])
        mx, mxB = self.ov_view("rp_mx", 256 + 4 * 2304, [128, 5], F32), P.buf()
        P.op("dve", lambda e: e.tensor_reduce(out=mx[:], in_=self.kn[:], axis=AX.X, op=ALU.max,
                                              apply_absolute_value=True), [self.knB], [mxB])
        for jb in range(2):
            self.ts("dve", self.Mneg[:, jb, :], mx[:, 0:3], mx[:, 3 + jb:4 + jb], -8.0, ALU.mult, ALU.mult,
                    [mxB], [self.MnegB])

    def head_norm_rope(self, k, n, gain, tile, tmps):
        (kB,) = tmps[0]
        (sq, sqB), (t1, t1B), (t2, t2B) = tmps[1], tmps[2], tmps[3]
        (ss, ssB) = tmps[4]
        H = k.shape[1]
        v3 = lambda a: a[:n, :H * 64].rearrange("p (h d) -> p h d", h=H)
        h3 = lambda a: a[:n, :H * 32].rearrange("p (h d) -> p h d", h=H)
        self.act(v3(sq), k, AF.Square, [kB], [sqB])
        self.P.op("dve", lambda e: e.tensor_reduce(out=ss[:n, :H], in_=v3(sq), axis=AX.X, op=ALU.add), [sqB], [ssB])
        self.act(ss[:n, :H], ss[:n, :H], AF.Sqrt, [ssB], [ssB], scale=1.0 / 64.0, bias=RMS_EPS)
        self.P.op("dve", lambda e: e.reciprocal(out=ss[:n, :H], in_=ss[:n, :H]), [ssB], [ssB])
        self.tt("dve", k, k, ss[:n, :H].unsqueeze(2).to_broadcast([n, H, 64]), ALU.mult, [kB, ssB], [kB])
        self.tt("dve", k, k, gain.unsqueeze(1).to_broadcast([n, H, 64]), ALU.mult, [kB, self.knB], [kB])
        c = self.ropeC[:n, tile, :].unsqueeze(1).to_broadcast([n, H, 32])
        s_ = self.ropeS[:n, tile, :].unsqueeze(1).to_broadcast([n, H, 32])
        x1, x2 = k[:, :, 0:32], k[:, :, 32:64]
        a1, a2 = h3(t1), h3(t2)
        b1 = t1[:n, H * 32:H * 64].rearrange("p (h d) -> p h d", h=H)
        b2 = t2[:n, H * 32:H * 64].rearrange("p (h d) -> p h d", h=H)
        self.tt("dve", a1, x1, c, ALU.mult, [kB, self.ropeB], [t1B])
        self.tt("dve", a2, x2, s_, ALU.mult, [kB, self.ropeB], [t2B])
        self.tt("dve", b1, x2, c, ALU.mult, [kB, self.ropeB], [t1B])
        self.tt("dve", b2, x1, s_, ALU.mult, [kB, self.ropeB], [t2B])
        self.tt("dve", x1, a1, a2, ALU.subtract, [t1B, t2B], [kB])
        self.tt("dve", x2, b1, b2, ALU.add, [t1B, t2B], [kB])

    PB = 49280

    def nsa_persist(self):
        P, PB = self.P, self.PB
        N = {}
        N["kTs"] = self.ov_view("kTs", PB, [128, 2, NT], BF16)
        N["kTw"] = self.ov_view("kTw", PB + 8256, [128, 2, NT], BF16)
        N["Vs"] = self.ov_view("Vs", PB + 16512, [128, 16, 4, 65], BF16)
        N["Vw"] = self.ov_view("Vw", PB + 24832, [128, 16, 4, 65], BF16)
        N["kcT"] = self.ov_view("kcT", PB + 33152, [128, 2, 64], BF16)
        N["vc"] = self.ov_view("vc", PB + 33408, [64, 4, 64], BF16)
        self.N = N
        self.NB = {k: P.buf("N_" + k) for k in N}

    def gelu_to(self, out, y, yB, w, wB, n, outB, cols):
        self.act(out, y, AF.Gelu_apprx_tanh, [yB], outB)

    def kv_phase(self):
        I, O, P = self.I, self.O, self.P
        P.barrier()
        self.rope_tables()
        self.nsa_persist()
        N, NB = self.N, self.NB
        self.rmsnorm(10)
        P.barrier()
        wkv, wkvB = self.ov_view("wkv", 9728, [128, NDC, 1536], BF16), P.buf()
        src = bass.AP(tensor=I["w_kv"], offset=0, ap=[[1536, 128], [128 * 1536, NDC], [1, 1536]])
        self.dma(wkv[:], src, [], [wkvB], q="pool")
        kvst = [self.ov_view("kvst%d" % i, 34304 + i * 6144, [128, 1536], F32) for i in range(2)]
        kvstB = P.bufs(2)
        scr = [(self.ov_view("kvscr%d" % i, i * 1024, [128, 256], F32), P.buf()) for i in range(3)]
        ssb = (self.ov_view("kvss", 3 * 1024, [128, 8], F32), P.buf())
        for key in ("Vs", "Vw"):
            self.ms("pool", N[key][:, :, :, 64:65], 1.0, [NB[key]])
        tiles = [(t * 128, 128, t) for t in range(16)] + [(TP, ND, 16)]
        for ti, (t0, n, tile) in enumerate(tiles):
            st, stB = kvst[ti % 2], kvstB[ti % 2]
            tt = min(t0 // 512, 4)
            for br in range(3):
                ps, psB = self.next_ps(0, 6)
                for dc in range(NDC):
                    self.mm(ps[:n, :], self.u[:, dc, t0:t0 + n], wkv[:, dc, br * 512:(br + 1) * 512], dc == 0, dc == NDC - 1,
                            [self.uB[dc][tt], wkvB], [psB], inc=(dc == NDC - 1))
                self.cp("act", st[:n, br * 512:(br + 1) * 512], ps[:n, :], [psB], [stB])
            for br in (1, 2):
                k = st[:n, br * 512:br * 512 + 256].rearrange("p (h d) -> p h d", h=4)
                self.head_norm_rope(k, n, self.kn[:n, br, :], tile, [(stB,), scr[0], scr[1], scr[2], ssb])
            for br, kkey, vkey in ((1, "kTs", "Vs"), (2, "kTw", "Vw")):
                ps, psB = self.next_ps(6, 8)
                for a_ in range(2):
                    self.tr(ps[:, a_ * 128:a_ * 128 + n], st[:n, br * 512 + a_ * 128:br * 512 + (a_ + 1) * 128],
                            self.identf[:n, :n], [stB, self.identB], [psB])
                self.cp("act", N[kkey][:, :, t0:t0 + n], ps[:, 0:256].rearrange("p (a t) -> p a t", a=2)[:, :, :n],
                        [psB], [NB[kkey]])
                if tile < 16:
                    self.cp("pool", N[vkey][:, tile, :, 0:64],
                            st[:, br * 512 + 256:br * 512 + 512].rearrange("p (h d) -> p h d", h=4), [stB], [NB[vkey]])
            if tile < 16:
                self.dma(O["cmp_p"].ap()[t0:t0 + n, :], st[:n, 0:512], [stB], [])
                self.dma(O["slc_p"].ap()[t0:t0 + n, :], st[:n, 512:1024], [stB], [])
                if t0 >= TP - 512:
                    self.dma(O["win_p"].ap()[t0 - (TP - 512):t0 - (TP - 512) + n, :], st[:n, 1024:1536], [stB], [])
            else:
                self.dma(O["cmp_s"].ap(), st[:n, 0:512], [stB], [])
                self.dma(O["slc_s"].ap(), st[:n, 512:1024], [stB], [])
                self.dma(O["win_s"].ap()[:, 511, :], st[:n, 1024:1536], [stB], [])
        self.compress_prompt()

    def page_idx(self, byte_off):
        P, I = self.P, self.I
        ptb, ptbB = self.ov_view("ptb", byte_off, [128, 256], I32), P.buf()
        ptf, ptfB = self.ov_view("ptf", byte_off + 1024, [128, 256], F32), P.buf()
        iop, iopB = self.ov_view("iop", byte_off + 2048, [128, 1], F32), P.buf()
        src = bass.AP(tensor=I["pt"], offset=0, ap=[[0, 128], [1, 256]])
        self.dma(ptb[:], src, [], [ptbB])
        P.op("pool", lambda e: e.iota(iop[:], pattern=[[0, 1]], base=0, channel_multiplier=1,
                                      allow_small_or_imprecise_dtypes=True), [], [iopB])
        self.cp("dve", ptf[:], ptb[:], [ptbB], [ptfB])
        self.ts("dve", ptf[:], ptf[:], 128.0, iop[:, 0:1], ALU.mult, ALU.add, [ptfB, iopB], [ptfB])
        self.cp("dve", ptb[:], ptf[:], [ptfB], [ptbB])
        return ptb[:].rearrange("p (b i) -> p b i", b=ND), ptbB

    def gather_page(self, dst, table, idx_col, R, W):
        return self.P.dma(lambda e: e.indirect_dma_start(out=dst, out_offset=None, in_=table,
                                                         in_offset=bass.IndirectOffsetOnAxis(ap=idx_col, axis=0)),
                          R, W, q="pool")

    def compress_prompt(self):
        I, O, P = self.I, self.O, self.P
        N, NB = self.N, self.NB
        P.barrier()
        cst = [self.ov_view("cst%d" % i, i * 2048, [128, 512], F32) for i in range(2)]
        cstB = P.bufs(2)
        cTs = [(self.ov_view("cT0", 4096, [128, 4, 2080], BF16), P.buf()),
               (self.ov_view("cT1", 20736, [128, 4, 2080], BF16), P.buf())]
        w1rB = P.buf()
        w1all = self.u[:].rearrange("p a b -> p (a b)")[:, 0:16384].rearrange("p (s l h) -> p s l h", s=2, l=32)
        for s_ in range(2):
            for c2 in range(2):
                for l4 in range(4):
                    src = bass.AP(tensor=I["cmp_w1"], offset=s_ * 2048 * 256 + l4 * 8 * 64 * 256,
                                  ap=[[256, 64], [64 * 256, 8], [1, 256]])
                    self.dma(w1all[64 * c2:64 * c2 + 64, s_, 8 * l4:8 * l4 + 8, :], src, [], [w1rB], q="pool")
        w2, w2B = self.ov_view("w2c", 48128, [128, 2, 2, 64], BF16), P.buf()
        posst, posstB = self.ov_view("posst", 37632, [32, 2, 2, 64], F32), P.buf()
        hT, hTB = self.ov_view("hT", 38656, [128, 2, 64], BF16), P.buf()
        ctok, ctokB = self.ov_view("ctok", 39168, [64, 2, 4, 64], F32), P.buf()
        scr = [(self.ov_view("cscr%d" % i, 41216 + i * 1024, [128, 256], F32), P.buf()) for i in range(3)]
        ssb = (self.ov_view("css", 41216 + 3072, [128, 8], F32), P.buf())
        yb = (self.ov_view("cy", 41216 + 3104, [128, 64], F32), P.buf())
        wb = (self.ov_view("cw", 41216 + 3104 + 256, [128, 64], F32), P.buf())
        bcol = (self.ov_view("cb", 41216 + 3104 + 512, [128, 2], F32), P.buf())
        kcd = (self.ov_view("kcd", 41216 + 3104 + 544, [128, 2, 64], BF16), P.buf())
        vcd = (self.ov_view("vcd", 41216 + 3104 + 800, [64, 4, 64], BF16), P.buf())
        idx, idxB = self.page_idx(46016)
        for s_ in range(2):
            for c2 in range(2):
                self.dma(posst[:, s_, c2, :], I["cmp_pos"].ap()[s_], [], [posstB])
        for s_ in range(2):
            ps, psB = self.next_ps(0, 6)
            self.tr(ps[:, 0:32], posst[:, s_, :, :].rearrange("p a b -> p (a b)"), self.identf[:32, :32],
                    [posstB, self.identB], [psB])
            for a_ in range(2):
                for cT, cTB in cTs:
                    self.cp("dve", cT[:, 2 * s_ + a_, 2048:2080], ps[:, 0:32], [psB], [cTB])
        src = bass.AP(tensor=I["cmp_w2"], offset=0, ap=[[64, 128], [256 * 64, 2], [128 * 64, 2], [1, 64]])
        self.dma(w2[:], src, [], [w2B], q="pool")
        cmp_tab = I["cache_cmp"].ap()
        nseq = 1 + (ND if self.stage >= 8 else 0)
        def fill(seq):
            cT, cTB = cTs[seq % 2]
            for t in range(16):
                st, stB = cst[t % 2], cstB[t % 2]
                if seq == 0:
                    self.dma(st[:], O["cmp_p"].ap()[t * 128:(t + 1) * 128, :], [], [stB])
                else:
                    self.gather_page(st[:], cmp_tab, idx[:, seq - 1, t:t + 1], [idxB], [stB])
                ps, psB = self.next_ps(0, 6)
                for j in range(4):
                    self.tr(ps[:, j * 128:(j + 1) * 128], st[:, j * 128:(j + 1) * 128], self.identf[:], [stB, self.identB], [psB])
                self.cp("act" if t % 2 == 0 else "dve", cT[:, :, t * 128:(t + 1) * 128],
                        ps[:].rearrange("p (j t) -> p j t", j=4), [psB], [cTB])
        fill(0)
        for seq in range(nseq):
            if seq + 1 < nseq:
                fill(seq + 1)
            cT, cTB = cTs[seq % 2]
            for s_ in range(2):
                w1r = w1all[:, s_, :, :]
                for h in range(4):
                    b_, a_ = h % 2, h // 2
                    rs = slice(64 * b_, 64 * b_ + 64)
                    ps, psB = self.next_ps(0, 6)
                    for hh in range(2):
                        for l in range(32):
                            rhs = cT[rs, 2 * s_ + a_, :].rearrange("p (n l) -> p n l", l=32)[:, :, l]
                            self.mm(ps[:, hh * 65:hh * 65 + 65], w1r[rs, l, hh * 128:(hh + 1) * 128], rhs,
                                    hh == 0 and l == 0, hh == 1 and l == 31, [w1rB, cTB], [psB],
                                    inc=(hh == 1 and l == 31), tp=(64 * b_, 0))
                    self.cp("act", bcol[0][:].rearrange("p (a b) -> p a b", b=1),
                            ps[:, 0:130].rearrange("p (a b) -> p a b", a=2)[:, :, 64:65], [psB], [bcol[1]])
                    for hh in range(2):
                        self.ts("dve", yb[0][:], ps[:, hh * 65:hh * 65 + 64], bcol[0][:, hh:hh + 1], None, ALU.add, None,
                                [psB, bcol[1]], [yb[1]])
                        self.gelu_to(hT[:, hh, :], yb[0][:], yb[1], wb[0][:], wb[1], 128, [hTB], 64)
                    ps2, ps2B = self.next_ps(0, 6)
                    for hh in range(2):
                        self.mm(ps2[:64, 0:64], hT[:, hh, :], w2[:, s_, hh, :], hh == 0, hh == 1, [hTB, w2B], [ps2B], inc=(hh == 1))
                    self.cp("act", ctok[:, s_, h, :], ps2[:64, 0:64], [ps2B], [ctokB])
            self.head_norm_rope(ctok[:, 0, :, :], 64, self.kn[:64, 0, :], 17, [(ctokB,), scr[0], scr[1], scr[2], ssb])
            ps, psB = self.next_ps(0, 6)
            for a_ in range(2):
                self.tr(ps[:, a_ * 64:(a_ + 1) * 64], ctok[:, 0, 2 * a_:2 * a_ + 2, :].rearrange("p a b -> p (a b)"),
                        self.identf[:64, :64], [ctokB, self.identB], [psB])
            if seq == 0:
                self.cp("dve", N["kcT"][:].rearrange("p a b -> p (a b)"), ps[:, 0:128], [psB], [NB["kcT"]])
                self.cp("dve", N["vc"][:], ctok[:, 1, :, :], [ctokB], [NB["vc"]])
            else:
                self.cp("dve", kcd[0][:].rearrange("p a b -> p (a b)"), ps[:, 0:128], [psB], [kcd[1]])
                self.cp("dve", vcd[0][:], ctok[:, 1, :, :], [ctokB], [vcd[1]])
                self.dma(self.kc_scr.ap()[seq - 1], kcd[0][:].rearrange("p a b -> p (a b)"), [kcd[1]], [])
                self.dma(self.vc_scr.ap()[seq - 1], vcd[0][:].rearrange("p a b -> p (a b)"), [vcd[1]], [])

    def nsa_layer(self, jb):
        I, O, P = self.I, self.O, self.P
        N, NB = self.N, self.NB
        layer = 2 + jb
        P.barrier()
        self.rmsnorm(layer)
        P.barrier()
        u, uB = self.u, self.uB
        win_, winB = self.ov_view("nsa_win", 0, [128, NDC, 1072], BF16), P.buf()
        src = bass.AP(tensor=I["nsa_w_in"], offset=jb * D * 1072, ap=[[1072, 128], [128 * 1072, NDC], [1, 1072]])
        self.dma(win_[:], src, [], [winB], q="pool")
        qst = [self.ov_view("qst%d" % i, 17152 + i * 4288, [128, 1072], F32) for i in range(2)]
        qstB = P.bufs(2)
        scr = [(self.ov_view("qscr%d" % i, 25728 + i * 4096, [128, 1024], F32), P.buf()) for i in range(3)]
        ssb = (self.ov_view("qss", 38016, [128, 16], F32), P.buf())
        gat, gatB = self.ov_view("gat", 46016, [128, 17, 48], F32), P.buf()
        tiles = [(t * 128, 128, t) for t in range(16)] + [(TP, ND, 16)]
        for ti, (t0, n, tile) in enumerate(tiles):
            st, stB = qst[ti % 2], qstB[ti % 2]
            tt = min(t0 // 512, 4)
            for c0, cw in ((0, 512), (512, 512), (1024, 48)):
                ps, psB = self.next_ps(0, 6)
                for dc in range(NDC):
                    self.mm(ps[:n, :cw], u[:, dc, t0:t0 + n], win_[:, dc, c0:c0 + cw], dc == 0, dc == NDC - 1,
                            [uB[dc][tt], winB], [psB], inc=(dc == NDC - 1))
                self.cp("act", st[:n, c0:c0 + cw], ps[:n, :cw], [psB], [stB])
            q = st[:n, 0:1024].rearrange("p (h d) -> p h d", h=16)
            self.head_norm_rope(q, n, self.kn[:n, 3 + jb, :], tile, [(stB,), scr[0], scr[1], scr[2], ssb])
            self.act(gat[:n, tile, :], st[:n, 1024:1072], AF.Sigmoid, [stB], [gatB])
            qp, qpB = scr[0]
            for a_ in range(2):
                self.cp("pool", qp[:n, 512 * a_:512 * a_ + 512].rearrange("p (r b d) -> p r b d", r=4, b=2),
                        st[:n, 512 * a_:512 * a_ + 512].rearrange("p (b r d) -> p r b d", b=2, r=4), [stB], [qpB])
            for a_ in range(2):
                ps, psB = self.next_ps(6, 8)
                for r in range(4):
                    c0_ = (4 * a_ + r) * 128
                    self.tr(ps[:, r * 128:r * 128 + n], qp[:n, c0_:c0_ + 128], self.identf[:n, :n], [qpB, self.identB], [psB])
                self.cp("act" if a_ == 0 else "dve", u[:, 4 * a_:4 * a_ + 4, t0:t0 + n],
                        ps[:].rearrange("p (r t) -> p r t", r=4)[:, :, :n], [psB],
                        [uB[sl][tt] for sl in range(4 * a_, 4 * a_ + 4)])
        STOP = 99
        STOPA = 99
        if STOP <= 1:
            return
        P.barrier()
        qT = u
        qTB = [b for row in uB for b in row]
        off = [0]

        def alloc(name, shape, dt):
            esz = 4 if dt in (F32, I32) else 2
            nbytes = esz
            for d_ in shape[1:]:
                nbytes *= d_
            v = self.ov_view(name, off[0], shape, dt)
            off[0] += (nbytes + 31) // 32 * 32
            assert off[0] <= 46016, off[0]
            return v, P.buf(name)
        wo = [alloc("wo%d" % i, [128, NDC, 128], BF16) for i in range(2)]
        pc, pcB = alloc("pc", [128, 16, 64], F32)
        pcT, pcTB = alloc("pcT", [64, 16, 128], BF16)
        i1, i1B = alloc("i1", [128, 16, 32], F32)
        imp, impB = alloc("imp", [128, 4, 32], F32)
        sc2, sc2B = alloc("sc2", [128, 4, 32], F32)
        sel, selB = alloc("sel", [128, 4, 32], F32)
        m8, m8B = alloc("m8", [128, 2, 8], F32)
        ssum, ssumB = alloc("ssum", [128, 16], F32)
        cm, cmB = alloc("cm", [128, 64], F32)
        FM, FMB = alloc("FM", [128, 32], F32)
        selT, selTB = alloc("selT", [32, 4, 512], BF16)
        Ex, ExB = alloc("Ex", [32, 16, 128], BF16)
        otok, otokB = alloc("otok", [128, 4, 1024], F32)
        mk = [alloc("mk%d" % i, [128, 512], BF16) for i in range(2)]
        W, WB = alloc("Wband", [128, 1408], BF16)
        wgt, wgtB = alloc("wgt", [128, 4, 4], F32)
        oT, oTB = self.sq, self.sqB
        PT = [(self.tmpf[i][:].bitcast(BF16)[:, 0:512], self.tmpfB[i]) for i in range(3)]
        BIG = 1.0e30
        self.ms("pool", W[:], 1.0, [WB])
        P.op("pool", lambda e: e.affine_select(out=W[:], in_=W[:], pattern=[[1, 1408]], compare_op=ALU.is_ge, fill=0.0,
                                               base=-384, channel_multiplier=-1), [WB], [WB])
        P.op("pool", lambda e: e.affine_select(out=W[:], in_=W[:], pattern=[[-1, 1408]], compare_op=ALU.is_ge, fill=0.0,
                                               base=384 + 511, channel_multiplier=1), [WB], [WB])
        self.ms("pool", Ex[:], 1.0, [ExB])
        P.op("pool", lambda e: e.affine_select(out=Ex[:], in_=Ex[:], pattern=[[128, 16], [1, 128]], compare_op=ALU.is_ge,
                                               fill=0.0, base=0, channel_multiplier=-64), [ExB], [ExB])
        P.op("pool", lambda e: e.affine_select(out=Ex[:], in_=Ex[:], pattern=[[-128, 16], [-1, 128]], compare_op=ALU.is_ge,
                                               fill=0.0, base=63, channel_multiplier=64), [ExB], [ExB])
        mneg = lambda br: self.Mneg[:, jb, br:br + 1]
        wo_i = [0]
        pt_i = [0]
        mk_i = [0]
        for qt in range(4):
            Q0 = 512 * qt
            for sb in range(4):
                gs = 4 * qt + sb
                t0 = Q0 + 128 * sb
                psa = [self.next_ps(0, 3), self.next_ps(0, 3)]
                for h in range(16):
                    a_, b_, r = h // 8, (h % 8) // 4, h % 4
                    rs = slice(64 * b_, 64 * b_ + 64)
                    ps, psB = psa[b_]
                    hl = 4 * a_ + r
                    self.mm(ps[:, hl * 64:hl * 64 + 64], qT[rs, 4 * a_ + r, t0:t0 + 128], N["kcT"][rs, a_, :],
                            True, True, qTB + [NB["kcT"]], [psB], tp=(64 * b_, 0))
                for half in range(2):
                    self.act(pc[:].rearrange("p (a b r) n -> p a b r n", a=2, b=2)[:, :, half, :, :],
                             psa[half][0][:].rearrange("p (a r n) -> p a r n", a=2, r=4), AF.Exp,
                             [psa[half][1], self.MnegB], [pcB], scale=0.125, bias=mneg(0))
                if STOPA <= 1:
                    return
                self.ms("pool", cm[:], 1.0, [cmB])
                P.op("pool", lambda e, t0=t0: e.affine_select(out=cm[:], in_=cm[:], pattern=[[-32, 64]], compare_op=ALU.is_ge,
                                                               fill=0.0, base=t0 - 31, channel_multiplier=1), [cmB], [cmB])
                self.tt("dve", pc[:], pc[:], cm[:].unsqueeze(1).to_broadcast([128, 16, 64]), ALU.mult, [pcB, cmB], [pcB])
                P.op("dve", lambda e: e.tensor_reduce(out=ssum[:], in_=pc[:], axis=AX.X, op=ALU.add), [pcB], [ssumB])
                self.ts("dve", ssum[:], ssum[:], 1e-30, None, ALU.max, None, [ssumB], [ssumB])
                P.op("dve", lambda e: e.reciprocal(out=ssum[:], in_=ssum[:]), [ssumB], [ssumB])
                self.tt("dve", pc[:], pc[:], ssum[:].unsqueeze(2).to_broadcast([128, 16, 64]), ALU.mult, [pcB, ssumB], [pcB])
                if STOPA <= 2:
                    return
                P.op("dve", lambda e: e.tensor_reduce(out=i1[:], in_=pc[:].rearrange("p h (j t) -> p h j t", t=2),
                                                      axis=AX.X, op=ALU.add), [pcB], [i1B])
                P.op("dve", lambda e: e.tensor_reduce(out=imp[:], in_=i1[:].rearrange("p (g r) j -> p g j r", r=4),
                                                      axis=AX.X, op=ALU.add), [i1B], [impB])
                if STOPA <= 3:
                    return
                self.ms("pool", FM[:], 0.0, [FMB])
                for p0, cur in ((0, 2 * gs), (64, 2 * gs + 1)):
                    if cur + 1 < 32:
                        self.ms("pool", FM[p0:p0 + 64, cur + 1:32], -BIG, [FMB])
                    self.ms("pool", FM[p0:p0 + 64, 0:1], 1000.0, [FMB])
                    self.ms("pool", FM[p0:p0 + 64, cur:cur + 1], 1000.0, [FMB])
                    if cur >= 1:
                        self.ms("pool", FM[p0:p0 + 64, cur - 1:cur], 1000.0, [FMB])
                self.tt("dve", imp[:], imp[:], FM[:].unsqueeze(1).to_broadcast([128, 4, 32]), ALU.add, [impB, FMB], [impB])
                if STOPA <= 4:
                    return
                for g in range(4):
                    P.op("dve", lambda e, g=g: e.max(out=m8[:, 0, :], in_=imp[:, g, :]), [impB], [m8B])
                    P.op("dve", lambda e, g=g: e.match_replace(out=sc2[:, g, :], in_to_replace=m8[:, 0, :],
                                                               in_values=imp[:, g, :], imm_value=-BIG), [impB, m8B], [sc2B])
                    P.op("dve", lambda e, g=g: e.max(out=m8[:, 1, :], in_=sc2[:, g, :]), [sc2B], [m8B])
                    self.ts("dve", sel[:, g, :], imp[:, g, :], m8[:, 1, 7:8], None, ALU.is_ge, None, [impB, m8B], [selB])
                if STOPA <= 5:
                    return
                ps, psB = self.next_ps(0, 3)
                for g in range(4):
                    self.tr(ps[:32, g * 128:(g + 1) * 128], sel[:, g, :], self.identf[:], [selB, self.identB], [psB])
                self.cp("act", selT[:, :, sb * 128:(sb + 1) * 128], ps[:32, :].rearrange("p (g t) -> p g t", g=4),
                        [psB], [selTB])
                if STOPA <= 6:
                    return
                for hq in range(4):
                    ps, psB = self.next_ps(0, 3)
                    for r4 in range(4):
                        self.tr(ps[:64, r4 * 128:(r4 + 1) * 128], pc[:, 4 * hq + r4, :], self.identf[:], [pcB, self.identB], [psB])
                    self.cp("act", pcT[:, 4 * hq:4 * hq + 4, :], ps[:64, :].rearrange("p (r t) -> p r t", r=4), [psB], [pcTB])
                for half in range(2):
                    ps, psB = self.next_ps(0, 3)
                    for h8 in range(8):
                        h = 8 * half + h8
                        self.mm(ps[:, h8 * 64:(h8 + 1) * 64], pcT[:, h, :], N["vc"][:, h // 4, :], True, True,
                                [pcTB, NB["vc"]], [psB], inc=(h8 == 7))
                    gc = gat[:, gs, :].rearrange("p (h b) -> p h b", b=3)[:, 8 * half:8 * half + 8, 0:1]
                    self.tt("dve", otok[:, sb, 512 * half:512 * half + 512].rearrange("p (h d) -> p h d", h=8),
                            ps[:].rearrange("p (h d) -> p h d", h=8), gc.to_broadcast([128, 8, 64]), ALU.mult,
                            [psB, gatB], [otokB])
            if STOP <= 2:
                return
            for g in range(4):
                a_, b_ = g // 2, g % 2
                rs = slice(64 * b_, 64 * b_ + 64)
                for branch in ("slc", "win"):
                    kT, V = (N["kTs"], N["Vs"]) if branch == "slc" else (N["kTw"], N["Vw"])
                    kTB_, VB_ = (NB["kTs"], NB["Vs"]) if branch == "slc" else (NB["kTw"], NB["Vw"])
                    bri = 1 if branch == "slc" else 2
                    kts = list(range(0, 4 * qt + 4)) if branch == "slc" else list(range(max(0, 4 * qt - 4), 4 * qt + 4))
                    started = [False] * 4
                    items = [(kt, r) for kt in kts for r in range(4)]
                    SB = [0, 1, 7]
                    AHEAD = 2
                    state = {}

                    def stage_scores(ii):
                        kt, r = items[ii]
                        K0 = 128 * kt
                        delta = Q0 - K0
                        if r == 0:
                            wsl = W[:, delta + 384:delta + 384 + 512] if -384 <= delta <= 512 else None
                            if branch == "slc":
                                m_, mB_ = mk[mk_i[0] % 2]
                                mk_i[0] += 1
                                psm, psmB = self.ps[2], self.psB[2]
                                self.mm(psm[:, :], Ex[:, kt, :], selT[:, g, :], True, True, [ExB, selTB], [psmB])
                                if kt >= 4 * qt:
                                    self.tt("dve", m_[:], psm[:, :], wsl, ALU.mult, [psmB, WB], [mB_])
                                else:
                                    self.cp("act", m_[:], psm[:, :], [psmB], [mB_])
                                state[("mask", kt)] = (m_[:], mB_)
                            else:
                                state[("mask", kt)] = (wsl, WB)
                        bank = SB[ii % 3]
                        pss, pssB = self.ps[bank], self.psB[bank]
                        self.mm(pss[:, :], kT[rs, a_, K0:K0 + 128], qT[rs, 4 * a_ + r, Q0:Q0 + 512], True, True,
                                [kTB_] + qTB, [pssB], tp=(64 * b_, 0))

                    def stage_rest(ii):
                        kt, r = items[ii]
                        bank = SB[ii % 3]
                        pss, pssB = self.ps[bank], self.psB[bank]
                        msk, mskB = state[("mask", kt)]
                        pt, ptB = PT[ii % 3]
                        if branch == "slc":
                            subs = [sb for sb in range(4) if kt <= 4 * qt + sb]
                        else:
                            subs = [sb for sb in range(4) if 4 * qt + sb - 4 <= kt <= 4 * qt + sb]
                        self.act(pt, pss[:, :], AF.Exp, [pssB, self.MnegB], [ptB], scale=0.125, bias=mneg(bri))
                        self.tt("dve", pt, pt, msk, ALU.mult, [ptB, mskB], [ptB])
                        ob, obB = self.ps[3 + r], self.psB[3 + r]
                        for sb in subs:
                            first = not started[r]
                            started[r] = True
                            self.mm(ob[:, sb * 65:sb * 65 + 65], pt[:, sb * 128:(sb + 1) * 128], V[:, kt, g, :],
                                    first, False, [ptB, VB_], [obB], inc=(sb == subs[-1]))
                    for ii in range(min(AHEAD, len(items))):
                        stage_scores(ii)
                    for ii in range(len(items)):
                        if ii + AHEAD < len(items):
                            stage_scores(ii + AHEAD)
                        stage_rest(ii)
                    for r in range(4):
                        h = 4 * g + r
                        ob, obB = self.ps[3 + r], self.psB[3 + r]
                        ov_ = ob[:, 0:260].rearrange("p (s c) -> p s c", s=4)
                        self.ts("dve", wgt[:, r, :].unsqueeze(2), ov_[:, :, 64:65], 1e-30, None, ALU.max, None, [obB], [wgtB])
                        P.op("dve", lambda e, r=r: e.reciprocal(out=wgt[:, r, :], in_=wgt[:, r, :]), [wgtB], [wgtB])
                        gsel = gat[:, 4 * qt:4 * qt + 4, 3 * h + bri]
                        self.tt("dve", wgt[:, r, :], wgt[:, r, :], gsel, ALU.mult, [wgtB, gatB], [wgtB])
                        for sb in range(4):
                            dst = otok[:, sb, h * 64:(h + 1) * 64]
                            self.stt(dst, ov_[:, sb, 0:64], wgt[:, r, sb:sb + 1], dst, ALU.mult, ALU.add,
                                     [obB, wgtB, otokB], [otokB])
            if STOP <= 3:
                return
            for sb in range(4):
                for half in range(2):
                    ps, psB = self.next_ps(0, 3)
                    for j in range(4):
                        dc = 4 * half + j
                        self.tr(ps[:, j * 128:(j + 1) * 128], otok[:, sb, dc * 128:(dc + 1) * 128], self.identf[:],
                                [otokB, self.identB], [psB])
                    self.cp("act" if half == 0 else "dve", oT[:, 4 * half:4 * half + 4, sb * 128:(sb + 1) * 128],
                            ps[:].rearrange("p (j t) -> p j t", j=4), [psB], [oTB])
            for dco in range(NDC):
                w_, wB_ = wo[wo_i[0] % 2]
                wo_i[0] += 1
                src = bass.AP(tensor=I["nsa_w_o"], offset=jb * D * D + dco * 128, ap=[[D, 128], [128 * D, NDC], [1, 128]])
                self.dma(w_[:], src, [], [wB_], q="pool")
                ps, psB = self.ps[7], self.psB[7]
                for dci in range(NDC):
                    self.mm(ps[:, :], w_[:, dci, :], oT[:, dci, :], dci == 0, dci == NDC - 1, [wB_, oTB], [psB],
                            inc=(dci == NDC - 1))
                self.tt("dve", self.x[:, dco, Q0:Q0 + 512], self.x[:, dco, Q0:Q0 + 512], ps[:, :], ALU.add,
                        [psB, self.xB[dco][qt]], [self.xB[dco][qt]])
        if self.stage >= 8:
            self.nsa_decode(jb, gat, gatB)

    def nsa_decode(self, jb, gat, gatB):
        I, O, P = self.I, self.O, self.P
        N, NB = self.N, self.NB
        P.barrier()
        qT = self.u
        qTB = [b for row in self.uB for b in row]
        off = [0]

        def alloc(name, shape, dt):
            esz = 4 if dt in (F32, I32) else 2
            nbytes = esz
            for d_ in shape[1:]:
                nbytes *= d_
            v = self.ov_view(name, off[0], shape, dt)
            off[0] += (nbytes + 31) // 32 * 32
            assert off[0] <= 46016, off[0]
            return v, P.buf(name)
        idx, idxB = self.page_idx(0)
        off[0] = 2176
        stg = [alloc("dstg%d" % i, [128, 512], F32) for i in range(2)]
        kTp = [alloc("dkTp%d" % i, [128, 2, 128], BF16) for i in range(2)]
        Vp = [alloc("dVp%d" % i, [128, 4, 65], BF16) for i in range(2)]
        Vp0 = alloc("dVp0", [128, 4, 65], BF16)
        ptd = [alloc("dptd%d" % i, [128, 2, 2, 4], BF16) for i in range(2)]
        ptn = alloc("dptn", [16, 2, 2, 4], BF16)
        Mk, MkB = alloc("dMk", [128, 4, 16], BF16)
        ohb, ohbB = alloc("dohb", [16, 128], F32)
        kcb = [alloc("dkcb%d" % i, [128, 2, 64], BF16) for i in range(2)]
        vcb = [alloc("dvcb%d" % i, [64, 4, 64], BF16) for i in range(2)]
        Sacc, SaccB = alloc("dSacc", [16, 16, 64], F32)
        pc, pcB = alloc("dpc", [16, 16, 64], F32)
        i1, i1B = alloc("di1", [16, 16, 32], F32)
        imp, impB = alloc("dimp", [16, 4, 33], F32)
        sc2, sc2B = alloc("dsc2", [16, 4, 33], F32)
        sel, selB = alloc("dsel", [16, 4, 33], F32)
        m8, m8B = alloc("dm8", [16, 2, 8], F32)
        ssum, ssumB = alloc("dssum", [16, 16], F32)
        pcT, pcTB = alloc("dpcT", [64, 16, 16], BF16)
        ocacc, ocaccB = pc, pcB
        otd, otdB = alloc("dotd", [16, 16, 64], F32)
        Osb = [alloc("dOsb%d" % i, [4, 4, 65], F32) for i in range(2)]
        Otm, OtmB = alloc("dOtm", [16, 2, 16, 65], F32)
        wgt, wgtB = alloc("dwgt", [16, 16], F32)
        tmpo, tmpoB = Sacc, SaccB
        oTd, oTdB = alloc("doTd", [128, NDC, 16], BF16)
        wod = [alloc("dwod%d" % i, [128, NDC, 128], BF16) for i in range(1)]
        Vn, VnB = alloc("dVn", [16, 2, 4, 65], F32)
        Vd = [alloc("dVd%d" % i, [16, 4, 65], BF16) for i in range(2)]
        odB = P.buf("od_scr")
        mneg = lambda br: self.Mneg[:, jb, br:br + 1]
        idf = self.identf
        self.ms("pool", Vn[:], 1.0, [VnB])
        self.dma(Vn[:, 0, :, 0:64], O["slc_s"].ap()[:, 256:512].rearrange("b (h d) -> b h d", h=4), [], [VnB])
        self.dma(Vn[:, 1, :, 0:64], O["win_s"].ap()[:, 511, 256:512].rearrange("b (h d) -> b h d", h=4), [], [VnB])
        for v_, vB_ in Vp + [Vp0]:
            self.ms("pool", v_[:, :, 64:65], 1.0, [vB_])
        self.ms("pool", Sacc[:], 0.0, [SaccB])
        Saccv = Sacc[:].rearrange("p (a b r) n -> p a b r n", a=2, b=2)
        for b in range(ND):
            kc_, kcB_ = kcb[b % 2]
            self.dma(kc_[:].rearrange("p a b -> p (a b)"), self.kc_scr.ap()[b], [], [kcB_])
            psc = [(self.ps[0], self.psB[0]), (self.ps[1], self.psB[1])]
            for h in range(16):
                a_, b_, r = h // 8, (h % 8) // 4, h % 4
                rs = slice(64 * b_, 64 * b_ + 64)
                hl = 4 * a_ + r
                self.mm(psc[b_][0][:ND, hl * 64:hl * 64 + 64], qT[rs, 4 * a_ + r, TP:NT], kc_[rs, a_, :], True, True,
                        qTB + [kcB_], [psc[b_][1]], tp=(64 * b_, 0))
            for half in range(2):
                dst = Saccv[:, :, half, :, :]
                self.stt(dst, psc[half][0][:ND, :].rearrange("p (a r n) -> p a r n", a=2, r=4), idf[:ND, b:b + 1], dst,
                         ALU.mult, ALU.add, [psc[half][1], self.identB, SaccB], [SaccB])
        self.act(pc[:], Sacc[:], AF.Exp, [SaccB, self.MnegB], [pcB], scale=0.125, bias=self.Mneg[:ND, jb, 0:1])
        P.op("dve", lambda e: e.tensor_reduce(out=ssum[:], in_=pc[:], axis=AX.X, op=ALU.add), [pcB], [ssumB])
        self.ts("dve", ssum[:], ssum[:], 1e-30, None, ALU.max, None, [ssumB], [ssumB])
        P.op("dve", lambda e: e.reciprocal(out=ssum[:], in_=ssum[:]), [ssumB], [ssumB])
        self.tt("dve", pc[:], pc[:], ssum[:].unsqueeze(2).to_broadcast([ND, 16, 64]), ALU.mult, [pcB, ssumB], [pcB])
        P.op("dve", lambda e: e.tensor_reduce(out=i1[:], in_=pc[:].rearrange("p h (j t) -> p h j t", t=2),
                                              axis=AX.X, op=ALU.add), [pcB], [i1B])
        self.ms("pool", imp[:], 0.0, [impB])
        P.op("dve", lambda e: e.tensor_reduce(out=imp[:, :, 0:32], in_=i1[:].rearrange("p (g r) j -> p g j r", r=4),
                                              axis=AX.X, op=ALU.add), [i1B, impB], [impB])
        for j in (0, 31, 32):
            self.ts("dve", imp[:, :, j:j + 1], imp[:, :, j:j + 1], 1000.0, None, ALU.add, None, [impB], [impB])
        for g in range(4):
            P.op("dve", lambda e, g=g: e.max(out=m8[:, 0, :], in_=imp[:, g, :]), [impB], [m8B])
            P.op("dve", lambda e, g=g: e.match_replace(out=sc2[:, g, :], in_to_replace=m8[:, 0, :],
                                                       in_values=imp[:, g, :], imm_value=-1.0e30), [impB, m8B], [sc2B])
            P.op("dve", lambda e, g=g: e.max(out=m8[:, 1, :], in_=sc2[:, g, :]), [sc2B], [m8B])
            self.ts("dve", sel[:, g, :], imp[:, g, :], m8[:, 1, 7:8], None, ALU.is_ge, None, [impB, m8B], [selB])
        for hq in range(4):
            ps, psB = self.ps[2], self.psB[2]
            for r4 in range(4):
                self.tr(ps[:64, r4 * ND:(r4 + 1) * ND], pc[:, 4 * hq + r4, :], idf[:ND, :ND], [pcB, self.identB], [psB])
            self.cp("act", pcT[:, 4 * hq:4 * hq + 4, :], ps[:64, 0:4 * ND].rearrange("p (r t) -> p r t", r=4), [psB], [pcTB])
        self.ms("pool", ocacc[:], 0.0, [ocaccB])
        for b in range(ND):
            vc_, vcB_ = vcb[b % 2]
            self.dma(vc_[:].rearrange("p a b -> p (a b)"), self.vc_scr.ap()[b], [], [vcB_])
            for half in range(2):
                ps, psB = self.ps[half], self.psB[half]
                for h8 in range(8):
                    h = 8 * half + h8
                    self.mm(ps[:ND, h8 * 64:(h8 + 1) * 64], pcT[:, h, :], vc_[:, h // 4, :], True, True, [pcTB, vcB_], [psB],
                            inc=(h8 == 7))
                dst = ocacc[:, 8 * half:8 * half + 8, :]
                self.stt(dst, ps[:ND, :].rearrange("p (h d) -> p h d", h=8), idf[:ND, b:b + 1], dst, ALU.mult, ALU.add,
                         [psB, self.identB, ocaccB], [ocaccB])
        gt = gat[:ND, 16, :].rearrange("p (h b) -> p h b", b=3)
        self.tt("dve", otd[:], ocacc[:], gt[:, :, 0:1].to_broadcast([ND, 16, 64]), ALU.mult, [ocaccB, gatB], [otdB])
        slc_tab = I["cache_slc"].ap()
        tiles = []
        for b in range(ND):
            for bri, branch in ((1, "slc"), (2, "win")):
                ntile = 16 if branch == "slc" else 4
                for i in range(ntile + 1):
                    tiles.append((b, bri, branch, i, ntile))
        st8 = {}

        def prefetch(k_):
            b, bri, branch, i, ntile = tiles[k_]
            if bri == 1 and i == 0:
                self.cp("dve", ohb[:], idf[:ND, b:b + 1].to_broadcast([ND, 128]), [self.identB], [ohbB])
                psm, psmB = self.ps[3], self.psB[3]
                self.mm(psm[:, 0:132], ohb[:], sel[:].rearrange("p g j -> p (g j)"), True, True, [ohbB, selB], [psmB])
                pv = psm[:, 0:132].rearrange("p (g j) -> p g j", g=4)[:, :, 0:32].rearrange("p g (i t) -> p g i t", t=2)
                self.cp("dve", Mk[0:64], pv[0:64, :, :, 0], [psmB], [MkB])
                self.cp("dve", Mk[64:128], pv[64:128, :, :, 1], [psmB], [MkB])
            new_tok = (i == ntile)
            if not new_tok:
                st, stB = stg[k_ % 2]
                kt_, ktB_ = kTp[k_ % 2]
                if branch == "slc":
                    self.gather_page(st[:], slc_tab, idx[:, b, i:i + 1], [idxB], [stB])
                    v_, vB_ = Vp[k_ % 2]
                else:
                    src = I["cache_win"].ap()[b:b + 1, :].rearrange("o (r c) -> (o r) c", c=512)[i * 128:(i + 1) * 128, :]
                    self.dma(st[:], src, [], [stB])
                    v_, vB_ = (Vp0 if i == 0 else Vp[k_ % 2])
                    if jb == 0:
                        if i == 0:
                            self.dma(O["win_s"].ap()[b, 0:127, :], st[1:128, :], [stB], [])
                        else:
                            self.dma(O["win_s"].ap()[b, 128 * i - 1:128 * i + 127, :], st[:, :], [stB], [])
                pst, pstB = self.ps[2 if k_ % 2 == 0 else 7], self.psB[2 if k_ % 2 == 0 else 7]
                for a_ in range(2):
                    self.tr(pst[:, a_ * 128:(a_ + 1) * 128], st[:, a_ * 128:(a_ + 1) * 128], idf[:], [stB, self.identB], [pstB])
                self.cp("act", kt_[:], pst[:, 0:256].rearrange("p (a t) -> p a t", a=2), [pstB], [ktB_])
                self.cp("pool", v_[:, :, 0:64], st[:, 256:512].rearrange("p (h d) -> p h d", h=4), [stB], [vB_])
                if branch == "win" and i == 0:
                    self.ms("pool", v_[0:1, :, :], 0.0, [vB_])
                st8[k_] = (kt_, ktB_, v_, vB_, 128)
            else:
                vd_, vdB_ = Vd[b % 2]
                self.ts("dve", vd_[:], Vn[:, bri - 1, :, :], idf[:ND, b:b + 1], None, ALU.mult, None, [VnB, self.identB], [vdB_])
                st8[k_] = (None, None, vd_, vdB_, ND)

        def compute(k_):
            b, bri, branch, i, ntile = tiles[k_]
            new_tok = (i == ntile)
            kt_, ktB_, v_, vB_, nk = st8.pop(k_)
            psO, psOB = self.ps[3 + bri], self.psB[3 + bri]
            kTd = N["kTs"] if branch == "slc" else N["kTw"]
            kTdB = NB["kTs"] if branch == "slc" else NB["kTw"]
            pss = [(self.ps[0], self.psB[0]), (self.ps[1], self.psB[1])]
            for g in range(4):
                a_, b_ = g // 2, g % 2
                rs = slice(64 * b_, 64 * b_ + 64)
                lhsT = kt_[rs, a_, :] if not new_tok else kTd[rs, a_, TP:NT]
                lB = [ktB_] if not new_tok else [kTdB]
                self.mm(pss[b_][0][:nk, a_ * 4:a_ * 4 + 4], lhsT, qT[rs, 4 * a_:4 * a_ + 4, TP + b], True, True,
                        lB + qTB, [pss[b_][1]], tp=(64 * b_, 0))
            if not new_tok:
                pt_, ptB_ = ptd[k_ % 2]
            else:
                pt_, ptB_ = ptn
            for b_ in range(2):
                self.act(pt_[:nk, b_, :, :], pss[b_][0][:nk, 0:8].rearrange("p (a r) -> p a r", a=2), AF.Exp,
                         [pss[b_][1], self.MnegB], [ptB_], scale=0.125, bias=self.Mneg[:nk, jb, bri:bri + 1])
            if branch == "slc" and not new_tok:
                mkv = Mk[:, :, i].rearrange("p (a b) -> p b a", b=2).unsqueeze(3).to_broadcast([128, 2, 2, 4])
                self.tt("dve", pt_[:], pt_[:], mkv, ALU.mult, [ptB_, MkB], [ptB_])
            for g in range(4):
                a_, b_ = g // 2, g % 2
                self.mm(psO[:4, g * 65:(g + 1) * 65], pt_[:nk, b_, a_, :], v_[:nk, g, :], (i == 0 and g == 0), False,
                        [ptB_, vB_], [psOB], inc=(g == 3))
            if new_tok:
                ob, obB = Osb[bri - 1]
                self.cp("act", ob[:].rearrange("p a b -> p (a b)"), psO[:4, 0:260], [psOB], [obB])
                self.dma(self.od_scr.ap()[bri - 1, b].rearrange("g r c -> r g c"), ob[:], [obB], [odB])
        prefetch(0)
        for k_ in range(len(tiles)):
            if k_ + 1 < len(tiles):
                prefetch(k_ + 1)
            compute(k_)
        self.dma(Otm[:], self.od_scr.ap().rearrange("t b g r c -> b t (g r) c"), [odB], [OtmB])
        for bri in (1, 2):
            self.ts("dve", wgt[:].unsqueeze(2), Otm[:, bri - 1, :, 64:65], 1e-30, None, ALU.max, None, [OtmB], [wgtB])
            P.op("dve", lambda e: e.reciprocal(out=wgt[:], in_=wgt[:]), [wgtB], [wgtB])
            self.tt("dve", wgt[:], wgt[:], gt[:, :, bri], ALU.mult, [wgtB, gatB], [wgtB])
            self.tt("dve", tmpo[:], Otm[:, bri - 1, :, 0:64], wgt[:].unsqueeze(2).to_broadcast([ND, 16, 64]), ALU.mult,
                    [OtmB, wgtB], [tmpoB])
            self.tt("dve", otd[:], otd[:], tmpo[:], ALU.add, [otdB, tmpoB], [otdB])
        otf = otd[:].rearrange("p h d -> p (h d)")
        for half in range(2):
            ps, psB = self.ps[half], self.psB[half]
            for j in range(4):
                dc = 4 * half + j
                self.tr(ps[:, j * ND:(j + 1) * ND], otf[:, dc * 128:(dc + 1) * 128], idf[:ND, :ND], [otdB, self.identB], [psB])
            self.cp("act", oTd[:, 4 * half:4 * half + 4, :], ps[:, 0:4 * ND].rearrange("p (j t) -> p j t", j=4), [psB], [oTdB])
        for dco in range(NDC):
            w_, wB_ = wod[0]
            src = bass.AP(tensor=I["nsa_w_o"], offset=jb * D * D + dco * 128, ap=[[D, 128], [128 * D, NDC], [1, 128]])
            self.dma(w_[:], src, [], [wB_], q="pool")
            ps, psB = self.ps[6], self.psB[6]
            for dci in range(NDC):
                self.mm(ps[:, :ND], w_[:, dci, :], oTd[:, dci, :], dci == 0, dci == NDC - 1, [wB_, oTdB], [psB],
                        inc=(dci == NDC - 1))
            self.tt("dve", self.x[:, dco, TP:NT], self.x[:, dco, TP:NT], ps[:, :ND], ALU.add,
                    [psB, self.xB[dco][4]], [self.xB[dco][4]])

    def win_cache_copy(self):
        I, O = self.I, self.O
        for b in range(ND):
            src = I["cache_win"].ap()[b:b + 1, 512:512 * 512]
            dst = O["win_s"].ap()[b:b + 1, 0:511, :].rearrange("b r c -> b (r c)")
            self.dma(dst, src, [], [], q="act")

    def store_x(self):
        O = self.O
        self.P.barrier()
        stg = [self.ov_view("ostg%d" % i, i * 4096, [128, 1024], F32) for i in range(3)]
        stgB = self.P.bufs(3)
        tiles = [(O["yp"].ap()[t * 128:(t + 1) * 128, :], 128, t * 128) for t in range(16)]
        tiles.append((O["ys"].ap(), ND, TP))
        for ti, (dst, n, t0) in enumerate(tiles):
            s, sB = stg[ti % 3], stgB[ti % 3]
            tt = min(t0 // 512, 4)
            for half in range(2):
                ps, psB = self.next_ps(0, 4)
                for j in range(4):
                    dc = half * 4 + j
                    self.tr(ps[:n, j * 128:(j + 1) * 128], self.x[:, dc, t0:t0 + n], self.identf[:],
                            [self.xB[dc][tt], self.identB], [psB])
                self.cp("act" if half == 0 else "dve", s[:n, half * 512:(half + 1) * 512], ps[:n, :], [psB], [sB])
            self.dma(dst, s[:n, :], [sB], [])

    def build(self):
        self.declare_io()
        self.setup()
        self.load_x()
        st = self.stage
        self.dbg("x0", self.x[:, :, 0:64], [b for r in self.xB for b in r])
        if st >= 1:
            self.s5_layer(0)
        if st >= 2:
            self.mlp(0)
        if st >= 3:
            self.s5_layer(1)
            self.mlp(1)
        if st >= 4:
            self.kv_phase()
        if st >= 6:
            self.nsa_layer(0)
            self.mlp(2)
        if st >= 7:
            self.nsa_layer(1)
            self.mlp(3)
        self.store_x()
        self.P.finish()


def build_nc(stage=99, debug=False):
    nc = bass.Bass("TRN2", target_bir_lowering=False)
    kb = KB(nc, stage)
    kb.debug = debug
    kb.build()
    return nc


def make_in_maps(inp, stage=99):
    f = lambda a: np.ascontiguousarray(a, dtype=np.float32)
    if stage >= 8:
        cc = f(inp["cache_cmp_kv"]).reshape(2560 * 128, 512)
        cs = f(inp["cache_slc_kv"]).reshape(2560 * 128, 512)
    shared = {
        "norm_mix": f(inp["norm_mix"]), "norm_mlp": f(inp["norm_mlp"]),
        "w_up": f(inp["w_up"]), "w_down": f(inp["w_down"]),
        "s5_a_re": f(inp["s5_a_re"]).reshape(2, 32, 128), "s5_a_im": f(inp["s5_a_im"]).reshape(2, 32, 128),
        "s5_log_dt": f(inp["s5_log_dt"]).reshape(2, 32, 2),
        "s5_b_re": f(inp["s5_b_re"]).reshape(2, 32, 128, 16), "s5_b_im": f(inp["s5_b_im"]).reshape(2, 32, 128, 16),
        "s5_c_re": f(inp["s5_c_re"]), "s5_c_im": f(inp["s5_c_im"]),
        "s5_d": f(inp["s5_d"]), "s5_w_glu": f(inp["s5_w_glu"]),
        "kv_norm": f(inp["kv_norm"]).reshape(1, D), "w_kv": f(inp["w_kv"]), "k_norm": f(inp["k_norm"]).reshape(1, 192),
        "q_norm": f(inp["q_norm"]).reshape(1, 128), "cmp_pos": f(inp["cmp_pos"]), "cmp_w1": f(inp["cmp_w1"]),
        "cmp_w2": f(inp["cmp_w2"]), "nsa_w_in": f(inp["nsa_w_in"]), "nsa_w_o": f(inp["nsa_w_o"]),
    }
    maps = []
    for c in range(NCORES):
        m = dict(shared)
        m["xp"] = f(inp["x_prompt"][c])
        m["xs"] = f(inp["x_sample"][c * ND:(c + 1) * ND, 0])
        m["st5"] = f(inp["state_s5"][:, c * ND:(c + 1) * ND]).reshape(2, ND, 8192)
        m["cache_win"] = f(inp["cache_win_kv"][c * ND:(c + 1) * ND]).reshape(ND, 512 * 512)
        m["pt"] = np.ascontiguousarray(inp["page_table"][c * ND:(c + 1) * ND], dtype=np.int32).reshape(1, ND * 16)
        if stage >= 8:
            m["cache_cmp"] = cc
            m["cache_slc"] = cs
        maps.append(m)
    return maps


def assemble(results):
    r = results
    cat = lambda k: np.stack([r[c][k] for c in range(NCORES)])
    y_prompt = cat("yp")
    y_sample = np.concatenate([r[c]["ys"] for c in range(NCORES)])[:, None, :]
    kvp = lambda k: cat(k).reshape(NCORES, TP, 2, 4, 64)
    kvs = lambda k: np.concatenate([r[c][k] for c in range(NCORES)]).reshape(NCORES * ND, 1, 2, 4, 64)
    win_p = cat("win_p").reshape(NCORES, 512, 2, 4, 64)
    win_s = np.concatenate([r[c]["win_s"] for c in range(NCORES)]).reshape(NCORES * ND, 512, 2, 4, 64)
    s5p = np.stack([r[c]["s5p"] for c in range(NCORES)], axis=1).reshape(2, NCORES, 64, 64, 2)
    s5s = np.concatenate([r[c]["s5s"] for c in range(NCORES)], axis=1).reshape(2, NCORES * ND, 64, 64, 2)
    return (y_prompt, y_sample, kvp("cmp_p"), kvs("cmp_s"), kvp("slc_p"), kvs("slc_s"), win_p, win_s, s5p, s5s)


_NC_CACHE = {}


def kernel(**inputs):
    stage = 99
    if stage not in _NC_CACHE:
        _NC_CACHE[stage] = build_nc(stage)
    nc = _NC_CACHE[stage]
    maps = make_in_maps(inputs)
    res = run_bass_kernel_spmd(nc, maps, core_ids=list(range(NCORES)))
    outs = assemble(res.results)
    return tuple(np.ascontiguousarray(o, dtype=np.float32) for o in outs)
```

```python
import math
from contextlib import ExitStack

import numpy as np
import concourse.bass as bass
import concourse.mybir as mybir
from concourse.bass_utils import run_bass_kernel_spmd

F32 = mybir.dt.float32
BF16 = mybir.dt.bfloat16
I32 = mybir.dt.int32
AF = mybir.ActivationFunctionType
ALU = mybir.AluOpType
AX = mybir.AxisListType

NS_DMA = 6
NCORES = 8
TP = 2048
ND = 16
NT = TP + ND
TT = [(0, 512), (512, 512), (1024, 512), (1536, 512), (2048, 16)]
D = 1024
NDC = 8
DFF = 4096
RMS_EPS = 1e-6
TWO_PI = 2.0 * math.pi


class Buf:
    __slots__ = ("name", "w", "r")

    def __init__(self, name):
        self.name = name
        self.w = None
        self.r = {}


class Prog:
    COMPUTE = ("pe", "act", "dve", "pool")
    QUEUES = ("sp", "pool", "act")

    def __init__(self, nc):
        self.nc = nc
        self.es = ExitStack()
        self.streams = {e: [] for e in ("pe", "act", "dve", "pool", "sp")}
        self.ccount = {e: 0 for e in self.COMPUTE}
        self.dcount = {q: 0 for q in self.QUEUES}
        self.known = {e: {} for e in self.streams}
        self.csem = {e: self.es.enter_context(nc.semaphore("cs_" + e)) for e in self.COMPUTE}
        self.dsem = {q: [self.es.enter_context(nc.semaphore("ds_%s%d" % (q, j))) for j in range(NS_DMA)]
                     for q in self.QUEUES}
        self.nbuf = 0
        self.pending_noinc = {e: False for e in self.COMPUTE}
        self.global_deps = []

    def barrier(self):
        deps = []
        for e in self.COMPUTE:
            assert not self.pending_noinc[e], e
            if self.ccount[e]:
                deps.append(("c", e, self.ccount[e]))
        for q in self.QUEUES:
            n = self.dcount[q]
            for j in range(max(0, n - NS_DMA), n):
                deps.append(("d", q, j))
        self.global_deps = deps

    def sbuf(self, name, shape, dt):
        return self.es.enter_context(self.nc.sbuf_tensor(name, list(shape), dt))

    def psum(self, name, shape, dt):
        return self.es.enter_context(self.nc.psum_tensor(name, list(shape), dt))

    def buf(self, name=None):
        self.nbuf += 1
        return Buf(name or ("b%d" % self.nbuf))

    def bufs(self, *dims):
        if len(dims) == 1:
            return [self.buf() for _ in range(dims[0])]
        return [self.bufs(*dims[1:]) for _ in range(dims[0])]

    def _deps(self, reads, writes):
        deps = []
        for b in reads:
            if b.w is not None:
                deps.append(b.w)
        for b in writes:
            if b.w is not None:
                deps.append(b.w)
            deps.extend(b.r.values())
        return deps

    def _emit_waits(self, stream, deps, is_dma=False):
        kn = self.known[stream]
        need = {}
        for kind, who, idx in deps:
            if kind == "c":
                if who == stream and not is_dma and who == "pe":
                    continue
                key = ("c", who)
                val = idx
            else:
                key = ("d", who, idx % NS_DMA)
                val = 16 * (idx // NS_DMA + 1)
            if kn.get(key, 0) >= val:
                continue
            if need.get(key, 0) < val:
                need[key] = val
        for key, val in need.items():
            kn[key] = val
            sem = self.csem[key[1]] if key[0] == "c" else self.dsem[key[1]][key[2]]
            self.streams[stream].append(("wait", sem, val))

    def op(self, eng, fn, reads=(), writes=(), inc=True):
        self._emit_waits(eng, self._deps(reads, writes) + self.global_deps)
        if inc:
            self.ccount[eng] += 1
            idx = self.ccount[eng]
            self.streams[eng].append(("op", fn, self.csem[eng], 1))
            self.pending_noinc[eng] = False
        else:
            idx = self.ccount[eng] + 1
            self.streams[eng].append(("op", fn, None, 0))
            self.pending_noinc[eng] = True
        me = ("c", eng, idx)
        for b in reads:
            b.r[eng] = me
        for b in writes:
            b.w = me
            b.r = {}
        return me

    def dma(self, fn, reads=(), writes=(), q="sp"):
        deps = self._deps(reads, writes) + self.global_deps
        n = self.dcount[q]
        if n >= NS_DMA:
            deps.append(("d", q, n - NS_DMA))
        self._emit_waits(q, deps, is_dma=True)
        self.dcount[q] += 1
        self.streams[q].append(("op", fn, self.dsem[q][n % NS_DMA], 16))
        me = ("d", q, n)
        key = "dma_%s%d" % (q, n % NS_DMA)
        for b in reads:
            b.r[key] = me
        for b in writes:
            b.w = me
            b.r = {}
        return me

    def finish(self):
        for e in self.COMPUTE:
            assert not self.pending_noinc[e], e
        deps = []
        for q in self.QUEUES:
            n = self.dcount[q]
            for j in range(max(0, n - NS_DMA), n):
                deps.append(("d", q, j))
        for e in self.COMPUTE:
            if self.ccount[e]:
                deps.append(("c", e, self.ccount[e]))
        self._emit_waits("sp", deps)
        streams = self.streams

        def run(engobj, items):
            for it in items:
                if it[0] == "wait":
                    engobj.wait_ge(it[1], it[2])
                else:
                    ins = it[1](engobj)
                    if it[2] is not None:
                        ins.then_inc(it[2], it[3])

        with self.nc.Block() as block:
            @block.sync
            def _(e):
                run(e, streams["sp"])

            @block.tensor
            def _(e):
                run(e, streams["pe"])

            @block.scalar
            def _(e):
                run(e, streams["act"])

            @block.vector
            def _(e):
                run(e, streams["dve"])

            @block.gpsimd
            def _(e):
                run(e, streams["pool"])
        self.es.close()


class KB:
    def __init__(self, nc, stage):
        self.nc = nc
        self.P = Prog(nc)
        self.stage = stage

    def mm(self, out, lhsT, rhs, start, stop, R, W, inc=True, tp=None):
        if tp is None:
            fn = lambda e: e.matmul(out, lhsT=lhsT, rhs=rhs, start=start, stop=stop)
        else:
            fn = lambda e: e.matmul(out, lhsT=lhsT, rhs=rhs, start=start, stop=stop, tile_position=tp)
        return self.P.op("pe", fn, R, W, inc=inc)

    def tr(self, out, in_, ident, R, W):
        return self.P.op("pe", lambda e: e.transpose(out, in_, ident), R, W)

    def act(self, out, in_, func, R, W, scale=1.0, bias=0.0):
        return self.P.op("act", lambda e: e.activation(out=out, in_=in_, func=func, bias=bias, scale=scale), R, W)

    def tt(self, eng, out, a, b, op, R, W):
        return self.P.op(eng, lambda e: e.tensor_tensor(out=out, in0=a, in1=b, op=op), R, W)

    def ts(self, eng, out, a, s1, s2, op0, op1, R, W):
        if s2 is None:
            fn = lambda e: e.tensor_scalar(out=out, in0=a, scalar1=s1, scalar2=None, op0=op0)
        else:
            fn = lambda e: e.tensor_scalar(out=out, in0=a, scalar1=s1, scalar2=s2, op0=op0, op1=op1)
        return self.P.op(eng, fn, R, W)

    def stt(self, out, in0, scalar, in1, op0, op1, R, W):
        return self.P.op("dve", lambda e: e.scalar_tensor_tensor(out=out, in0=in0, scalar=scalar, in1=in1,
                                                                  op0=op0, op1=op1), R, W)

    def cp(self, eng, out, in_, R, W):
        if eng == "act":
            return self.P.op("act", lambda e: e.activation(out=out, in_=in_, func=AF.Copy), R, W)
        return self.P.op(eng, lambda e: e.tensor_copy(out=out, in_=in_), R, W)

    def ms(self, eng, ap, val, W):
        return self.P.op(eng, lambda e: e.memset(ap, val), (), W)

    def dma(self, out, in_, R, W, q="sp"):
        return self.P.dma(lambda e: e.dma_start(out=out, in_=in_), R, W, q=q)

    def dbg(self, name, ap, R):
        if not getattr(self, "debug", False):
            return
        shape = list(ap.shape)
        t = self.nc.dram_tensor("dbg_" + name, shape, ap.dtype, kind="ExternalOutput")
        self.dma(t.ap(), ap, R, [])

    def declare_io(self):
        nc = self.nc
        di = lambda n, s, dt=F32: nc.dram_tensor(n, list(s), dt, kind="ExternalInput")
        do = lambda n, s, dt=F32: nc.dram_tensor(n, list(s), dt, kind="ExternalOutput")
        I = {}
        I["xp"] = di("xp", [TP, D])
        I["xs"] = di("xs", [ND, D])
        I["st5"] = di("st5", [2, ND, 8192])
        I["norm_mix"] = di("norm_mix", [4, D])
        I["norm_mlp"] = di("norm_mlp", [4, D])
        I["w_up"] = di("w_up", [4, D, DFF])
        I["w_down"] = di("w_down", [4, DFF, D])
        I["s5_a_re"] = di("s5_a_re", [2, 32, 128])
        I["s5_a_im"] = di("s5_a_im", [2, 32, 128])
        I["s5_log_dt"] = di("s5_log_dt", [2, 32, 2])
        I["s5_b_re"] = di("s5_b_re", [2, 32, 128, 16])
        I["s5_b_im"] = di("s5_b_im", [2, 32, 128, 16])
        I["s5_c_re"] = di("s5_c_re", [2, 64, 16, 64])
        I["s5_c_im"] = di("s5_c_im", [2, 64, 16, 64])
        I["s5_d"] = di("s5_d", [2, D])
        I["s5_w_glu"] = di("s5_w_glu", [2, D, 2 * D])
        I["kv_norm"] = di("kv_norm", [1, D])
        I["w_kv"] = di("w_kv", [D, 1536])
        I["k_norm"] = di("k_norm", [1, 192])
        I["cache_win"] = di("cache_win", [ND, 512 * 512])
        I["q_norm"] = di("q_norm", [1, 128])
        I["cmp_pos"] = di("cmp_pos", [2, 32, 64])
        I["cmp_w1"] = di("cmp_w1", [2, 2048, 256])
        I["cmp_w2"] = di("cmp_w2", [2, 256, 64])
        I["nsa_w_in"] = di("nsa_w_in", [2, D, 1072])
        I["nsa_w_o"] = di("nsa_w_o", [2, D, D])
        I["pt"] = di("pt", [1, ND * 16], I32)
        if self.stage >= 8:
            I["cache_cmp"] = di("cache_cmp", [2560 * 128, 512])
            I["cache_slc"] = di("cache_slc", [2560 * 128, 512])
        self.kc_scr = nc.dram_tensor("kc_scr", [ND, 128, 128], BF16, kind="Internal")
        self.vc_scr = nc.dram_tensor("vc_scr", [ND, 64, 256], BF16, kind="Internal")
        self.od_scr = nc.dram_tensor("od_scr", [2, ND, 4, 4, 65], F32, kind="Internal")
        O = {}
        O["yp"] = do("yp", [TP, D])
        O["ys"] = do("ys", [ND, D])
        O["cmp_p"] = do("cmp_p", [TP, 512])
        O["cmp_s"] = do("cmp_s", [ND, 512])
        O["slc_p"] = do("slc_p", [TP, 512])
        O["slc_s"] = do("slc_s", [ND, 512])
        O["win_p"] = do("win_p", [512, 512])
        O["win_s"] = do("win_s", [ND, 512, 512])
        O["s5p"] = do("s5p", [2, 32, 256])
        O["s5s"] = do("s5s", [2, ND, 8192])
        self.I, self.O = I, O

    def setup(self):
        P = self.P
        self.x = P.sbuf("x", [128, NDC, NT], F32)
        self.xB = P.bufs(NDC, 5)
        self.u = P.sbuf("u", [128, NDC, NT], BF16)
        self.uB = P.bufs(NDC, 5)
        self.identf = P.sbuf("identf", [128, 128], F32)
        self.identB = P.buf()
        self.onesb = P.sbuf("onesb", [128, 128], BF16)
        self.onesB = P.buf()
        self.G = P.sbuf("G", [128, NDC, 11], F32)
        self.GrowB = P.buf()
        self.ropeC = P.sbuf("ropeC", [128, 18, 32], F32)
        self.ropeS = P.sbuf("ropeS", [128, 18, 32], F32)
        self.ropeB = P.buf()
        self.Mneg = P.sbuf("Mneg", [128, 2, 3], F32)
        self.MnegB = P.buf()
        self.kn = P.sbuf("kn", [128, 5, 64], F32)
        self.knB = P.buf()
        self.GB = P.buf()
        self.sq = P.sbuf("sq", [128, NDC, 512], BF16)
        self.sqB = P.buf()
        self.rstd = [P.sbuf("rstd%d" % i, [128, 512], F32) for i in range(2)]
        self.rstdB = P.bufs(2)
        self.tmpf = [P.sbuf("tmpf%d" % i, [128, 512], F32) for i in range(3)]
        self.tmpfB = P.bufs(3)
        self.tmpi = 0
        self.ps = [P.psum("ps%d" % i, [128, 512], F32) for i in range(8)]
        self.psB = P.bufs(8)
        self.psrr = 0
        self.ov_bytes = 82 * 1024
        self.ov = P.sbuf("ov", [128, self.ov_bytes // 4], F32)
        self.ovB = P.buf()
        self.Grow = self.ov_view("Grow", 16384, [11, D], F32)

        self.ms("pool", self.identf[:], 1.0, [self.identB])
        P.op("pool", lambda e: e.affine_select(out=self.identf[:], in_=self.identf[:], pattern=[[-1, 128]],
                                               compare_op=ALU.is_equal, fill=0.0, base=0, channel_multiplier=1),
             [self.identB], [self.identB])
        self.ms("pool", self.onesb[:], 1.0, [self.onesB])
        I = self.I
        for i in range(4):
            self.dma(self.Grow[i:i + 1, :], I["norm_mix"].ap()[i:i + 1, :], [], [self.GrowB])
            self.dma(self.Grow[4 + i:5 + i, :], I["norm_mlp"].ap()[i:i + 1, :], [], [self.GrowB])
        for i in range(2):
            self.dma(self.Grow[8 + i:9 + i, :], I["s5_d"].ap()[i:i + 1, :], [], [self.GrowB])
        self.dma(self.Grow[10:11, :], I["kv_norm"].ap(), [], [self.GrowB])
        ps, psB = self.ps[0], self.psB[0]
        for dc in range(NDC):
            self.tr(ps[:, dc * 11:(dc + 1) * 11], self.Grow[:, dc * 128:(dc + 1) * 128], self.identf[:11, :11],
                    [self.GrowB, self.identB], [psB])
        self.cp("dve", self.G[:].rearrange("p a b -> p (a b)"), ps[:, 0:88], [psB], [self.GB])

    def next_ps(self, lo=0, hi=8):
        n = hi - lo
        i = lo + (self.psrr % n)
        self.psrr += 1
        return self.ps[i], self.psB[i]

    def next_tmp(self):
        i = self.tmpi % 3
        self.tmpi += 1
        return self.tmpf[i], self.tmpfB[i]

    def ov_view(self, name, byte_off, shape, dt):
        esz = 4 if dt in (F32, I32) else 2
        n = 1
        for s in shape[1:]:
            n *= s
        assert byte_off % 4 == 0 and byte_off + n * esz <= self.ov_bytes, (name, byte_off, n * esz)
        base = self.ov[:shape[0], byte_off // 4: byte_off // 4 + (n * esz + 3) // 4]
        v = base.bitcast(dt) if dt != F32 else base
        if len(shape) > 2:
            names = " ".join("a%d" % i for i in range(len(shape) - 1))
            kw = {"a%d" % i: shape[i + 1] for i in range(len(shape) - 1)}
            v = v.rearrange("p (%s) -> p %s" % (names, names), **kw)
        return v

    def load_x(self):
        I = self.I
        stg = [self.ov_view("stg%d" % i, i * 4096, [128, 1024], F32) for i in range(3)]
        stgB = self.P.bufs(3)
        tiles = [(I["xp"].ap()[t * 128:(t + 1) * 128, :], 128, t * 128) for t in range(16)]
        tiles.append((I["xs"].ap(), ND, TP))
        for ti, (src, n, t0) in enumerate(tiles):
            s, sB = stg[ti % 3], stgB[ti % 3]
            self.dma(s[:n, :], src, [], [sB])
            tt = min(t0 // 512, 4)
            for half in range(2):
                ps, psB = self.next_ps(0, 4)
                for j in range(4):
                    dc = half * 4 + j
                    self.tr(ps[:, j * 128:j * 128 + n], s[:n, dc * 128:(dc + 1) * 128], self.identf[:n, :n],
                            [sB, self.identB], [psB])
                eng = "act" if half == 0 else "dve"
                src_ap = ps[:].rearrange("p (j t) -> p j t", j=4)[:, :, :n]
                self.cp(eng, self.x[:, half * 4:half * 4 + 4, t0:t0 + n], src_ap,
                        [psB], [self.xB[dc][tt] for dc in range(half * 4, half * 4 + 4)])

    def rmsnorm(self, gi, dstB=None):
        for tt, (t0, n) in enumerate(TT):
            self.act(self.sq[:, :, :n], self.x[:, :, t0:t0 + n], AF.Square,
                     [self.xB[dc][tt] for dc in range(NDC)], [self.sqB])
            ps, psB = self.next_ps(0, 4)
            for dc in range(NDC):
                self.mm(ps[:, :n], self.onesb[:], self.sq[:, dc, :n], dc == 0, dc == NDC - 1,
                        [self.sqB, self.onesB], [psB], inc=(dc == NDC - 1))
            r, rB = self.rstd[tt % 2], self.rstdB[tt % 2]
            self.act(r[:, :n], ps[:, :n], AF.Sqrt, [psB], [rB], scale=1.0 / D, bias=RMS_EPS)
            self.P.op("dve", lambda e, r=r, n=n: e.reciprocal(out=r[:, :n], in_=r[:, :n]), [rB], [rB])
            for dc in range(NDC):
                self.stt(self.u[:, dc, t0:t0 + n], self.x[:, dc, t0:t0 + n], self.G[:, dc, gi:gi + 1], r[:, :n],
                         ALU.mult, ALU.mult, [self.xB[dc][tt], self.GB, rB], [self.uB[dc][tt]])

    def mlp(self, layer):
        I = self.I
        self.P.barrier()
        self.rmsnorm(4 + layer)
        FB = 512
        nfb = DFF // FB
        wup = [self.ov_view("wup%d" % i, i * 8192, [128, NDC, FB], BF16) for i in range(2)]
        wdn = [self.ov_view("wdn%d" % i, 16384 + i * 8192, [128, 4, D], BF16) for i in range(2)]
        hb = self.ov_view("hblk", 32768, [128, 4, NT], BF16)
        wupB, wdnB = self.P.bufs(2), self.P.bufs(2)
        hB = self.P.bufs(4, 5)
        wu, wd = I["w_up"], I["w_down"]
        for fb in range(nfb):
            k = fb % 2
            src = bass.AP(tensor=wu, offset=layer * D * DFF + fb * FB, ap=[[DFF, 128], [128 * DFF, NDC], [1, FB]])
            self.dma(wup[k][:], src, [], [wupB[k]], q="pool")
            src = bass.AP(tensor=wd, offset=layer * DFF * D + fb * FB * D, ap=[[D, 128], [128 * D, 4], [1, D]])
            self.dma(wdn[k][:], src, [], [wdnB[k]], q="pool")
            def up(tt):
                t0, n = TT[tt]
                for fc in range(4):
                    ps, psB = self.next_ps(0, 4)
                    for dc in range(NDC):
                        self.mm(ps[:, :n], wup[k][:, dc, fc * 128:(fc + 1) * 128], self.u[:, dc, t0:t0 + n],
                                dc == 0, dc == NDC - 1, [wupB[k], self.uB[dc][tt]], [psB], inc=(dc == NDC - 1))
                    tm, tmB = self.next_tmp()
                    self.act(tm[:, :n], ps[:, :n], AF.Relu, [psB], [tmB])
                    self.act(hb[:, fc, t0:t0 + n], tm[:, :n], AF.Square, [tmB], [hB[fc][tt]])

            def down(tt):
                t0, n = TT[tt]
                for dc in range(NDC):
                    ps, psB = self.next_ps(4, 8)
                    for fc in range(4):
                        self.mm(ps[:, :n], wdn[k][:, fc, dc * 128:(dc + 1) * 128], hb[:, fc, t0:t0 + n],
                                fc == 0, fc == 3, [wdnB[k], hB[fc][tt]], [psB], inc=(fc == 3))
                    self.tt("dve", self.x[:, dc, t0:t0 + n], self.x[:, dc, t0:t0 + n], ps[:, :n], ALU.add,
                            [psB, self.xB[dc][tt]], [self.xB[dc][tt]])
            up(0)
            for tt in range(1, 5):
                up(tt)
                down(tt - 1)
            down(4)

    def s5_prep(self, layer):
        I, P = self.I, self.P
        L = layer
        o = 0

        def ovt(name, shape, dt):
            nonlocal o
            esz = 4 if dt in (F32, I32) else 2
            n = 1
            for s in shape[1:]:
                n *= s
            v = self.ov_view(name, o, shape, dt)
            o += (n * esz + 31) // 32 * 32
            return v
        T = {}
        T["Wbr"] = ovt("Wbr", [128, NDC, 128], BF16)
        T["Wbi"] = ovt("Wbi", [128, NDC, 128], BF16)
        T["Cwr"] = ovt("Cwr", [128, 32, 32], BF16)
        T["Cwi"] = ovt("Cwi", [128, 32, 32], BF16)
        T["Pr"] = ovt("Pr", [128, 11, 32], F32)
        T["Pi"] = ovt("Pi", [128, 11, 32], F32)
        T["NPi"] = ovt("NPi", [128, 11, 32], F32)
        T["S0r"] = ovt("S0r", [128, 32, ND], F32)
        T["S0i"] = ovt("S0i", [128, 32, ND], F32)
        T["S1r"] = ovt("S1r", [128, 32, ND], F32)
        T["S1i"] = ovt("S1i", [128, 32, ND], F32)
        T["Finr"] = ovt("Finr", [128, 32], F32)
        T["Fini"] = ovt("Fini", [128, 32], F32)
        self.s5_scan_off = o
        TB = {k: P.buf("T_" + k) for k in T}
        self.T, self.TB = T, TB
        o2 = o
        def tmp(name, shape, dt=F32):
            nonlocal o2
            esz = 4 if dt in (F32, I32) else 2
            n = 1
            for s in shape[1:]:
                n *= s
            v = self.ov_view(name, o2, shape, dt)
            o2 += (n * esz + 31) // 32 * 32
            return v, P.buf(name)
        araw, arawB = tmp("araw", [32, 3, 128])
        st0c = [tmp("st0c%d" % i, [ND, 2048]) for i in range(2)]
        cnc = [tmp("cnc%d" % i, [16, 1024]) for i in range(2)]
        bre, breB = tmp("bre", [128, 32, 16])
        bim, bimB = tmp("bim", [128, 32, 16])
        A, AB = tmp("A", [128, 3, 32])
        E = {}
        for nm in ("lam", "th", "mag", "phi", "sn", "cs", "ar1", "ai1", "den", "t1", "t2", "zr", "zi"):
            E[nm] = tmp("e_" + nm, [128, 32])
        bbr, bbrB = tmp("bbr", [128, 32, 16])
        bbi, bbiB = tmp("bbi", [128, 32, 16])
        tb1, tb1B = tmp("tb1", [128, 32, 16])
        bpad, bpadB = tmp("bpad", [128, NDC, 128])
        ctr, ctrB = tmp("ctr", [128, 32, 16])
        cti, ctiB = tmp("cti", [128, 32, 16])

        self.dma(araw[:, 0, :], I["s5_a_re"].ap()[L], [], [arawB])
        self.dma(araw[:, 1, :], I["s5_a_im"].ap()[L], [], [arawB])
        ldr, ldrB = tmp("ldr", [32, 2])
        self.dma(ldr[:], I["s5_log_dt"].ap()[L], [], [ldrB])
        self.cp("dve", araw[:, 2, :].rearrange("p (g q) -> p g q", g=2), ldr[:].unsqueeze(2).to_broadcast([32, 2, 64]),
                [ldrB], [arawB])
        for c8 in range(8):
            src = bass.AP(tensor=I["s5_b_re"], offset=L * 65536 + c8 * 4 * 2048, ap=[[16, 128], [2048, 4], [1, 16]])
            self.dma(bre[:, 4 * c8:4 * c8 + 4, :], src, [], [breB])
            src = bass.AP(tensor=I["s5_b_im"], offset=L * 65536 + c8 * 4 * 2048, ap=[[16, 128], [2048, 4], [1, 16]])
            self.dma(bim[:, 4 * c8:4 * c8 + 4, :], src, [], [bimB])

        ps, psB = self.next_ps(0, 4)
        for j in range(3):
            self.tr(ps[:, j * 32:(j + 1) * 32], araw[:, j, :], self.identf[:32, :32], [arawB, self.identB], [psB])
        self.cp("dve", A[:].rearrange("p a b -> p (a b)"), ps[:, 0:96], [psB], [AB])
        ar, ai, ldt = A[:, 0, :], A[:, 1, :], A[:, 2, :]
        e = lambda nm: E[nm][0][:]
        eb = lambda nm: E[nm][1]
        V = "dve"
        self.act(e("t1"), ldt, AF.Exp, [AB], [eb("t1")])
        self.tt(V, e("lam"), ar, e("t1"), ALU.mult, [AB, eb("t1")], [eb("lam")])
        self.tt(V, e("th"), ai, e("t1"), ALU.mult, [AB, eb("t1")], [eb("th")])
        self.act(e("mag"), e("lam"), AF.Exp, [eb("lam")], [eb("mag")])
        qi, qiB = tmp("qi", [128, 32], I32)
        for shift, outn, key in ((0.0, "sn", "Pi"), (0.25, "cs", "Pr")):
            self.ts(V, e("phi"), e("th"), 1.0 / TWO_PI, shift, ALU.mult, ALU.add, [eb("th")], [eb("phi")])
            self.cp(V, qi[:], e("phi"), [eb("phi")], [qiB])
            self.cp(V, e("t2"), qi[:], [qiB], [eb("t2")])
            self.tt(V, e("phi"), e("phi"), e("t2"), ALU.subtract, [eb("phi"), eb("t2")], [eb("phi")])
            self.stt(e("t2"), e("phi"), 0.5, e("phi"), ALU.is_gt, ALU.subtract, [eb("phi")], [eb("t2")])
            self.stt(e("phi"), e("t2"), 0.5, e("t2"), ALU.is_gt, ALU.subtract, [eb("t2")], [eb("phi")])
            self.act(e(outn), e("phi"), AF.Sin, [eb("phi")], [eb(outn)], scale=TWO_PI)
            self.tt(V, T[key][:, 0, :], e(outn), e("mag"), ALU.mult, [eb(outn), eb("mag")], [TB[key]])
        self.dbg("A%d" % L, A[:], [AB])
        for nm_ in ("lam", "th", "mag", "sn", "cs", "phi"):
            self.dbg("e_%s%d" % (nm_, L), e(nm_), [eb(nm_)])
        for s in range(10):
            pr, pi = T["Pr"][:, s, :], T["Pi"][:, s, :]
            self.tt(V, e("t1"), pr, pr, ALU.mult, [TB["Pr"]], [eb("t1")])
            self.tt(V, e("t2"), pi, pi, ALU.mult, [TB["Pi"]], [eb("t2")])
            self.tt(V, T["Pr"][:, s + 1, :], e("t1"), e("t2"), ALU.subtract, [eb("t1"), eb("t2")], [TB["Pr"]])
            self.stt(T["Pi"][:, s + 1, :], pr, 2.0, pi, ALU.mult, ALU.mult, [TB["Pr"], TB["Pi"]], [TB["Pi"]])
        self.ts(V, T["NPi"][:], T["Pi"][:], -1.0, None, ALU.mult, None, [TB["Pi"]], [TB["NPi"]])
        self.tt(V, e("t1"), ar, ar, ALU.mult, [AB], [eb("t1")])
        self.tt(V, e("t2"), ai, ai, ALU.mult, [AB], [eb("t2")])
        self.tt(V, e("den"), e("t1"), e("t2"), ALU.add, [eb("t1"), eb("t2")], [eb("den")])
        self.P.op(V, lambda en: en.reciprocal(out=e("den"), in_=e("den")), [eb("den")], [eb("den")])
        self.ts(V, e("ar1"), T["Pr"][:, 0, :], -1.0, None, ALU.add, None, [TB["Pr"]], [eb("ar1")])
        self.tt(V, e("t1"), e("ar1"), ar, ALU.mult, [eb("ar1"), AB], [eb("t1")])
        self.tt(V, e("t2"), T["Pi"][:, 0, :], ai, ALU.mult, [TB["Pi"], AB], [eb("t2")])
        self.tt(V, e("zr"), e("t1"), e("t2"), ALU.add, [eb("t1"), eb("t2")], [eb("zr")])
        self.tt(V, e("zr"), e("zr"), e("den"), ALU.mult, [eb("zr"), eb("den")], [eb("zr")])
        self.tt(V, e("t1"), T["Pi"][:, 0, :], ar, ALU.mult, [TB["Pi"], AB], [eb("t1")])
        self.tt(V, e("t2"), e("ar1"), ai, ALU.mult, [eb("ar1"), AB], [eb("t2")])
        self.tt(V, e("zi"), e("t1"), e("t2"), ALU.subtract, [eb("t1"), eb("t2")], [eb("zi")])
        self.tt(V, e("zi"), e("zi"), e("den"), ALU.mult, [eb("zi"), eb("den")], [eb("zi")])
        zrb = E["zr"][0][:].unsqueeze(2).to_broadcast([128, 32, 16])
        zib = E["zi"][0][:].unsqueeze(2).to_broadcast([128, 32, 16])
        self.tt(V, bbr[:], bre[:], zrb, ALU.mult, [breB, eb("zr")], [bbrB])
        self.tt(V, tb1[:], bim[:], zib, ALU.mult, [bimB, eb("zi")], [tb1B])
        self.tt(V, bbr[:], bbr[:], tb1[:], ALU.subtract, [bbrB, tb1B], [bbrB])
        self.tt(V, bbi[:], bim[:], zrb, ALU.mult, [bimB, eb("zr")], [bbiB])
        self.tt(V, tb1[:], bre[:], zib, ALU.mult, [breB, eb("zi"), bbrB], [tb1B])
        self.tt(V, bbi[:], bbi[:], tb1[:], ALU.add, [bbiB, tb1B], [bbiB])
        for bb, bbB, key in ((bbr, bbrB, "Wbr"), (bbi, bbiB, "Wbi")):
            self.ms("pool", bpad[:], 0.0, [bpadB])
            for g2 in range(2):
                dst = bpad[64 * g2:64 * g2 + 64].rearrange("p d (q g c) -> p d q g c", q=4, g=2)[:, :, :, g2, :]
                srcv = bb[64 * g2:64 * g2 + 64].rearrange("p (d q) c -> p d q c", q=4)
                self.cp("pool", dst, srcv, [bbB], [bpadB])
            for half in range(2):
                ps, psB = self.next_ps(0, 4)
                for j in range(4):
                    dc = half * 4 + j
                    self.tr(ps[:, j * 128:(j + 1) * 128], bpad[:, dc, :], self.identf[:], [bpadB, self.identB], [psB])
                self.cp("act", T[key][:, half * 4:half * 4 + 4, :],
                        ps[:].rearrange("p (j t) -> p j t", j=4), [psB], [TB[key]])
        ci = 0
        for key, ct, ctB in (("s5_c_re", ctr, ctrB), ("s5_c_im", cti, ctiB)):
            ps, psB = self.next_ps(0, 4)
            for ch in range(4):
                cn, cnB = cnc[ci % 2]
                ci += 1
                src = bass.AP(tensor=I[key], offset=L * 65536 + ch * 16 * 1024, ap=[[64, 16], [1024, 16], [1, 64]])
                self.dma(cn[:].rearrange("c (g p) -> c g p", g=16), src, [], [cnB])
                for j in range(8):
                    pr_ = ch * 8 + j
                    self.tr(ps[:, pr_ * 16:(pr_ + 1) * 16], cn[:, j * 128:(j + 1) * 128], self.identf[:16, :16],
                            [cnB, self.identB], [psB])
            self.cp("dve", ct[:].rearrange("p a b -> p (a b)"), ps[:], [psB], [ctB])
        self.ms("pool", T["Cwr"][:], 0.0, [TB["Cwr"]])
        self.ms("pool", T["Cwi"][:], 0.0, [TB["Cwi"]])
        for g2 in range(2):
            self.cp("pool", T["Cwr"][64 * g2:64 * g2 + 64, :, 16 * g2:16 * g2 + 16], ctr[64 * g2:64 * g2 + 64],
                    [ctrB], [TB["Cwr"]])
            self.ts("pool", T["Cwi"][64 * g2:64 * g2 + 64, :, 16 * g2:16 * g2 + 16], cti[64 * g2:64 * g2 + 64],
                    -1.0, None, ALU.mult, None, [ctiB], [TB["Cwi"]])
        psr, psrB = self.next_ps(0, 4)
        psi, psiB = self.next_ps(0, 4)
        for ch in range(4):
            st0, st0B = st0c[ch % 2]
            self.dma(st0[:], I["st5"].ap()[L, :, ch * 2048:(ch + 1) * 2048], [], [st0B])
            st0v = st0[:].rearrange("b (q n r) -> b q n r", q=8, r=2)
            for j in range(8):
                pr_ = ch * 8 + j
                self.tr(psr[:, pr_ * ND:(pr_ + 1) * ND], st0v[:, j, :, 0], self.identf[:ND, :ND],
                        [st0B, self.identB], [psrB])
                self.tr(psi[:, pr_ * ND:(pr_ + 1) * ND], st0v[:, j, :, 1], self.identf[:ND, :ND],
                        [st0B, self.identB], [psiB])
        self.cp("dve", T["S0r"][:].rearrange("p a b -> p (a b)"), psr[:], [psrB], [TB["S0r"]])
        self.cp("dve", T["S0i"][:].rearrange("p a b -> p (a b)"), psi[:], [psiB], [TB["S0i"]])
        P.barrier()

    def s5_layer(self, layer):
        I, P, O = self.I, self.P, self.O
        P.barrier()
        self.rmsnorm(layer)
        self.dbg("u%d" % layer, self.u[:, :, 0:64], [b for r in self.uB for b in r])
        self.s5_prep(layer)
        T, TB = self.T, self.TB
        for k_ in ("Pr", "Pi", "Wbr", "Cwr", "Cwi", "S0r"):
            self.dbg("%s%d" % (k_, layer), T[k_][:], [TB[k_]])
        o = self.s5_scan_off
        LCH = 16
        NCH = TP // LCH
        WX = TP + LCH
        XA = [self.ov_view("XAr", o, [128, WX], F32), self.ov_view("XAi", o + 8256, [128, WX], F32)]
        XBf = [self.ov_view("XBr", o + 16512, [128, WX], F32), self.ov_view("XBi", o + 24768, [128, WX], F32)]
        S16 = [self.ov_view("S16r", o + 33024, [128, NT], BF16), self.ov_view("S16i", o + 33024 + 4160, [128, NT], BF16)]
        bud = [self.ov_view("budr", o + 33024 + 8320, [128, ND], F32), self.ov_view("budi", o + 33024 + 8320 + 64, [128, ND], F32)]
        HA = self.ov_view("HA", o + 41472 + 8192 + 1024, [128, 2, NCH], F32)
        HBt = self.ov_view("HB", o + 41472 + 8192 + 2048, [128, 2, NCH], F32)
        HAB, HBB = P.buf(), P.buf()
        XAB, XBB, S16B, budB = P.bufs(2), P.bufs(2), P.bufs(2), P.bufs(2)
        u, uB = self.u, self.uB
        ybank = [4, 5, 6, 7, 3]
        for dc in range(NDC):
            for q4 in range(4):
                pair = dc * 4 + q4
                rs = slice(32 * q4, 32 * q4 + 32)
                for tt, (t0, n) in enumerate(TT):
                    for ri, key in ((0, "Wbr"), (1, "Wbi")):
                        ps, psB = self.next_ps(0, 3)
                        self.mm(ps[:, :n], T[key][rs, dc, :], u[rs, dc, t0:t0 + n], True, True,
                                [TB[key], uB[dc][tt]], [psB], tp=(32 * q4, 0))
                        if tt < 4:
                            self.cp("act", XA[ri][:, t0:t0 + n], ps[:, :n], [psB], [XAB[ri]])
                        else:
                            self.cp("act", bud[ri][:], ps[:, :n], [psB], [budB[ri]])
                pr0, pi0, npi0 = T["Pr"][:, 0, pair:pair + 1], T["Pi"][:, 0, pair:pair + 1], T["NPi"][:, 0, pair:pair + 1]
                s0r, s0i = T["S0r"][:, pair, :], T["S0i"][:, pair, :]
                s1r, s1i = T["S1r"][:, pair, :], T["S1i"][:, pair, :]
                self.stt(s1r, s0r, pr0, bud[0][:], ALU.mult, ALU.add, [TB["S0r"], TB["Pr"], budB[0]], [TB["S1r"]])
                self.stt(s1r, s0i, npi0, s1r, ALU.mult, ALU.add, [TB["S0i"], TB["NPi"], TB["S1r"]], [TB["S1r"]])
                self.stt(s1i, s0i, pr0, bud[1][:], ALU.mult, ALU.add, [TB["S0i"], TB["Pr"], budB[1]], [TB["S1i"]])
                self.stt(s1i, s0r, pi0, s1i, ALU.mult, ALU.add, [TB["S0r"], TB["Pi"], TB["S1i"]], [TB["S1i"]])
                self.cp("pool", S16[0][:, TP:NT], s1r, [TB["S1r"]], [S16B[0]])
                self.cp("pool", S16[1][:, TP:NT], s1i, [TB["S1i"]], [S16B[1]])
                cur, curB, nxt, nxtB = XA, XAB, XBf, XBB
                v3 = lambda ap: ap[:, 0:WX].rearrange("p (k j) -> p k j", j=LCH)
                for ri, tab in ((0, "Pr"), (1, "Pi")):
                    self.ms("dve", cur[ri][:, TP:WX], 0.0, [curB[ri]])
                    self.cp("dve", cur[ri][:, TP:TP + 1], T[tab][:, 0, pair:pair + 1], [TB[tab]], [curB[ri]])
                for s in range(4):
                    d = 1 << s
                    pr = T["Pr"][:, s, pair:pair + 1]
                    pi = T["Pi"][:, s, pair:pair + 1]
                    npi = T["NPi"][:, s, pair:pair + 1]
                    RB = [curB[0], curB[1], TB["Pr"], TB["Pi"], TB["NPi"]]
                    c3 = [v3(cur[0]), v3(cur[1])]
                    n3 = [v3(nxt[0]), v3(nxt[1])]
                    self.cp("act", n3[0][:, :, 0:d], c3[0][:, :, 0:d], [curB[0]], [nxtB[0]])
                    self.cp("act", n3[1][:, :, 0:d], c3[1][:, :, 0:d], [curB[1]], [nxtB[1]])
                    self.stt(n3[0][:, :, d:LCH], c3[0][:, :, 0:LCH - d], pr, c3[0][:, :, d:LCH], ALU.mult, ALU.add, RB, [nxtB[0]])
                    self.stt(n3[0][:, :, d:LCH], c3[1][:, :, 0:LCH - d], npi, n3[0][:, :, d:LCH], ALU.mult, ALU.add, RB + [nxtB[0]], [nxtB[0]])
                    self.stt(n3[1][:, :, d:LCH], c3[1][:, :, 0:LCH - d], pr, c3[1][:, :, d:LCH], ALU.mult, ALU.add, RB, [nxtB[1]])
                    self.stt(n3[1][:, :, d:LCH], c3[0][:, :, 0:LCH - d], pi, n3[1][:, :, d:LCH], ALU.mult, ALU.add, RB + [nxtB[1]], [nxtB[1]])
                    cur, curB, nxt, nxtB = nxt, nxtB, cur, curB
                c3 = [v3(cur[0]), v3(cur[1])]
                for ri in range(2):
                    self.cp("act", HA[:, ri, :], c3[ri][:, 0:NCH, LCH - 1], [curB[ri]], [HAB])
                hc, hcB, hn, hnB = HA, HAB, HBt, HBB
                for s in range(7):
                    d = 1 << s
                    pr = T["Pr"][:, 4 + s, pair:pair + 1]
                    pi = T["Pi"][:, 4 + s, pair:pair + 1]
                    npi = T["NPi"][:, 4 + s, pair:pair + 1]
                    RB = [hcB, TB["Pr"], TB["Pi"], TB["NPi"]]
                    self.cp("act", hn[:, :, 0:d], hc[:, :, 0:d], [hcB], [hnB])
                    self.stt(hn[:, 0, d:NCH], hc[:, 0, 0:NCH - d], pr, hc[:, 0, d:NCH], ALU.mult, ALU.add, RB, [hnB])
                    self.stt(hn[:, 0, d:NCH], hc[:, 1, 0:NCH - d], npi, hn[:, 0, d:NCH], ALU.mult, ALU.add, RB + [hnB], [hnB])
                    self.stt(hn[:, 1, d:NCH], hc[:, 1, 0:NCH - d], pr, hc[:, 1, d:NCH], ALU.mult, ALU.add, RB, [hnB])
                    self.stt(hn[:, 1, d:NCH], hc[:, 0, 0:NCH - d], pi, hn[:, 1, d:NCH], ALU.mult, ALU.add, RB + [hnB], [hnB])
                    hc, hcB, hn, hnB = hn, hnB, hc, hcB
                shp = [128, NCH - 1, LCH]
                Trb = c3[0][:, NCH:NCH + 1, :].to_broadcast(shp)
                Tib = c3[1][:, NCH:NCH + 1, :].to_broadcast(shp)
                Hrb = hc[:, 0, 0:NCH - 1].unsqueeze(2).to_broadcast(shp)
                Hib = hc[:, 1, 0:NCH - 1].unsqueeze(2).to_broadcast(shp)
                g0 = v3(nxt[0])[:, 1:NCH, :]
                g1 = v3(nxt[1])[:, 1:NCH, :]
                xr = c3[0][:, 1:NCH, :]
                xi = c3[1][:, 1:NCH, :]
                CB = [curB[0], curB[1], hcB]
                self.tt("dve", g0, Trb, Hrb, ALU.mult, CB, [nxtB[0]])
                self.tt("dve", g1, Tib, Hib, ALU.mult, CB, [nxtB[1]])
                self.tt("dve", xr, xr, g0, ALU.add, [curB[0], nxtB[0]], [curB[0]])
                self.tt("dve", xr, xr, g1, ALU.subtract, [curB[0], nxtB[1]], [curB[0]])
                self.tt("dve", g0, Trb, Hib, ALU.mult, CB, [nxtB[0]])
                self.tt("dve", g1, Tib, Hrb, ALU.mult, CB, [nxtB[1]])
                self.tt("dve", xi, xi, g0, ALU.add, [curB[1], nxtB[0]], [curB[1]])
                self.tt("dve", xi, xi, g1, ALU.add, [curB[1], nxtB[1]], [curB[1]])
                self.cp("pool", T["Finr"][:, pair:pair + 1], cur[0][:, TP - 1:TP], [curB[0]], [TB["Finr"]])
                self.cp("pool", T["Fini"][:, pair:pair + 1], cur[1][:, TP - 1:TP], [curB[1]], [TB["Fini"]])
                self.cp("act", S16[0][:, 0:TP], cur[0][:, 0:TP], [curB[0]], [S16B[0]])
                self.cp("act", S16[1][:, 0:TP], cur[1][:, 0:TP], [curB[1]], [S16B[1]])
                for tt, (t0, n) in enumerate(TT):
                    yb = ybank[tt]
                    self.mm(self.ps[yb][rs, :n], T["Cwr"][:, pair, :], S16[0][:, t0:t0 + n], True, False,
                            [TB["Cwr"], S16B[0]], [self.psB[yb]], inc=False, tp=(0, 32 * q4))
                    self.mm(self.ps[yb][rs, :n], T["Cwi"][:, pair, :], S16[1][:, t0:t0 + n], False, True,
                            [TB["Cwi"], S16B[1]], [self.psB[yb]], tp=(0, 32 * q4))
            for tt, (t0, n) in enumerate(TT):
                yb = ybank[tt]
                y, yB = self.next_tmp()
                w, wB = self.next_tmp()
                self.stt(y[:, :n], u[:, dc, t0:t0 + n], self.G[:, dc, 8 + layer:9 + layer], self.ps[yb][:, :n],
                         ALU.mult, ALU.add, [uB[dc][tt], self.GB, self.psB[yb]], [yB])
                self.act(u[:, dc, t0:t0 + n], y[:, :n], AF.Gelu_apprx_tanh, [yB], [uB[dc][tt]])
        P.barrier()
        rows, rowsB = self.ov_view("s5rows", o + 41472 + 8192, [32, 128, 2], F32), P.buf()
        for ri, key in ((0, "Finr"), (1, "Fini")):
            ps, psB = self.next_ps(0, 3)
            self.tr(ps[:32, :128], T[key][:], self.identf[:], [TB[key], self.identB], [psB])
            self.cp("dve", rows[:, :, ri], ps[:32, :128], [psB, XAB[0], XAB[1]], [rowsB])
        self.dma(O["s5p"].ap()[layer], rows[:].rearrange("p a b -> p (a b)"), [rowsB], [])
        drow, drowB = self.ov_view("s5drows", o, [ND, 32, 128, 2], F32), P.buf()
        for ri, key in ((0, "S1r"), (1, "S1i")):
            for half in range(8):
                ps, psB = self.next_ps(0, 3)
                for j in range(4):
                    pr_ = half * 4 + j
                    self.tr(ps[:ND, j * 128:(j + 1) * 128], T[key][:, pr_, :], self.identf[:], [TB[key], self.identB], [psB])
                self.cp("dve", drow[:, half * 4:half * 4 + 4, :, ri], ps[:ND, :].rearrange("p (j t) -> p j t", j=4),
                        [psB, XBB[0], XBB[1]], [drowB])
        self.dma(O["s5s"].ap()[layer], drow[:].rearrange("p a b c -> p (a b c)"), [drowB], [])
        wg = [self.ov_view("wg%d" % i, o + 41472 + i * 4096, [128, NDC, 256], BF16) for i in range(2)]
        wgB = P.bufs(2)
        wsrc = I["s5_w_glu"]
        for fc in range(NDC):
            k = fc % 2
            for h in range(2):
                src = bass.AP(tensor=wsrc, offset=layer * D * 2 * D + h * D + fc * 128,
                              ap=[[2 * D, 128], [128 * 2 * D, NDC], [1, 128]])
                self.dma(wg[k][:, :, h * 128:(h + 1) * 128], src, [], [wgB[k]], q="pool")
            for tt, (t0, n) in enumerate(TT):
                ps1, ps1B = self.next_ps(0, 3)
                ps2, ps2B = self.next_ps(0, 3)
                for dc in range(NDC):
                    self.mm(ps1[:, :n], wg[k][:, dc, 0:128], u[:, dc, t0:t0 + n], dc == 0, dc == NDC - 1,
                            [wgB[k], uB[dc][tt]], [ps1B], inc=(dc == NDC - 1))
                for dc in range(NDC):
                    self.mm(ps2[:, :n], wg[k][:, dc, 128:256], u[:, dc, t0:t0 + n], dc == 0, dc == NDC - 1,
                            [wgB[k], uB[dc][tt]], [ps2B], inc=(dc == NDC - 1))
                sg, sgB = self.next_tmp()
                self.act(sg[:, :n], ps2[:, :n], AF.Sigmoid, [ps2B], [sgB])
                self.tt("dve", sg[:, :n], sg[:, :n], ps1[:, :n], ALU.mult, [sgB, ps1B], [sgB])
                self.tt("pool", self.x[:, fc, t0:t0 + n], self.x[:, fc, t0:t0 + n], sg[:, :n], ALU.add,
                        [sgB, self.xB[fc][tt]], [self.xB[fc][tt]])


    def rope_tables(self):
        P = self.P
        pos, posB = self.ov_view("rp_pos", 0, [128, 18], F32), P.buf()
        jf, jfB = self.ov_view("rp_j", 128, [128, 32], F32), P.buf()
        ang, angB = self.ov_view("rp_ang", 256, [128, 18, 32], F32), P.buf()
        q1, q1B = self.ov_view("rp_q1", 256 + 2304, [128, 18, 32], F32), P.buf()
        q2, q2B = self.ov_view("rp_q2", 256 + 2 * 2304, [128, 18, 32], F32), P.buf()
        qi, qiB = self.ov_view("rp_qi", 256 + 3 * 2304, [128, 18, 32], I32), P.buf()
        P.op("pool", lambda e: e.iota(pos[:, 0:16], pattern=[[128, 16]], base=0, channel_multiplier=1,
                                      allow_small_or_imprecise_dtypes=True), [], [posB])
        self.ms("pool", pos[:, 16:17], float(TP), [posB])
        P.op("pool", lambda e: e.iota(pos[:, 17:18], pattern=[[0, 1]], base=31, channel_multiplier=32,
                                      allow_small_or_imprecise_dtypes=True), [], [posB])
        P.op("pool", lambda e: e.iota(jf[:], pattern=[[1, 32]], base=0, channel_multiplier=0,
                                      allow_small_or_imprecise_dtypes=True), [], [jfB])
        self.act(jf[:], jf[:], AF.Exp, [jfB], [jfB], scale=-math.log(10000.0) / 32.0)
        self.tt("dve", ang[:], pos[:].unsqueeze(2).to_broadcast([128, 18, 32]),
                jf[:].unsqueeze(1).to_broadcast([128, 18, 32]), ALU.mult, [posB, jfB], [angB])
        for shift, dst in ((0.0, self.ropeS), (0.25, self.ropeC)):
            self.ts("dve", q1[:], ang[:], 1.0 / TWO_PI, shift, ALU.mult, ALU.add, [angB], [q1B])
            self.cp("dve", qi[:], q1[:], [q1B], [qiB])
            self.cp("dve", q2[:], qi[:], [qiB], [q2B])
            self.tt("dve", q1[:], q1[:], q2[:], ALU.subtract, [q1B, q2B], [q1B])
            self.stt(q2[:], q1[:], 0.5, q1[:], ALU.is_gt, ALU.subtract, [q1B], [q2B])
            self.stt(q1[:], q2[:], 0.5, q2[:], ALU.is_gt, ALU.subtract, [q2B], [q1B])
            self.act(dst[:], q1[:], AF.Sin, [q1B], [self.ropeB], scale=TWO_PI)
        src = bass.AP(tensor=self.I["k_norm"], offset=0, ap=[[0, 128], [1, 192]])
        self.dma(self.kn[:, 0:3, :].rearrange("p a b -> p (a b)"), src, [], [self.knB])
        src = bass.AP(tensor=self.I["q_norm"], offset=0, ap=[[0, 128], [1, 128]])
        self.dma(self.kn[:, 3:5, :].rearrange("p a b -> p (a b)"), src, [], [self.knB])
        mx, mxB = self.ov_view("rp_mx", 256 + 4 * 2304, [128, 5], F32), P.buf()
        P.op("dve", lambda e: e.tensor_reduce(out=mx[:], in_=self.kn[:], axis=AX.X, op=ALU.max,
                                              apply_absolute_value=True), [self.knB], [mxB])
        for jb in range(2):
            self.ts("dve", self.Mneg[:, jb, :], mx[:, 0:3], mx[:, 3 + jb:4 + jb], -8.0, ALU.mult, ALU.mult,
                    [mxB], [self.MnegB])

    def head_norm_rope(self, k, n, gain, tile, tmps):
        (kB,) = tmps[0]
        (sq, sqB), (t1, t1B), (t2, t2B) = tmps[1], tmps[2], tmps[3]
        (ss, ssB) = tmps[4]
        H = k.shape[1]
        v3 = lambda a: a[:n, :H * 64].rearrange("p (h d) -> p h d", h=H)
        h3 = lambda a: a[:n, :H * 32].rearrange("p (h d) -> p h d", h=H)
        self.act(v3(sq), k, AF.Square, [kB], [sqB])
        self.P.op("dve", lambda e: e.tensor_reduce(out=ss[:n, :H], in_=v3(sq), axis=AX.X, op=ALU.add), [sqB], [ssB])
        self.act(ss[:n, :H], ss[:n, :H], AF.Sqrt, [ssB], [ssB], scale=1.0 / 64.0, bias=RMS_EPS)
        self.P.op("dve", lambda e: e.reciprocal(out=ss[:n, :H], in_=ss[:n, :H]), [ssB], [ssB])
        self.tt("dve", k, k, ss[:n, :H].unsqueeze(2).to_broadcast([n, H, 64]), ALU.mult, [kB, ssB], [kB])
        self.tt("dve", k, k, gain.unsqueeze(1).to_broadcast([n, H, 64]), ALU.mult, [kB, self.knB], [kB])
        c = self.ropeC[:n, tile, :].unsqueeze(1).to_broadcast([n, H, 32])
        s_ = self.ropeS[:n, tile, :].unsqueeze(1).to_broadcast([n, H, 32])
        x1, x2 = k[:, :, 0:32], k[:, :, 32:64]
        a1, a2 = h3(t1), h3(t2)
        b1 = t1[:n, H * 32:H * 64].rearrange("p (h d) -> p h d", h=H)
        b2 = t2[:n, H * 32:H * 64].rearrange("p (h d) -> p h d", h=H)
        self.tt("dve", a1, x1, c, ALU.mult, [kB, self.ropeB], [t1B])
        self.tt("dve", a2, x2, s_, ALU.mult, [kB, self.ropeB], [t2B])
        self.tt("dve", b1, x2, c, ALU.mult, [kB, self.ropeB], [t1B])
        self.tt("dve", b2, x1, s_, ALU.mult, [kB, self.ropeB], [t2B])
        self.tt("dve", x1, a1, a2, ALU.subtract, [t1B, t2B], [kB])
        self.tt("dve", x2, b1, b2, ALU.add, [t1B, t2B], [kB])

    PB = 49280

    def nsa_persist(self):
        P, PB = self.P, self.PB
        N = {}
        N["kTs"] = self.ov_view("kTs", PB, [128, 2, NT], BF16)
        N["kTw"] = self.ov_view("kTw", PB + 8256, [128, 2, NT], BF16)
        N["Vs"] = self.ov_view("Vs", PB + 16512, [128, 16, 4, 65], BF16)
        N["Vw"] = self.ov_view("Vw", PB + 24832, [128, 16, 4, 65], BF16)
        N["kcT"] = self.ov_view("kcT", PB + 33152, [128, 2, 64], BF16)
        N["vc"] = self.ov_view("vc", PB + 33408, [64, 4, 64], BF16)
        self.N = N
        self.NB = {k: P.buf("N_" + k) for k in N}

    def gelu_to(self, out, y, yB, w, wB, n, outB, cols):
        self.act(out, y, AF.Gelu_apprx_tanh, [yB], outB)

    def kv_phase(self):
        I, O, P = self.I, self.O, self.P
        P.barrier()
        self.rope_tables()
        self.nsa_persist()
        N, NB = self.N, self.NB
        self.rmsnorm(10)
        P.barrier()
        wkv, wkvB = self.ov_view("wkv", 9728, [128, NDC, 1536], BF16), P.buf()
        src = bass.AP(tensor=I["w_kv"], offset=0, ap=[[1536, 128], [128 * 1536, NDC], [1, 1536]])
        self.dma(wkv[:], src, [], [wkvB], q="pool")
        kvst = [self.ov_view("kvst%d" % i, 34304 + i * 6144, [128, 1536], F32) for i in range(2)]
        kvstB = P.bufs(2)
        scr = [(self.ov_view("kvscr%d" % i, i * 1024, [128, 256], F32), P.buf()) for i in range(3)]
        ssb = (self.ov_view("kvss", 3 * 1024, [128, 8], F32), P.buf())
        for key in ("Vs", "Vw"):
            self.ms("pool", N[key][:, :, :, 64:65], 1.0, [NB[key]])
        tiles = [(t * 128, 128, t) for t in range(16)] + [(TP, ND, 16)]
        for ti, (t0, n, tile) in enumerate(tiles):
            st, stB = kvst[ti % 2], kvstB[ti % 2]
            tt = min(t0 // 512, 4)
            for br in range(3):
                ps, psB = self.next_ps(0, 6)
                for dc in range(NDC):
                    self.mm(ps[:n, :], self.u[:, dc, t0:t0 + n], wkv[:, dc, br * 512:(br + 1) * 512], dc == 0, dc == NDC - 1,
                            [self.uB[dc][tt], wkvB], [psB], inc=(dc == NDC - 1))
                self.cp("act", st[:n, br * 512:(br + 1) * 512], ps[:n, :], [psB], [stB])
            for br in (1, 2):
                k = st[:n, br * 512:br * 512 + 256].rearrange("p (h d) -> p h d", h=4)
                self.head_norm_rope(k, n, self.kn[:n, br, :], tile, [(stB,), scr[0], scr[1], scr[2], ssb])
            for br, kkey, vkey in ((1, "kTs", "Vs"), (2, "kTw", "Vw")):
                ps, psB = self.next_ps(6, 8)
                for a_ in range(2):
                    self.tr(ps[:, a_ * 128:a_ * 128 + n], st[:n, br * 512 + a_ * 128:br * 512 + (a_ + 1) * 128],
                            self.identf[:n, :n], [stB, self.identB], [psB])
                self.cp("act", N[kkey][:, :, t0:t0 + n], ps[:, 0:256].rearrange("p (a t) -> p a t", a=2)[:, :, :n],
                        [psB], [NB[kkey]])
                if tile < 16:
                    self.cp("pool", N[vkey][:, tile, :, 0:64],
                            st[:, br * 512 + 256:br * 512 + 512].rearrange("p (h d) -> p h d", h=4), [stB], [NB[vkey]])
            if tile < 16:
                self.dma(O["cmp_p"].ap()[t0:t0 + n, :], st[:n, 0:512], [stB], [])
                self.dma(O["slc_p"].ap()[t0:t0 + n, :], st[:n, 512:1024], [stB], [])
                if t0 >= TP - 512:
                    self.dma(O["win_p"].ap()[t0 - (TP - 512):t0 - (TP - 512) + n, :], st[:n, 1024:1536], [stB], [])
            else:
                self.dma(O["cmp_s"].ap(), st[:n, 0:512], [stB], [])
                self.dma(O["slc_s"].ap(), st[:n, 512:1024], [stB], [])
                self.dma(O["win_s"].ap()[:, 511, :], st[:n, 1024:1536], [stB], [])
        self.compress_prompt()

    def page_idx(self, byte_off):
        P, I = self.P, self.I
        ptb, ptbB = self.ov_view("ptb", byte_off, [128, 256], I32), P.buf()
        ptf, ptfB = self.ov_view("ptf", byte_off + 1024, [128, 256], F32), P.buf()
        iop, iopB = self.ov_view("iop", byte_off + 2048, [128, 1], F32), P.buf()
        src = bass.AP(tensor=I["pt"], offset=0, ap=[[0, 128], [1, 256]])
        self.dma(ptb[:], src, [], [ptbB])
        P.op("pool", lambda e: e.iota(iop[:], pattern=[[0, 1]], base=0, channel_multiplier=1,
                                      allow_small_or_imprecise_dtypes=True), [], [iopB])
        self.cp("dve", ptf[:], ptb[:], [ptbB], [ptfB])
        self.ts("dve", ptf[:], ptf[:], 128.0, iop[:, 0:1], ALU.mult, ALU.add, [ptfB, iopB], [ptfB])
        self.cp("dve", ptb[:], ptf[:], [ptfB], [ptbB])
        return ptb[:].rearrange("p (b i) -> p b i", b=ND), ptbB

    def gather_page(self, dst, table, idx_col, R, W):
        return self.P.dma(lambda e: e.indirect_dma_start(out=dst, out_offset=None, in_=table,
                                                         in_offset=bass.IndirectOffsetOnAxis(ap=idx_col, axis=0)),
                          R, W, q="pool")

    def compress_prompt(self):
        I, O, P = self.I, self.O, self.P
        N, NB = self.N, self.NB
        P.barrier()
        cst = [self.ov_view("cst%d" % i, i * 2048, [128, 512], F32) for i in range(2)]
        cstB = P.bufs(2)
        cTs = [(self.ov_view("cT0", 4096, [128, 4, 2080], BF16), P.buf()),
               (self.ov_view("cT1", 20736, [128, 4, 2080], BF16), P.buf())]
        w1rB = P.buf()
        w1all = self.u[:].rearrange("p a b -> p (a b)")[:, 0:16384].rearrange("p (s l h) -> p s l h", s=2, l=32)
        for s_ in range(2):
            for c2 in range(2):
                for l4 in range(4):
                    src = bass.AP(tensor=I["cmp_w1"], offset=s_ * 2048 * 256 + l4 * 8 * 64 * 256,
                                  ap=[[256, 64], [64 * 256, 8], [1, 256]])
                    self.dma(w1all[64 * c2:64 * c2 + 64, s_, 8 * l4:8 * l4 + 8, :], src, [], [w1rB], q="pool")
        w2, w2B = self.ov_view("w2c", 48128, [128, 2, 2, 64], BF16), P.buf()
        posst, posstB = self.ov_view("posst", 37632, [32, 2, 2, 64], F32), P.buf()
        hT, hTB = self.ov_view("hT", 38656, [128, 2, 64], BF16), P.buf()
        ctok, ctokB = self.ov_view("ctok", 39168, [64, 2, 4, 64], F32), P.buf()
        scr = [(self.ov_view("cscr%d" % i, 41216 + i * 1024, [128, 256], F32), P.buf()) for i in range(3)]
        ssb = (self.ov_view("css", 41216 + 3072, [128, 8], F32), P.buf())
        yb = (self.ov_view("cy", 41216 + 3104, [128, 64], F32), P.buf())
        wb = (self.ov_view("cw", 41216 + 3104 + 256, [128, 64], F32), P.buf())
        bcol = (self.ov_view("cb", 41216 + 3104 + 512, [128, 2], F32), P.buf())
        kcd = (self.ov_view("kcd", 41216 + 3104 + 544, [128, 2, 64], BF16), P.buf())
        vcd = (self.ov_view("vcd", 41216 + 3104 + 800, [64, 4, 64], BF16), P.buf())
        idx, idxB = self.page_idx(46016)
        for s_ in range(2):
            for c2 in range(2):
                self.dma(posst[:, s_, c2, :], I["cmp_pos"].ap()[s_], [], [posstB])
        for s_ in range(2):
            ps, psB = self.next_ps(0, 6)
            self.tr(ps[:, 0:32], posst[:, s_, :, :].rearrange("p a b -> p (a b)"), self.identf[:32, :32],
                    [posstB, self.identB], [psB])
            for a_ in range(2):
                for cT, cTB in cTs:
                    self.cp("dve", cT[:, 2 * s_ + a_, 2048:2080], ps[:, 0:32], [psB], [cTB])
        src = bass.AP(tensor=I["cmp_w2"], offset=0, ap=[[64, 128], [256 * 64, 2], [128 * 64, 2], [1, 64]])
        self.dma(w2[:], src, [], [w2B], q="pool")
        cmp_tab = I["cache_cmp"].ap()
        nseq = 1 + (ND if self.stage >= 8 else 0)
        def fill(seq):
            cT, cTB = cTs[seq % 2]
            for t in range(16):
                st, stB = cst[t % 2], cstB[t % 2]
                if seq == 0:
                    self.dma(st[:], O["cmp_p"].ap()[t * 128:(t + 1) * 128, :], [], [stB])
                else:
                    self.gather_page(st[:], cmp_tab, idx[:, seq - 1, t:t + 1], [idxB], [stB])
                ps, psB = self.next_ps(0, 6)
                for j in range(4):
                    self.tr(ps[:, j * 128:(j + 1) * 128], st[:, j * 128:(j + 1) * 128], self.identf[:], [stB, self.identB], [psB])
                self.cp("act" if t % 2 == 0 else "dve", cT[:, :, t * 128:(t + 1) * 128],
                        ps[:].rearrange("p (j t) -> p j t", j=4), [psB], [cTB])
        fill(0)
        for seq in range(nseq):
            if seq + 1 < nseq:
                fill(seq + 1)
            cT, cTB = cTs[seq % 2]
            for s_ in range(2):
                w1r = w1all[:, s_, :, :]
                for h in range(4):
                    b_, a_ = h % 2, h // 2
                    rs = slice(64 * b_, 64 * b_ + 64)
                    ps, psB = self.next_ps(0, 6)
                    for hh in range(2):
                        for l in range(32):
                            rhs = cT[rs, 2 * s_ + a_, :].rearrange("p (n l) -> p n l", l=32)[:, :, l]
                            self.mm(ps[:, hh * 65:hh * 65 + 65], w1r[rs, l, hh * 128:(hh + 1) * 128], rhs,
                                    hh == 0 and l == 0, hh == 1 and l == 31, [w1rB, cTB], [psB],
                                    inc=(hh == 1 and l == 31), tp=(64 * b_, 0))
                    self.cp("act", bcol[0][:].rearrange("p (a b) -> p a b", b=1),
                            ps[:, 0:130].rearrange("p (a b) -> p a b", a=2)[:, :, 64:65], [psB], [bcol[1]])
                    for hh in range(2):
                        self.ts("dve", yb[0][:], ps[:, hh * 65:hh * 65 + 64], bcol[0][:, hh:hh + 1], None, ALU.add, None,
                                [psB, bcol[1]], [yb[1]])
                        self.gelu_to(hT[:, hh, :], yb[0][:], yb[1], wb[0][:], wb[1], 128, [hTB], 64)
                    ps2, ps2B = self.next_ps(0, 6)
                    for hh in range(2):
                        self.mm(ps2[:64, 0:64], hT[:, hh, :], w2[:, s_, hh, :], hh == 0, hh == 1, [hTB, w2B], [ps2B], inc=(hh == 1))
                    self.cp("act", ctok[:, s_, h, :], ps2[:64, 0:64], [ps2B], [ctokB])
            self.head_norm_rope(ctok[:, 0, :, :], 64, self.kn[:64, 0, :], 17, [(ctokB,), scr[0], scr[1], scr[2], ssb])
            ps, psB = self.next_ps(0, 6)
            for a_ in range(2):
                self.tr(ps[:, a_ * 64:(a_ + 1) * 64], ctok[:, 0, 2 * a_:2 * a_ + 2, :].rearrange("p a b -> p (a b)"),
                        self.identf[:64, :64], [ctokB, self.identB], [psB])
            if seq == 0:
                self.cp("dve", N["kcT"][:].rearrange("p a b -> p (a b)"), ps[:, 0:128], [psB], [NB["kcT"]])
                self.cp("dve", N["vc"][:], ctok[:, 1, :, :], [ctokB], [NB["vc"]])
            else:
                self.cp("dve", kcd[0][:].rearrange("p a b -> p (a b)"), ps[:, 0:128], [psB], [kcd[1]])
                self.cp("dve", vcd[0][:], ctok[:, 1, :, :], [ctokB], [vcd[1]])
                self.dma(self.kc_scr.ap()[seq - 1], kcd[0][:].rearrange("p a b -> p (a b)"), [kcd[1]], [])
                self.dma(self.vc_scr.ap()[seq - 1], vcd[0][:].rearrange("p a b -> p (a b)"), [vcd[1]], [])

    def nsa_layer(self, jb):
        I, O, P = self.I, self.O, self.P
        N, NB = self.N, self.NB
        layer = 2 + jb
        P.barrier()
        self.rmsnorm(layer)
        u, uB = self.u, self.uB
        win_, winB = self.ov_view("nsa_win", 0, [128, NDC, 1072], BF16), P.buf()
        src = bass.AP(tensor=I["nsa_w_in"], offset=jb * D * 1072, ap=[[1072, 128], [128 * 1072, NDC], [1, 1072]])
        self.dma(win_[:], src, [], [winB], q="pool")
        qst = [self.ov_view("qst%d" % i, 17152 + i * 4288, [128, 1072], F32) for i in range(2)]
        qstB = P.bufs(2)
        scr = [(self.ov_view("qscr%d" % i, 25728 + i * 4096, [128, 1024], F32), P.buf()) for i in range(3)]
        ssb = (self.ov_view("qss", 38016, [128, 16], F32), P.buf())
        gat, gatB = self.ov_view("gat", 46016, [128, 17, 48], F32), P.buf()
        tiles = [(t * 128, 128, t) for t in range(16)] + [(TP, ND, 16)]
        for ti, (t0, n, tile) in enumerate(tiles):
            st, stB = qst[ti % 2], qstB[ti % 2]
            tt = min(t0 // 512, 4)
            for c0, cw in ((0, 512), (512, 512), (1024, 48)):
                ps, psB = self.next_ps(0, 6)
                for dc in range(NDC):
                    self.mm(ps[:n, :cw], u[:, dc, t0:t0 + n], win_[:, dc, c0:c0 + cw], dc == 0, dc == NDC - 1,
                            [uB[dc][tt], winB], [psB], inc=(dc == NDC - 1))
                self.cp("act", st[:n, c0:c0 + cw], ps[:n, :cw], [psB], [stB])
            q = st[:n, 0:1024].rearrange("p (h d) -> p h d", h=16)
            self.head_norm_rope(q, n, self.kn[:n, 3 + jb, :], tile, [(stB,), scr[0], scr[1], scr[2], ssb])
            self.act(gat[:n, tile, :], st[:n, 1024:1072], AF.Sigmoid, [stB], [gatB])
            qp, qpB = scr[0]
            for a_ in range(2):
                self.cp("pool", qp[:n, 512 * a_:512 * a_ + 512].rearrange("p (r b d) -> p r b d", r=4, b=2),
                        st[:n, 512 * a_:512 * a_ + 512].rearrange("p (b r d) -> p r b d", b=2, r=4), [stB], [qpB])
            for a_ in range(2):
                ps, psB = self.next_ps(6, 8)
                for r in range(4):
                    c0_ = (4 * a_ + r) * 128
                    self.tr(ps[:, r * 128:r * 128 + n], qp[:n, c0_:c0_ + 128], self.identf[:n, :n], [qpB, self.identB], [psB])
                self.cp("act" if a_ == 0 else "dve", u[:, 4 * a_:4 * a_ + 4, t0:t0 + n],
                        ps[:].rearrange("p (r t) -> p r t", r=4)[:, :, :n], [psB],
                        [uB[sl][tt] for sl in range(4 * a_, 4 * a_ + 4)])
        STOP = 99
        STOPA = 99
        if STOP <= 1:
            return
        P.barrier()
        qT = u
        qTB = [b for row in uB for b in row]
        off = [0]

        def alloc(name, shape, dt):
            esz = 4 if dt in (F32, I32) else 2
            nbytes = esz
            for d_ in shape[1:]:
                nbytes *= d_
            v = self.ov_view(name, off[0], shape, dt)
            off[0] += (nbytes + 31) // 32 * 32
            assert off[0] <= 46016, off[0]
            return v, P.buf(name)
        wo = [alloc("wo%d" % i, [128, NDC, 128], BF16) for i in range(2)]
        pc, pcB = alloc("pc", [128, 16, 64], F32)
        pcT, pcTB = alloc("pcT", [64, 16, 128], BF16)
        i1, i1B = alloc("i1", [128, 16, 32], F32)
        imp, impB = alloc("imp", [128, 4, 32], F32)
        sc2, sc2B = alloc("sc2", [128, 4, 32], F32)
        sel, selB = alloc("sel", [128, 4, 32], F32)
        m8, m8B = alloc("m8", [128, 2, 8], F32)
        ssum, ssumB = alloc("ssum", [128, 16], F32)
        cm, cmB = alloc("cm", [128, 64], F32)
        FM, FMB = alloc("FM", [128, 32], F32)
        selT, selTB = alloc("selT", [32, 4, 512], BF16)
        Ex, ExB = alloc("Ex", [32, 16, 128], BF16)
        otok, otokB = alloc("otok", [128, 4, 1024], F32)
        mk = [alloc("mk%d" % i, [128, 512], BF16) for i in range(2)]
        W, WB = alloc("Wband", [128, 1408], BF16)
        wgt, wgtB = alloc("wgt", [128, 4, 4], F32)
        oT, oTB = self.sq, self.sqB
        PT = [(self.tmpf[i][:].bitcast(BF16)[:, 0:512], self.tmpfB[i]) for i in range(3)]
        BIG = 1.0e30
        self.ms("pool", W[:], 1.0, [WB])
        P.op("pool", lambda e: e.affine_select(out=W[:], in_=W[:], pattern=[[1, 1408]], compare_op=ALU.is_ge, fill=0.0,
                                               base=-384, channel_multiplier=-1), [WB], [WB])
        P.op("pool", lambda e: e.affine_select(out=W[:], in_=W[:], pattern=[[-1, 1408]], compare_op=ALU.is_ge, fill=0.0,
                                               base=384 + 511, channel_multiplier=1), [WB], [WB])
        self.ms("pool", Ex[:], 1.0, [ExB])
        P.op("pool", lambda e: e.affine_select(out=Ex[:], in_=Ex[:], pattern=[[128, 16], [1, 128]], compare_op=ALU.is_ge,
                                               fill=0.0, base=0, channel_multiplier=-64), [ExB], [ExB])
        P.op("pool", lambda e: e.affine_select(out=Ex[:], in_=Ex[:], pattern=[[-128, 16], [-1, 128]], compare_op=ALU.is_ge,
                                               fill=0.0, base=63, channel_multiplier=64), [ExB], [ExB])
        mneg = lambda br: self.Mneg[:, jb, br:br + 1]
        wo_i = [0]
        pt_i = [0]
        mk_i = [0]
        for qt in range(4):
            Q0 = 512 * qt
            for sb in range(4):
                gs = 4 * qt + sb
                t0 = Q0 + 128 * sb
                psa = [self.next_ps(0, 3), self.next_ps(0, 3)]
                for h in range(16):
                    a_, b_, r = h // 8, (h % 8) // 4, h % 4
                    rs = slice(64 * b_, 64 * b_ + 64)
                    ps, psB = psa[b_]
                    hl = 4 * a_ + r
                    self.mm(ps[:, hl * 64:hl * 64 + 64], qT[rs, 4 * a_ + r, t0:t0 + 128], N["kcT"][rs, a_, :],
                            True, True, qTB + [NB["kcT"]], [psB], tp=(64 * b_, 0))
                for half in range(2):
                    self.act(pc[:].rearrange("p (a b r) n -> p a b r n", a=2, b=2)[:, :, half, :, :],
                             psa[half][0][:].rearrange("p (a r n) -> p a r n", a=2, r=4), AF.Exp,
                             [psa[half][1], self.MnegB], [pcB], scale=0.125, bias=mneg(0))
                if STOPA <= 1:
                    return
                self.ms("pool", cm[:], 1.0, [cmB])
                P.op("pool", lambda e, t0=t0: e.affine_select(out=cm[:], in_=cm[:], pattern=[[-32, 64]], compare_op=ALU.is_ge,
                                                               fill=0.0, base=t0 - 31, channel_multiplier=1), [cmB], [cmB])
                self.tt("dve", pc[:], pc[:], cm[:].unsqueeze(1).to_broadcast([128, 16, 64]), ALU.mult, [pcB, cmB], [pcB])
                P.op("dve", lambda e: e.tensor_reduce(out=ssum[:], in_=pc[:], axis=AX.X, op=ALU.add), [pcB], [ssumB])
                self.ts("dve", ssum[:], ssum[:], 1e-30, None, ALU.max, None, [ssumB], [ssumB])
                P.op("dve", lambda e: e.reciprocal(out=ssum[:], in_=ssum[:]), [ssumB], [ssumB])
                self.tt("dve", pc[:], pc[:], ssum[:].unsqueeze(2).to_broadcast([128, 16, 64]), ALU.mult, [pcB, ssumB], [pcB])
                if STOPA <= 2:
                    return
                P.op("dve", lambda e: e.tensor_reduce(out=i1[:], in_=pc[:].rearrange("p h (j t) -> p h j t", t=2),
                                                      axis=AX.X, op=ALU.add), [pcB], [i1B])
                P.op("dve", lambda e: e.tensor_reduce(out=imp[:], in_=i1[:].rearrange("p (g r) j -> p g j r", r=4),
                                                      axis=AX.X, op=ALU.add), [i1B], [impB])
                if STOPA <= 3:
                    return
                self.ms("pool", FM[:], 0.0, [FMB])
                for p0, cur in ((0, 2 * gs), (64, 2 * gs + 1)):
                    if cur + 1 < 32:
                        self.ms("pool", FM[p0:p0 + 64, cur + 1:32], -BIG, [FMB])
                    self.ms("pool", FM[p0:p0 + 64, 0:1], 1000.0, [FMB])
                    self.ms("pool", FM[p0:p0 + 64, cur:cur + 1], 1000.0, [FMB])
                    if cur >= 1:
                        self.ms("pool", FM[p0:p0 + 64, cur - 1:cur], 1000.0, [FMB])
                self.tt("dve", imp[:], imp[:], FM[:].unsqueeze(1).to_broadcast([128, 4, 32]), ALU.add, [impB, FMB], [impB])
                if STOPA <= 4:
                    return
                for g in range(4):
                    P.op("dve", lambda e, g=g: e.max(out=m8[:, 0, :], in_=imp[:, g, :]), [impB], [m8B])
                    P.op("dve", lambda e, g=g: e.match_replace(out=sc2[:, g, :], in_to_replace=m8[:, 0, :],
                                                               in_values=imp[:, g, :], imm_value=-BIG), [impB, m8B], [sc2B])
                    P.op("dve", lambda e, g=g: e.max(out=m8[:, 1, :], in_=sc2[:, g, :]), [sc2B], [m8B])
                    self.ts("dve", sel[:, g, :], imp[:, g, :], m8[:, 1, 7:8], None, ALU.is_ge, None, [impB, m8B], [selB])
                if STOPA <= 5:
                    return
                ps, psB = self.next_ps(0, 3)
                for g in range(4):
                    self.tr(ps[:32, g * 128:(g + 1) * 128], sel[:, g, :], self.identf[:], [selB, self.identB], [psB])
                self.cp("act", selT[:, :, sb * 128:(sb + 1) * 128], ps[:32, :].rearrange("p (g t) -> p g t", g=4),
                        [psB], [selTB])
                if STOPA <= 6:
                    return
                for hq in range(4):
                    ps, psB = self.next_ps(0, 3)
                    for r4 in range(4):
                        self.tr(ps[:64, r4 * 128:(r4 + 1) * 128], pc[:, 4 * hq + r4, :], self.identf[:], [pcB, self.identB], [psB])
                    self.cp("act", pcT[:, 4 * hq:4 * hq + 4, :], ps[:64, :].rearrange("p (r t) -> p r t", r=4), [psB], [pcTB])
                for half in range(2):
                    ps, psB = self.next_ps(0, 3)
                    for h8 in range(8):
                        h = 8 * half + h8
                        self.mm(ps[:, h8 * 64:(h8 + 1) * 64], pcT[:, h, :], N["vc"][:, h // 4, :], True, True,
                                [pcTB, NB["vc"]], [psB], inc=(h8 == 7))
                    gc = gat[:, gs, :].rearrange("p (h b) -> p h b", b=3)[:, 8 * half:8 * half + 8, 0:1]
                    self.tt("dve", otok[:, sb, 512 * half:512 * half + 512].rearrange("p (h d) -> p h d", h=8),
                            ps[:].rearrange("p (h d) -> p h d", h=8), gc.to_broadcast([128, 8, 64]), ALU.mult,
                            [psB, gatB], [otokB])
            if STOP <= 2:
                return
            for g in range(4):
                a_, b_ = g // 2, g % 2
                rs = slice(64 * b_, 64 * b_ + 64)
                for branch in ("slc", "win"):
                    kT, V = (N["kTs"], N["Vs"]) if branch == "slc" else (N["kTw"], N["Vw"])
                    kTB_, VB_ = (NB["kTs"], NB["Vs"]) if branch == "slc" else (NB["kTw"], NB["Vw"])
                    bri = 1 if branch == "slc" else 2
                    kts = list(range(0, 4 * qt + 4)) if branch == "slc" else list(range(max(0, 4 * qt - 4), 4 * qt + 4))
                    started = [False] * 4
                    items = [(kt, r) for kt in kts for r in range(4)]
                    SB = [0, 1, 7]
                    AHEAD = 2
                    state = {}

                    def stage_scores(ii):
                        kt, r = items[ii]
                        K0 = 128 * kt
                        delta = Q0 - K0
                        if r == 0:
                            wsl = W[:, delta + 384:delta + 384 + 512] if -384 <= delta <= 512 else None
                            if branch == "slc":
                                m_, mB_ = mk[mk_i[0] % 2]
                                mk_i[0] += 1
                                psm, psmB = self.ps[2], self.psB[2]
                                self.mm(psm[:, :], Ex[:, kt, :], selT[:, g, :], True, True, [ExB, selTB], [psmB])
                                if kt >= 4 * qt:
                                    self.tt("dve", m_[:], psm[:, :], wsl, ALU.mult, [psmB, WB], [mB_])
                                else:
                                    self.cp("act", m_[:], psm[:, :], [psmB], [mB_])
                                state[("mask", kt)] = (m_[:], mB_)
                            else:
                                state[("mask", kt)] = (wsl, WB)
                        bank = SB[ii % 3]
                        pss, pssB = self.ps[bank], self.psB[bank]
                        self.mm(pss[:, :], kT[rs, a_, K0:K0 + 128], qT[rs, 4 * a_ + r, Q0:Q0 + 512], True, True,
                                [kTB_] + qTB, [pssB], tp=(64 * b_, 0))

                    def stage_rest(ii):
                        kt, r = items[ii]
                        bank = SB[ii % 3]
                        pss, pssB = self.ps[bank], self.psB[bank]
                        msk, mskB = state[("mask", kt)]
                        pt, ptB = PT[ii % 3]
                        if branch == "slc":
                            subs = [sb for sb in range(4) if kt <= 4 * qt + sb]
                        else:
                            subs = [sb for sb in range(4) if 4 * qt + sb - 4 <= kt <= 4 * qt + sb]
                        self.act(pt, pss[:, :], AF.Exp, [pssB, self.MnegB], [ptB], scale=0.125, bias=mneg(bri))
                        self.tt("dve", pt, pt, msk, ALU.mult, [ptB, mskB], [ptB])
                        ob, obB = self.ps[3 + r], self.psB[3 + r]
                        for sb in subs:
                            first = not started[r]
                            started[r] = True
                            self.mm(ob[:, sb * 65:sb * 65 + 65], pt[:, sb * 128:(sb + 1) * 128], V[:, kt, g, :],
                                    first, False, [ptB, VB_], [obB], inc=(sb == subs[-1]))
                    for ii in range(min(AHEAD, len(items))):
                        stage_scores(ii)
                    for ii in range(len(items)):
                        if ii + AHEAD < len(items):
                            stage_scores(ii + AHEAD)
                        stage_rest(ii)
                    for r in range(4):
                        h = 4 * g + r
                        ob, obB = self.ps[3 + r], self.psB[3 + r]
                        ov_ = ob[:, 0:260].rearrange("p (s c) -> p s c", s=4)
                        self.ts("dve", wgt[:, r, :].unsqueeze(2), ov_[:, :, 64:65], 1e-30, None, ALU.max, None, [obB], [wgtB])
                        P.op("dve", lambda e, r=r: e.reciprocal(out=wgt[:, r, :], in_=wgt[:, r, :]), [wgtB], [wgtB])
                        gsel = gat[:, 4 * qt:4 * qt + 4, 3 * h + bri]
                        self.tt("dve", wgt[:, r, :], wgt[:, r, :], gsel, ALU.mult, [wgtB, gatB], [wgtB])
                        for sb in range(4):
                            dst = otok[:, sb, h * 64:(h + 1) * 64]
                            self.stt(dst, ov_[:, sb, 0:64], wgt[:, r, sb:sb + 1], dst, ALU.mult, ALU.add,
                                     [obB, wgtB, otokB], [otokB])
            if STOP <= 3:
                return
            for sb in range(4):
                for half in range(2):
                    ps, psB = self.next_ps(0, 3)
                    for j in range(4):
                        dc = 4 * half + j
                        self.tr(ps[:, j * 128:(j + 1) * 128], otok[:, sb, dc * 128:(dc + 1) * 128], self.identf[:],
                                [otokB, self.identB], [psB])
                    self.cp("act" if half == 0 else "dve", oT[:, 4 * half:4 * half + 4, sb * 128:(sb + 1) * 128],
                            ps[:].rearrange("p (j t) -> p j t", j=4), [psB], [oTB])
            for dco in range(NDC):
                w_, wB_ = wo[wo_i[0] % 2]
                wo_i[0] += 1
                src = bass.AP(tensor=I["nsa_w_o"], offset=jb * D * D + dco * 128, ap=[[D, 128], [128 * D, NDC], [1, 128]])
                self.dma(w_[:], src, [], [wB_], q="pool")
                ps, psB = self.ps[7], self.psB[7]
                for dci in range(NDC):
                    self.mm(ps[:, :], w_[:, dci, :], oT[:, dci, :], dci == 0, dci == NDC - 1, [wB_, oTB], [psB],
                            inc=(dci == NDC - 1))
                self.tt("dve", self.x[:, dco, Q0:Q0 + 512], self.x[:, dco, Q0:Q0 + 512], ps[:, :], ALU.add,
                        [psB, self.xB[dco][qt]], [self.xB[dco][qt]])
        if self.stage >= 8:
            self.nsa_decode(jb, gat, gatB)

    def nsa_decode(self, jb, gat, gatB):
        I, O, P = self.I, self.O, self.P
        N, NB = self.N, self.NB
        P.barrier()
        qT = self.u
        qTB = [b for row in self.uB for b in row]
        off = [0]

        def alloc(name, shape, dt):
            esz = 4 if dt in (F32, I32) else 2
            nbytes = esz
            for d_ in shape[1:]:
                nbytes *= d_
            v = self.ov_view(name, off[0], shape, dt)
            off[0] += (nbytes + 31) // 32 * 32
            assert off[0] <= 46016, off[0]
            return v, P.buf(name)
        idx, idxB = self.page_idx(0)
        off[0] = 2176
        stg = [alloc("dstg%d" % i, [128, 512], F32) for i in range(3)]
        kTp = [alloc("dkTp%d" % i, [128, 2, 128], BF16) for i in range(2)]
        Vp = [alloc("dVp%d" % i, [128, 4, 65], BF16) for i in range(2)]
        Vp0 = alloc("dVp0", [128, 4, 65], BF16)
        ptd = [alloc("dptd%d" % i, [128, 2, 2, 4], BF16) for i in range(2)]
        ptn = alloc("dptn", [16, 2, 2, 4], BF16)
        Mk, MkB = alloc("dMk", [128, 4, 16], BF16)
        ohb, ohbB = alloc("dohb", [16, 128], F32)
        kcb = [alloc("dkcb%d" % i, [128, 2, 64], BF16) for i in range(2)]
        vcb = [alloc("dvcb%d" % i, [64, 4, 64], BF16) for i in range(2)]
        Sacc, SaccB = alloc("dSacc", [16, 16, 64], F32)
        pc, pcB = alloc("dpc", [16, 16, 64], F32)
        i1, i1B = alloc("di1", [16, 16, 32], F32)
        imp, impB = alloc("dimp", [16, 4, 33], F32)
        sc2, sc2B = alloc("dsc2", [16, 4, 33], F32)
        sel, selB = alloc("dsel", [16, 4, 33], F32)
        m8, m8B = alloc("dm8", [16, 2, 8], F32)
        ssum, ssumB = alloc("dssum", [16, 16], F32)
        pcT, pcTB = alloc("dpcT", [64, 16, 16], BF16)
        ocacc, ocaccB = pc, pcB
        otd, otdB = alloc("dotd", [16, 16, 64], F32)
        Osb = [alloc("dOsb%d" % i, [4, 4, 65], F32) for i in range(2)]
        Otm, OtmB = alloc("dOtm", [16, 2, 16, 65], F32)
        wgt, wgtB = alloc("dwgt", [16, 16], F32)
        tmpo, tmpoB = Sacc, SaccB
        oTd, oTdB = alloc("doTd", [128, NDC, 16], BF16)
        wod = [alloc("dwod%d" % i, [128, NDC, 128], BF16) for i in range(1)]
        Vn, VnB = alloc("dVn", [16, 2, 4, 65], F32)
        Vd = [alloc("dVd%d" % i, [16, 4, 65], BF16) for i in range(2)]
        odB = P.buf("od_scr")
        mneg = lambda br: self.Mneg[:, jb, br:br + 1]
        idf = self.identf
        self.ms("pool", Vn[:], 1.0, [VnB])
        self.dma(Vn[:, 0, :, 0:64], O["slc_s"].ap()[:, 256:512].rearrange("b (h d) -> b h d", h=4), [], [VnB])
        self.dma(Vn[:, 1, :, 0:64], O["win_s"].ap()[:, 511, 256:512].rearrange("b (h d) -> b h d", h=4), [], [VnB])
        for v_, vB_ in Vp + [Vp0]:
            self.ms("pool", v_[:, :, 64:65], 1.0, [vB_])
        self.ms("pool", Sacc[:], 0.0, [SaccB])
        Saccv = Sacc[:].rearrange("p (a b r) n -> p a b r n", a=2, b=2)
        for b in range(ND):
            kc_, kcB_ = kcb[b % 2]
            self.dma(kc_[:].rearrange("p a b -> p (a b)"), self.kc_scr.ap()[b], [], [kcB_])
            psc = [(self.ps[0], self.psB[0]), (self.ps[1], self.psB[1])]
            for h in range(16):
                a_, b_, r = h // 8, (h % 8) // 4, h % 4
                rs = slice(64 * b_, 64 * b_ + 64)
                hl = 4 * a_ + r
                self.mm(psc[b_][0][:ND, hl * 64:hl * 64 + 64], qT[rs, 4 * a_ + r, TP:NT], kc_[rs, a_, :], True, True,
                        qTB + [kcB_], [psc[b_][1]], tp=(64 * b_, 0))
            for half in range(2):
                dst = Saccv[:, :, half, :, :]
                self.stt(dst, psc[half][0][:ND, :].rearrange("p (a r n) -> p a r n", a=2, r=4), idf[:ND, b:b + 1], dst,
                         ALU.mult, ALU.add, [psc[half][1], self.identB, SaccB], [SaccB])
        self.act(pc[:], Sacc[:], AF.Exp, [SaccB, self.MnegB], [pcB], scale=0.125, bias=self.Mneg[:ND, jb, 0:1])
        P.op("dve", lambda e: e.tensor_reduce(out=ssum[:], in_=pc[:], axis=AX.X, op=ALU.add), [pcB], [ssumB])
        self.ts("dve", ssum[:], ssum[:], 1e-30, None, ALU.max, None, [ssumB], [ssumB])
        P.op("dve", lambda e: e.reciprocal(out=ssum[:], in_=ssum[:]), [ssumB], [ssumB])
        self.tt("dve", pc[:], pc[:], ssum[:].unsqueeze(2).to_broadcast([ND, 16, 64]), ALU.mult, [pcB, ssumB], [pcB])
        P.op("dve", lambda e: e.tensor_reduce(out=i1[:], in_=pc[:].rearrange("p h (j t) -> p h j t", t=2),
                                              axis=AX.X, op=ALU.add), [pcB], [i1B])
        self.ms("pool", imp[:], 0.0, [impB])
        P.op("dve", lambda e: e.tensor_reduce(out=imp[:, :, 0:32], in_=i1[:].rearrange("p (g r) j -> p g j r", r=4),
                                              axis=AX.X, op=ALU.add), [i1B, impB], [impB])
        for j in (0, 31, 32):
            self.ts("dve", imp[:, :, j:j + 1], imp[:, :, j:j + 1], 1000.0, None, ALU.add, None, [impB], [impB])
        for g in range(4):
            P.op("dve", lambda e, g=g: e.max(out=m8[:, 0, :], in_=imp[:, g, :]), [impB], [m8B])
            P.op("dve", lambda e, g=g: e.match_replace(out=sc2[:, g, :], in_to_replace=m8[:, 0, :],
                                                       in_values=imp[:, g, :], imm_value=-1.0e30), [impB, m8B], [sc2B])
            P.op("dve", lambda e, g=g: e.max(out=m8[:, 1, :], in_=sc2[:, g, :]), [sc2B], [m8B])
            self.ts("dve", sel[:, g, :], imp[:, g, :], m8[:, 1, 7:8], None, ALU.is_ge, None, [impB, m8B], [selB])
        for hq in range(4):
            ps, psB = self.ps[2], self.psB[2]
            for r4 in range(4):
                self.tr(ps[:64, r4 * ND:(r4 + 1) * ND], pc[:, 4 * hq + r4, :], idf[:ND, :ND], [pcB, self.identB], [psB])
            self.cp("act", pcT[:, 4 * hq:4 * hq + 4, :], ps[:64, 0:4 * ND].rearrange("p (r t) -> p r t", r=4), [psB], [pcTB])
        self.ms("pool", ocacc[:], 0.0, [ocaccB])
        for b in range(ND):
            vc_, vcB_ = vcb[b % 2]
            self.dma(vc_[:].rearrange("p a b -> p (a b)"), self.vc_scr.ap()[b], [], [vcB_])
            for half in range(2):
                ps, psB = self.ps[half], self.psB[half]
                for h8 in range(8):
                    h = 8 * half + h8
                    self.mm(ps[:ND, h8 * 64:(h8 + 1) * 64], pcT[:, h, :], vc_[:, h // 4, :], True, True, [pcTB, vcB_], [psB],
                            inc=(h8 == 7))
                dst = ocacc[:, 8 * half:8 * half + 8, :]
                self.stt(dst, ps[:ND, :].rearrange("p (h d) -> p h d", h=8), idf[:ND, b:b + 1], dst, ALU.mult, ALU.add,
                         [psB, self.identB, ocaccB], [ocaccB])
        gt = gat[:ND, 16, :].rearrange("p (h b) -> p h b", b=3)
        self.tt("dve", otd[:], ocacc[:], gt[:, :, 0:1].to_broadcast([ND, 16, 64]), ALU.mult, [ocaccB, gatB], [otdB])
        slc_tab = I["cache_slc"].ap()
        tiles = []
        for b in range(ND):
            for bri, branch in ((1, "slc"), (2, "win")):
                ntile = 16 if branch == "slc" else 4
                for i in range(ntile + 1):
                    tiles.append((b, bri, branch, i, ntile))
        st8 = {}

        def issue_load(k_):
            b, bri, branch, i, ntile = tiles[k_]
            if i == ntile:
                return
            st, stB = stg[k_ % 3]
            if branch == "slc":
                self.gather_page(st[:], slc_tab, idx[:, b, i:i + 1], [idxB], [stB])
            else:
                src = I["cache_win"].ap()[b:b + 1, :].rearrange("o (r c) -> (o r) c", c=512)[i * 128:(i + 1) * 128, :]
                self.dma(st[:], src, [], [stB])
                if jb == 0:
                    if i == 0:
                        self.dma(O["win_s"].ap()[b, 0:127, :], st[1:128, :], [stB], [])
                    else:
                        self.dma(O["win_s"].ap()[b, 128 * i - 1:128 * i + 127, :], st[:, :], [stB], [])

        def prefetch(k_):
            b, bri, branch, i, ntile = tiles[k_]
            if bri == 1 and i == 0:
                self.cp("dve", ohb[:], idf[:ND, b:b + 1].to_broadcast([ND, 128]), [self.identB], [ohbB])
                psm, psmB = self.ps[3], self.psB[3]
                self.mm(psm[:, 0:132], ohb[:], sel[:].rearrange("p g j -> p (g j)"), True, True, [ohbB, selB], [psmB])
                pv = psm[:, 0:132].rearrange("p (g j) -> p g j", g=4)[:, :, 0:32].rearrange("p g (i t) -> p g i t", t=2)
                self.cp("dve", Mk[0:64], pv[0:64, :, :, 0], [psmB], [MkB])
                self.cp("dve", Mk[64:128], pv[64:128, :, :, 1], [psmB], [MkB])
            new_tok = (i == ntile)
            if not new_tok:
                st, stB = stg[k_ % 3]
                kt_, ktB_ = kTp[k_ % 2]
                if branch == "slc":
                    v_, vB_ = Vp[k_ % 2]
                else:
                    v_, vB_ = (Vp0 if i == 0 else Vp[k_ % 2])
                pst, pstB = self.ps[2 if k_ % 2 == 0 else 7], self.psB[2 if k_ % 2 == 0 else 7]
                for a_ in range(2):
                    self.tr(pst[:, a_ * 128:(a_ + 1) * 128], st[:, a_ * 128:(a_ + 1) * 128], idf[:], [stB, self.identB], [pstB])
                self.cp("act", kt_[:], pst[:, 0:256].rearrange("p (a t) -> p a t", a=2), [pstB], [ktB_])
                self.cp("pool", v_[:, :, 0:64], st[:, 256:512].rearrange("p (h d) -> p h d", h=4), [stB], [vB_])
                if branch == "win" and i == 0:
                    self.ms("pool", v_[0:1, :, :], 0.0, [vB_])
                st8[k_] = (kt_, ktB_, v_, vB_, 128)
            else:
                vd_, vdB_ = Vd[b % 2]
                self.ts("dve", vd_[:], Vn[:, bri - 1, :, :], idf[:ND, b:b + 1], None, ALU.mult, None, [VnB, self.identB], [vdB_])
                st8[k_] = (None, None, vd_, vdB_, ND)

        def compute(k_):
            b, bri, branch, i, ntile = tiles[k_]
            new_tok = (i == ntile)
            kt_, ktB_, v_, vB_, nk = st8.pop(k_)
            psO, psOB = self.ps[3 + bri], self.psB[3 + bri]
            kTd = N["kTs"] if branch == "slc" else N["kTw"]
            kTdB = NB["kTs"] if branch == "slc" else NB["kTw"]
            pss = [(self.ps[0], self.psB[0]), (self.ps[1], self.psB[1])]
            for g in range(4):
                a_, b_ = g // 2, g % 2
                rs = slice(64 * b_, 64 * b_ + 64)
                lhsT = kt_[rs, a_, :] if not new_tok else kTd[rs, a_, TP:NT]
                lB = [ktB_] if not new_tok else [kTdB]
                self.mm(pss[b_][0][:nk, a_ * 4:a_ * 4 + 4], lhsT, qT[rs, 4 * a_:4 * a_ + 4, TP + b], True, True,
                        lB + qTB, [pss[b_][1]], tp=(64 * b_, 0))
            if not new_tok:
                pt_, ptB_ = ptd[k_ % 2]
            else:
                pt_, ptB_ = ptn
            for b_ in range(2):
                self.act(pt_[:nk, b_, :, :], pss[b_][0][:nk, 0:8].rearrange("p (a r) -> p a r", a=2), AF.Exp,
                         [pss[b_][1], self.MnegB], [ptB_], scale=0.125, bias=self.Mneg[:nk, jb, bri:bri + 1])
            if branch == "slc" and not new_tok:
                mkv = Mk[:, :, i].rearrange("p (a b) -> p b a", b=2).unsqueeze(3).to_broadcast([128, 2, 2, 4])
                self.tt("dve", pt_[:], pt_[:], mkv, ALU.mult, [ptB_, MkB], [ptB_])
            for g in range(4):
                a_, b_ = g // 2, g % 2
                self.mm(psO[:4, g * 65:(g + 1) * 65], pt_[:nk, b_, a_, :], v_[:nk, g, :], (i == 0 and g == 0), False,
                        [ptB_, vB_], [psOB], inc=(g == 3))
            if new_tok:
                ob, obB = Osb[bri - 1]
                self.cp("act", ob[:].rearrange("p a b -> p (a b)"), psO[:4, 0:260], [psOB], [obB])
                self.dma(self.od_scr.ap()[bri - 1, b].rearrange("g r c -> r g c"), ob[:], [obB], [odB])
        issue_load(0)
        issue_load(1)
        prefetch(0)
        for k_ in range(len(tiles)):
            if k_ + 2 < len(tiles):
                issue_load(k_ + 2)
            if k_ + 1 < len(tiles):
                prefetch(k_ + 1)
            compute(k_)
        self.dma(Otm[:], self.od_scr.ap().rearrange("t b g r c -> b t (g r) c"), [odB], [OtmB])
        for bri in (1, 2):
            self.ts("dve", wgt[:].unsqueeze(2), Otm[:, bri - 1, :, 64:65], 1e-30, None, ALU.max, None, [OtmB], [wgtB])
            P.op("dve", lambda e: e.reciprocal(out=wgt[:], in_=wgt[:]), [wgtB], [wgtB])
            self.tt("dve", wgt[:], wgt[:], gt[:, :, bri], ALU.mult, [wgtB, gatB], [wgtB])
            self.tt("dve", tmpo[:], Otm[:, bri - 1, :, 0:64], wgt[:].unsqueeze(2).to_broadcast([ND, 16, 64]), ALU.mult,
                    [OtmB, wgtB], [tmpoB])
            self.tt("dve", otd[:], otd[:], tmpo[:], ALU.add, [otdB, tmpoB], [otdB])
        otf = otd[:].rearrange("p h d -> p (h d)")
        for half in range(2):
            ps, psB = self.ps[half], self.psB[half]
            for j in range(4):
                dc = 4 * half + j
                self.tr(ps[:, j * ND:(j + 1) * ND], otf[:, dc * 128:(dc + 1) * 128], idf[:ND, :ND], [otdB, self.identB], [psB])
            self.cp("act", oTd[:, 4 * half:4 * half + 4, :], ps[:, 0:4 * ND].rearrange("p (j t) -> p j t", j=4), [psB], [oTdB])
        for dco in range(NDC):
            w_, wB_ = wod[0]
            src = bass.AP(tensor=I["nsa_w_o"], offset=jb * D * D + dco * 128, ap=[[D, 128], [128 * D, NDC], [1, 128]])
            self.dma(w_[:], src, [], [wB_], q="pool")
            ps, psB = self.ps[6], self.psB[6]
            for dci in range(NDC):
                self.mm(ps[:, :ND], w_[:, dci, :], oTd[:, dci, :], dci == 0, dci == NDC - 1, [wB_, oTdB], [psB],
                        inc=(dci == NDC - 1))
            self.tt("dve", self.x[:, dco, TP:NT], self.x[:, dco, TP:NT], ps[:, :ND], ALU.add,
                    [psB, self.xB[dco][4]], [self.xB[dco][4]])

    def win_cache_copy(self):
        I, O = self.I, self.O
        for b in range(ND):
            src = I["cache_win"].ap()[b:b + 1, 512:512 * 512]
            dst = O["win_s"].ap()[b:b + 1, 0:511, :].rearrange("b r c -> b (r c)")
            self.dma(dst, src, [], [], q="act")

    def store_x(self):
        O = self.O
        self.P.barrier()
        stg = [self.ov_view("ostg%d" % i, i * 4096, [128, 1024], F32) for i in range(3)]
        stgB = self.P.bufs(3)
        tiles = [(O["yp"].ap()[t * 128:(t + 1) * 128, :], 128, t * 128) for t in range(16)]
        tiles.append((O["ys"].ap(), ND, TP))
        for ti, (dst, n, t0) in enumerate(tiles):
            s, sB = stg[ti % 3], stgB[ti % 3]
            tt = min(t0 // 512, 4)
            for half in range(2):
                ps, psB = self.next_ps(0, 4)
                for j in range(4):
                    dc = half * 4 + j
                    self.tr(ps[:n, j * 128:(j + 1) * 128], self.x[:, dc, t0:t0 + n], self.identf[:],
                            [self.xB[dc][tt], self.identB], [psB])
                self.cp("act" if half == 0 else "dve", s[:n, half * 512:(half + 1) * 512], ps[:n, :], [psB], [sB])
            self.dma(dst, s[:n, :], [sB], [])

    def build(self):
        self.declare_io()
        self.setup()
        self.load_x()
        st = self.stage
        self.dbg("x0", self.x[:, :, 0:64], [b for r in self.xB for b in r])
        if st >= 1:
            self.s5_layer(0)
        if st >= 2:
            self.mlp(0)
        if st >= 3:
            self.s5_layer(1)
            self.mlp(1)
        if st >= 4:
            self.kv_phase()
        if st >= 6:
            self.nsa_layer(0)
            self.mlp(2)
        if st >= 7:
            self.nsa_layer(1)
            self.mlp(3)
        self.store_x()
        self.P.finish()


def build_nc(stage=99, debug=False):
    nc = bass.Bass("TRN2", target_bir_lowering=False)
    kb = KB(nc, stage)
    kb.debug = debug
    kb.build()
    return nc


def make_in_maps(inp, stage=99):
    f = lambda a: np.ascontiguousarray(a, dtype=np.float32)
    if stage >= 8:
        cc = f(inp["cache_cmp_kv"]).reshape(2560 * 128, 512)
        cs = f(inp["cache_slc_kv"]).reshape(2560 * 128, 512)
    shared = {
        "norm_mix": f(inp["norm_mix"]), "norm_mlp": f(inp["norm_mlp"]),
        "w_up": f(inp["w_up"]), "w_down": f(inp["w_down"]),
        "s5_a_re": f(inp["s5_a_re"]).reshape(2, 32, 128), "s5_a_im": f(inp["s5_a_im"]).reshape(2, 32, 128),
        "s5_log_dt": f(inp["s5_log_dt"]).reshape(2, 32, 2),
        "s5_b_re": f(inp["s5_b_re"]).reshape(2, 32, 128, 16), "s5_b_im": f(inp["s5_b_im"]).reshape(2, 32, 128, 16),
        "s5_c_re": f(inp["s5_c_re"]), "s5_c_im": f(inp["s5_c_im"]),
        "s5_d": f(inp["s5_d"]), "s5_w_glu": f(inp["s5_w_glu"]),
        "kv_norm": f(inp["kv_norm"]).reshape(1, D), "w_kv": f(inp["w_kv"]), "k_norm": f(inp["k_norm"]).reshape(1, 192),
        "q_norm": f(inp["q_norm"]).reshape(1, 128), "cmp_pos": f(inp["cmp_pos"]), "cmp_w1": f(inp["cmp_w1"]),
        "cmp_w2": f(inp["cmp_w2"]), "nsa_w_in": f(inp["nsa_w_in"]), "nsa_w_o": f(inp["nsa_w_o"]),
    }
    maps = []
    for c in range(NCORES):
        m = dict(shared)
        m["xp"] = f(inp["x_prompt"][c])
        m["xs"] = f(inp["x_sample"][c * ND:(c + 1) * ND, 0])
        m["st5"] = f(inp["state_s5"][:, c * ND:(c + 1) * ND]).reshape(2, ND, 8192)
        m["cache_win"] = f(inp["cache_win_kv"][c * ND:(c + 1) * ND]).reshape(ND, 512 * 512)
        m["pt"] = np.ascontiguousarray(inp["page_table"][c * ND:(c + 1) * ND], dtype=np.int32).reshape(1, ND * 16)
        if stage >= 8:
            m["cache_cmp"] = cc
            m["cache_slc"] = cs
        maps.append(m)
    return maps


def assemble(results):
    r = results
    cat = lambda k: np.stack([r[c][k] for c in range(NCORES)])
    y_prompt = cat("yp")
    y_sample = np.concatenate([r[c]["ys"] for c in range(NCORES)])[:, None, :]
    kvp = lambda k: cat(k).reshape(NCORES, TP, 2, 4, 64)
    kvs = lambda k: np.concatenate([r[c][k] for c in range(NCORES)]).reshape(NCORES * ND, 1, 2, 4, 64)
    win_p = cat("win_p").reshape(NCORES, 512, 2, 4, 64)
    win_s = np.concatenate([r[c]["win_s"] for c in range(NCORES)]).reshape(NCORES * ND, 512, 2, 4, 64)
    s5p = np.stack([r[c]["s5p"] for c in range(NCORES)], axis=1).reshape(2, NCORES, 64, 64, 2)
    s5s = np.concatenate([r[c]["s5s"] for c in range(NCORES)], axis=1).reshape(2, NCORES * ND, 64, 64, 2)
    return (y_prompt, y_sample, kvp("cmp_p"), kvs("cmp_s"), kvp("slc_p"), kvs("slc_s"), win_p, win_s, s5p, s5s)


_NC_CACHE = {}


def kernel(**inputs):
    stage = 99
    if stage not in _NC_CACHE:
        _NC_CACHE[stage] = build_nc(stage)
    nc = _NC_CACHE[stage]
    maps = make_in_maps(inputs)
    res = run_bass_kernel_spmd(nc, maps, core_ids=list(range(NCORES)))
    outs = assemble(res.results)
    return tuple(np.ascontiguousarray(o, dtype=np.float32) for o in outs)
```
